# Optimizing a Trainium2 kernel written in Bass

```python
import math
import jax
import jax.numpy as jnp
from jax import lax
import numpy as np

D_MODEL = 1024
BATCH = 8
SEQ = 8192
DEPTH = 1
DEC_BATCH = 8
DEC_SEQ = 4096
PAST_LEN = 128

EPS = 1e-6
HEAD_DIM = D_MODEL // 16
N_HEADS = 12
DILATED_GROUPS = ((128, 1), (512, 4), (2048, 16))
HEADS_PER_GROUP = N_HEADS // len(DILATED_GROUPS)
ATTN_WIDTH = N_HEADS * HEAD_DIM
ATTN_OUT_WIDTH = HEADS_PER_GROUP * HEAD_DIM
NUM_BUCKETS = 32
MAX_DISTANCE = 1024
NEG_INF = -1e30
D_HYENA = 3 * D_MODEL // 4
SHORT_CONV = 3
FILTER_BANDS = 16
FILTER_EMB = 1 + 2 * FILTER_BANDS
FILTER_HIDDEN = 64
DECAY_TARGET = 1e-2
FAST_DECAY_PCT = 0.3
SLOW_DECAY_PCT = 1.5
N_BRANCHES = 2
D_FF = 4 * D_MODEL
IN_PROJ_WIDTH = 3 * D_HYENA + 3 * ATTN_WIDTH + N_BRANCHES * D_MODEL

kernel_name = 'hyena_dilated_attn_encoder'


def rmsnorm(x, g):
    xf = x.astype(jnp.float32)
    y = xf * lax.rsqrt(jnp.mean(xf * xf, axis=-1, keepdims=True) + EPS)
    return (y * g.astype(jnp.float32)).astype(x.dtype)


def t5_bucket(rel):
    half = NUM_BUCKETS // 2
    max_exact = half // 2
    n = jnp.abs(rel)
    ret = jnp.where(rel > 0, half, 0)
    large = max_exact + (jnp.log(jnp.maximum(n, 1).astype(jnp.float32) / max_exact)
                         / math.log(MAX_DISTANCE / max_exact) * (half - max_exact)).astype(jnp.int32)
    large = jnp.minimum(large, half - 1)
    return ret + jnp.where(n < max_exact, n, large)


def banded_attention(q, k, v, bias_vec, radius):
    blk = radius
    S, hd = q.shape[-2], q.shape[-1]
    nb = -(-S // blk)
    sp = nb * blk
    lead = q.shape[:-2]
    def pad_cfg(lo, hi):
        return [(0, 0)] * len(lead) + [(lo, hi), (0, 0)]
    qb = jnp.pad(q, pad_cfg(0, sp - S)).reshape(lead + (nb, blk, hd))
    def windows(t):
        tb = jnp.pad(t, pad_cfg(radius, sp - S + radius)).reshape(lead + (nb + 2, blk, hd))
        return jnp.concatenate([tb[..., i:i + nb, :, :] for i in range(3)], axis=-2)
    kw, vw = windows(k), windows(v)
    s = jnp.einsum('...bqd,...bkd->...bqk', qb, kw, preferred_element_type=jnp.float32)
    qi = jnp.arange(blk)[:, None]
    kj = jnp.arange(3 * blk)[None, :]
    rel = kj - radius - qi
    key_pos = jnp.arange(nb)[:, None, None] * blk + (kj - radius)[None]
    valid = (jnp.abs(rel) <= radius)[None] & (key_pos >= 0) & (key_pos < S)
    bias = bias_vec.astype(jnp.float32)[:, jnp.clip(rel + radius, 0, 2 * radius)][:, None]
    s = jnp.where(valid, s + bias, NEG_INF)
    m = jnp.max(s, axis=-1, keepdims=True)
    p = jnp.exp(s - m)
    den = jnp.sum(p, axis=-1, keepdims=True)
    o = jnp.einsum('...bqk,...bkd->...bqd', p, vw.astype(jnp.float32)) / den
    lse = (m + jnp.log(den))[..., 0]
    o = o.reshape(lead + (sp, hd))[..., :S, :]
    lse = lse.reshape(lead + (sp,))[..., :S]
    return o, lse


def dilated_group(q, k, v, bias_tab, window, dil):
    B, L, Hg, hd = q.shape
    S = L // dil
    radius = window // (2 * dil)
    def to_sub(t):
        return t.reshape(B, S, dil, Hg, hd).transpose(0, 2, 3, 1, 4)
    offs = jnp.arange(-radius, radius + 1) * dil
    bias_vec = bias_tab[t5_bucket(offs)].T
    o, lse = banded_attention(to_sub(q), to_sub(k), to_sub(v), bias_vec, radius)
    o = o.transpose(0, 3, 1, 2, 4).reshape(B, L, Hg, hd)
    lse = lse.transpose(0, 3, 1, 2).reshape(B, L, Hg)
    return o, lse


def dilated_attention(q, k, v, rel_bias, q_norm_g, k_norm_g):
    B, L, _ = q.shape
    q = rmsnorm(q.reshape(B, L, N_HEADS, HEAD_DIM), q_norm_g) * (HEAD_DIM ** -0.5)
    k = rmsnorm(k.reshape(B, L, N_HEADS, HEAD_DIM), k_norm_g)
    v = v.reshape(B, L, N_HEADS, HEAD_DIM)
    outs, lses = [], []
    for gi, (window, dil) in enumerate(DILATED_GROUPS):
        hs = slice(gi * HEADS_PER_GROUP, (gi + 1) * HEADS_PER_GROUP)
        o, lse = dilated_group(q[:, :, hs], k[:, :, hs], v[:, :, hs], rel_bias[:, hs], window, dil)
        outs.append(o)
        lses.append(lse)
    alpha = jax.nn.softmax(jnp.stack(lses), axis=0)
    o = jnp.sum(alpha[..., None] * jnp.stack(outs), axis=0)
    return o.reshape(B, L, ATTN_OUT_WIDTH).astype(q.dtype)


def short_conv(z, w, b):
    zp = jnp.pad(z, ((0, 0), (1, 1), (0, 0)))
    return zp[:, :-2] * w[0] + zp[:, 1:-1] * w[1] + zp[:, 2:] * w[2] + b


def implicit_filter(L, w1, b1, w2, b2, w3, b3, freq, w_out):
    f32 = jnp.float32
    t = jnp.linspace(0.0, 1.0, L, dtype=f32)[:, None]
    bands = jnp.linspace(1e-4, FILTER_BANDS - 1, FILTER_BANDS, dtype=f32)[None, :]
    w = 2.0 * math.pi * jnp.arange(L, dtype=f32)[:, None] / L
    feats = jnp.concatenate([t, jnp.cos(bands * w), -jnp.sin(bands * w)], axis=-1)
    fr = freq.astype(f32)
    h = jnp.sin(fr * (feats @ w1.astype(f32) + b1.astype(f32)))
    h = jnp.sin(fr * (h @ w2.astype(f32) + b2.astype(f32)))
    h = jnp.sin(fr * (h @ w3.astype(f32) + b3.astype(f32)))
    h = h @ w_out.astype(f32)
    deltas = jnp.abs(jnp.linspace(math.log(DECAY_TARGET) / SLOW_DECAY_PCT,
                                  math.log(DECAY_TARGET) / FAST_DECAY_PCT, D_HYENA, dtype=f32))
    decay = jnp.exp(-t * deltas[None, :])
    h_fwd, h_bwd = jnp.split(h, 2, axis=-1)
    h_fwd, h_bwd = h_fwd * decay, h_bwd * decay
    k2 = jnp.concatenate([h_fwd, jnp.zeros((1, D_HYENA), f32), h_bwd[1:][::-1]], axis=0)
    return k2 / jnp.sum(jnp.abs(k2), axis=0, keepdims=True)


def long_conv(u, k2, d):
    L = u.shape[1]
    uf = u.astype(jnp.float32)
    U = jnp.fft.rfft(uf, n=2 * L, axis=1)
    K = jnp.fft.rfft(k2, n=2 * L, axis=0)
    y = jnp.fft.irfft(U * K[None], n=2 * L, axis=1)[:, :L]
    return (y + uf * d.astype(jnp.float32)).astype(u.dtype)


def hyena_branch(z, conv_w, conv_b, w1, b1, w2, b2, w3, b3, freq, w_out, hyena_d):
    L = z.shape[1]
    zc = short_conv(z, conv_w, conv_b)
    x0, x1, v = jnp.split(zc, 3, axis=-1)
    k2 = implicit_filter(L, w1, b1, w2, b2, w3, b3, freq, w_out)
    return x0 * long_conv(x1 * v, k2, hyena_d)


def encoder_layer(x, c, rel_bias, ada_w, ada_b, norm1_g, w_in, conv_w, conv_b,
                  filt_w1, filt_b1, filt_w2, filt_b2, filt_w3, filt_b3, filt_freq, filt_w_out,
                  hyena_d, q_norm_g, k_norm_g, w_hy_br, w_at_br, w_out, norm2_g, w_up, w_down):
    mod = jax.nn.silu(c) @ ada_w + ada_b
    sh1, sc1, gt1, sh2, sc2, gt2 = jnp.split(mod[:, None, :], 6, axis=-1)
    u = rmsnorm(x, norm1_g) * (1.0 + sc1) + sh1
    z = u @ w_in
    o1 = 3 * D_HYENA
    z_hy, q, k, v, g = jnp.split(z, [o1, o1 + ATTN_WIDTH, o1 + 2 * ATTN_WIDTH, o1 + 3 * ATTN_WIDTH], axis=-1)
    y_hy = hyena_branch(z_hy, conv_w, conv_b, filt_w1, filt_b1, filt_w2, filt_b2,
                        filt_w3, filt_b3, filt_freq, filt_w_out, hyena_d)
    y_at = dilated_attention(q, k, v, rel_bias, q_norm_g, k_norm_g)
    g_hy, g_at = jnp.split(jax.nn.sigmoid(g), 2, axis=-1)
    mixed = (g_hy * (y_hy @ w_hy_br) + g_at * (y_at @ w_at_br)) @ w_out
    h = x + gt1 * mixed
    u2 = rmsnorm(h, norm2_g) * (1.0 + sc2) + sh2
    ff = jnp.square(jax.nn.relu(u2 @ w_up)) @ w_down
    return h + gt2 * ff


def setup_inputs(seed: int = 0) -> dict:
    key = jax.random.key(seed)
    ks = jax.random.split(key, 32)
    def nrm(k, shape, scale):
        return jax.random.normal(k, shape, jnp.float32) * scale
    return {
        'x_prompt': nrm(ks[0], (BATCH, SEQ, D_MODEL), 1.0),
        'x_sample': nrm(ks[1], (DEC_BATCH, DEC_SEQ, D_MODEL), 1.0),
        'c_prompt': nrm(ks[2], (BATCH, D_MODEL), 1.0),
        'c_sample': nrm(ks[3], (DEC_BATCH, D_MODEL), 1.0),
        'rel_bias': nrm(ks[4], (NUM_BUCKETS, N_HEADS), 0.5),
        'ada_w': nrm(ks[5], (DEPTH, D_MODEL, 6 * D_MODEL), D_MODEL ** -0.5),
        'ada_b': nrm(ks[6], (DEPTH, 6 * D_MODEL), 0.02),
        'norm1_g': 1.0 + nrm(ks[7], (DEPTH, D_MODEL), 0.02),
        'w_in': nrm(ks[8], (DEPTH, D_MODEL, IN_PROJ_WIDTH), D_MODEL ** -0.5),
        'conv_w': nrm(ks[9], (DEPTH, SHORT_CONV, 3 * D_HYENA), SHORT_CONV ** -0.5),
        'conv_b': nrm(ks[10], (DEPTH, 3 * D_HYENA), 0.02),
        'filt_w1': nrm(ks[11], (DEPTH, FILTER_EMB, FILTER_HIDDEN), FILTER_EMB ** -0.5),
        'filt_b1': nrm(ks[12], (DEPTH, FILTER_HIDDEN), 0.02),
        'filt_w2': nrm(ks[13], (DEPTH, FILTER_HIDDEN, FILTER_HIDDEN), FILTER_HIDDEN ** -0.5),
        'filt_b2': nrm(ks[14], (DEPTH, FILTER_HIDDEN), 0.02),
        'filt_w3': nrm(ks[15], (DEPTH, FILTER_HIDDEN, FILTER_HIDDEN), FILTER_HIDDEN ** -0.5),
        'filt_b3': nrm(ks[16], (DEPTH, FILTER_HIDDEN), 0.02),
        'filt_freq': 1.0 + nrm(ks[17], (DEPTH, FILTER_HIDDEN), 0.02),
        'filt_w_out': nrm(ks[18], (DEPTH, FILTER_HIDDEN, 2 * D_HYENA), FILTER_HIDDEN ** -0.5),
        'hyena_d': nrm(ks[19], (DEPTH, D_HYENA), 0.1),
        'q_norm_g': 1.0 + nrm(ks[20], (DEPTH, N_HEADS, HEAD_DIM), 0.02),
        'k_norm_g': 1.0 + nrm(ks[21], (DEPTH, N_HEADS, HEAD_DIM), 0.02),
        'w_hy_br': nrm(ks[22], (DEPTH, D_HYENA, D_MODEL), D_HYENA ** -0.5),
        'w_at_br': nrm(ks[23], (DEPTH, ATTN_OUT_WIDTH, D_MODEL), ATTN_OUT_WIDTH ** -0.5),
        'w_out': nrm(ks[24], (DEPTH, D_MODEL, D_MODEL), D_MODEL ** -0.5),
        'norm2_g': 1.0 + nrm(ks[25], (DEPTH, D_MODEL), 0.02),
        'w_up': nrm(ks[26], (DEPTH, D_MODEL, D_FF), D_MODEL ** -0.5),
        'w_down': nrm(ks[27], (DEPTH, D_FF, D_MODEL), D_FF ** -0.5),
    }


def reference(x_prompt, x_sample, c_prompt, c_sample, rel_bias, ada_w, ada_b, norm1_g, w_in,
              conv_w, conv_b, filt_w1, filt_b1, filt_w2, filt_b2, filt_w3, filt_b3, filt_freq,
              filt_w_out, hyena_d, q_norm_g, k_norm_g, w_hy_br, w_at_br, w_out, norm2_g, w_up, w_down):
    y_prompt, y_sample = x_prompt, x_sample
    for l in range(DEPTH):
        layer_params = (ada_w[l], ada_b[l], norm1_g[l], w_in[l], conv_w[l], conv_b[l],
                        filt_w1[l], filt_b1[l], filt_w2[l], filt_b2[l], filt_w3[l], filt_b3[l],
                        filt_freq[l], filt_w_out[l], hyena_d[l], q_norm_g[l], k_norm_g[l],
                        w_hy_br[l], w_at_br[l], w_out[l], norm2_g[l], w_up[l], w_down[l])
        y_prompt = encoder_layer(y_prompt, c_prompt, rel_bias, *layer_params)
        y_sample = encoder_layer(y_sample, c_sample, rel_bias, *layer_params)
    return (y_prompt, y_sample)
```

```python
import contextlib
import math
import numpy as np
import ml_dtypes
import concourse.bass as bass
import concourse.mybir as mybir
from concourse.bass_utils import run_bass_kernel_spmd

F32 = mybir.dt.float32
BF16 = mybir.dt.bfloat16
ALU = mybir.AluOpType
AF = mybir.ActivationFunctionType
AX = mybir.AxisListType

D = 1024
DH = 768
NH = 12
HD = 64
DFF = 4096
WIN = 6656
EPS = 1e-6
NEG = -30000.0
TWO_PI = 2.0 * math.pi
MAGIC = 12582912.0


class Res:
    __slots__ = ("name", "w", "r", "multi", "dsem")

    def __init__(self, name, multi=False):
        self.name = name
        self.w = {}
        self.r = {}
        self.multi = multi
        self.dsem = {}


class DSem:
    def __init__(self, sem, key, kind):
        self.sem = sem
        self.key = key
        self.kind = kind
        self.n = 0


class Sched:
    def __init__(self, nc, es, n_hw=44, n_sw=24, same_engine_sync=True):
        self.nc = nc
        self.E = {"pe": nc.tensor, "act": nc.scalar, "dve": nc.vector, "pool": nc.gpsimd, "sp": nc.sync}
        self.esem = {k: es.enter_context(nc.semaphore("e_" + k)) for k in ("pe", "act", "dve", "pool")}
        self.cnt = {k: 0 for k in self.esem}
        self.seen = {k: {} for k in self.E}
        self.same = same_engine_sync
        self.pool_ds = {
            "hw": [DSem(es.enter_context(nc.semaphore(f"dh{i}")), f"dh{i}", "hw") for i in range(n_hw)],
            "sw": [DSem(es.enter_context(nc.semaphore(f"ds{i}")), f"ds{i}", "sw") for i in range(n_sw)],
        }
        self.all_ds = self.pool_ds["hw"] + self.pool_ds["sw"]
        self.free_ds = {"hw": list(self.pool_ds["hw"]), "sw": list(self.pool_ds["sw"])}
        self.stage_res = []
        self.ninst = 0

    @staticmethod
    def _add(deps, d):
        for k, (s, v) in d.items():
            if k not in deps or deps[k][1] < v:
                deps[k] = (s, v)

    def _wait(self, eng, deps):
        seen = self.seen[eng]
        for k, (s, v) in deps.items():
            if k == "e_" + eng and (eng == "pe" or not self.same):
                continue
            if seen.get(k, 0) >= v:
                continue
            self.E[eng].wait_ge(s, v)
            seen[k] = v
            self.ninst += 1

    def _deps(self, reads, writes):
        deps = {}
        for r in reads:
            self._add(deps, r.w)
        for w in writes:
            self._add(deps, w.r)
            if not w.multi:
                self._add(deps, w.w)
        return deps

    @staticmethod
    def _record(key, sem, ev, reads, writes):
        for r in reads:
            if r.r.get(key, (None, 0))[1] < ev:
                r.r[key] = (sem, ev)
        for w in writes:
            if w.multi:
                if w.w.get(key, (None, 0))[1] < ev:
                    w.w[key] = (sem, ev)
            else:
                w.w = {key: (sem, ev)}
                w.r = {}

    def op(self, eng, fn, reads=(), writes=(), signal=True):
        self._wait(eng, self._deps(reads, writes))
        inst = fn(self.E[eng])
        self.ninst += 1
        sem = self.esem[eng]
        if signal:
            self.cnt[eng] += 1
            inst.then_inc(sem, 1)
            ev = self.cnt[eng]
        else:
            ev = self.cnt[eng] + 1
        self._record("e_" + eng, sem, ev, reads, writes)
        return inst

    def dma(self, q, out, in_, reads, writes, sb, **kw):
        kind = "sw" if q == "pool" else "hw"
        ds = sb.dsem.get(kind)
        if ds is None:
            assert self.free_ds[kind], "out of dma semaphores " + kind
            ds = self.free_ds[kind].pop()
            sb.dsem[kind] = ds
            self.stage_res.append(sb)
        deps = self._deps(reads, writes)
        if ds.n:
            self._add(deps, {ds.key: (ds.sem, 16 * ds.n)})
        self._wait(q, deps)
        inst = self.E[q].dma_start(out=out, in_=in_, **kw)
        inst.then_inc(ds.sem, 16)
        self.ninst += 1
        ds.n += 1
        self._record(ds.key, ds.sem, 16 * ds.n, reads, writes)
        return inst

    def barrier(self):
        deps = {}
        for k, s in self.esem.items():
            if self.cnt[k]:
                deps["e_" + k] = (s, self.cnt[k])
        for ds in self.all_ds:
            if ds.n:
                deps[ds.key] = (ds.sem, 16 * ds.n)
        for e in self.E:
            d = {k: v for k, v in deps.items() if k != "e_" + e}
            self._wait(e, d)

    def stage_end(self):
        self.barrier()
        for r in self.stage_res:
            for kind, ds in r.dsem.items():
                self.free_ds[kind].append(ds)
            r.dsem = {}
        self.stage_res = []


def t5_bucket_np(rel):
    half = 16
    max_exact = 8
    n = np.abs(rel)
    ret = np.where(rel > 0, half, 0)
    nf = np.maximum(n, 1).astype(np.float32)
    large = max_exact + (np.log(nf / np.float32(max_exact)) / np.float32(math.log(1024 / max_exact))
                         * np.float32(half - max_exact)).astype(np.int32)
    large = np.minimum(large, half - 1)
    return ret + np.where(n < max_exact, n, large)


def host_consts(Ls):
    c = {}
    c["ident_b"] = np.eye(128, dtype=np.float32).astype(ml_dtypes.bfloat16)
    c["ident_f"] = np.eye(128, dtype=np.float32)
    c["antiid"] = np.eye(128, dtype=np.float32)[::-1].copy()
    bd = np.zeros((128, 128), np.float32)
    bd[:64, :64] = 1.0
    bd[64:, 64:] = 1.0
    c["bdones"] = bd.astype(ml_dtypes.bfloat16)
    oh = np.zeros((3, 33, 512), np.float32)
    for g, dil in enumerate((1, 4, 16)):
        for ab in range(2):
            for m in range(255):
                delta = 127 - m
                if ab == 0:
                    valid = delta >= 0
                    rel = delta - 64
                else:
                    valid = delta <= 0
                    rel = delta + 64
                if valid and abs(rel) <= 64:
                    b = int(t5_bucket_np(np.array(rel * dil)))
                    oh[g, b, ab * 256 + m] = 1.0
                else:
                    oh[g, 32, ab * 256 + m] = NEG
            oh[g, 32, ab * 256 + 255] = NEG
    c["bias_oh"] = oh
    for L in sorted(set(Ls)):
        N = 2 * L
        N1 = N // 128
        f32 = np.float32
        t = np.linspace(0.0, 1.0, L, dtype=f32)[:, None]
        bands = np.linspace(1e-4, 15, 16, dtype=f32)[None, :]
        w = (f32(2.0 * math.pi) * np.arange(L, dtype=f32)[:, None] / f32(L)).astype(f32)
        feats = np.concatenate([t, np.cos(bands * w), -np.sin(bands * w)], axis=-1).astype(f32)
        ft = np.zeros((2, 33, L), f32)
        ft[0] = feats.T
        ft[1, :, 1:] = feats[1:][::-1].T
        c[f"feats{L}"] = ft
        tv = np.zeros((2, L), f32)
        tv[0] = t[:, 0]
        tv[1, 1:] = t[1:, 0][::-1]
        c[f"tvec{L}"] = tv
        n1 = np.arange(N1)[:, None]
        k1 = np.arange(N1)[None, :]
        th = 2 * np.pi * n1 * k1 / N1
        c[f"f1tab{L}"] = np.concatenate([np.cos(th), -np.sin(th)], axis=1).astype(ml_dtypes.bfloat16)
        n2 = np.arange(128)[:, None, None]
        kk1 = np.arange(N1)[None, :, None]
        kk2 = np.arange(128)[None, None, :]
        th = 2 * np.pi * ((n2 * (kk1 + N1 * kk2)) % N) / N
        gr = np.cos(th)
        gi = -np.sin(th)
        c[f"gtab{L}"] = np.stack([gr, gi, -gi], axis=2).astype(ml_dtypes.bfloat16)
        k2 = np.arange(128)[:, None]
        t2 = np.arange(128)[None, :]
        th = 2 * np.pi * k2 * t2 / 128
        c2 = np.cos(th)
        s2 = np.sin(th)
        c[f"i1tab{L}"] = np.stack([np.concatenate([c2, s2], 1), np.concatenate([-s2, c2], 1)], axis=1).astype(
            ml_dtypes.bfloat16)
        kk = np.arange(N1)[:, None, None]
        tt2 = np.arange(128)[None, :, None]
        tt1 = np.arange(N1 // 2)[None, None, :]
        th = 2 * np.pi * ((kk * (tt2 + 128 * tt1)) % N) / N
        wk = np.zeros((N1, 1, 1))
        wk[0] = 1.0
        wk[N1 // 2] = 1.0
        wk[1:N1 // 2] = 2.0
        c[f"i2tab{L}"] = np.stack([wk * np.cos(th) / N, -wk * np.sin(th) / N], axis=2).astype(ml_dtypes.bfloat16)
    deltas = np.abs(np.linspace(math.log(1e-2) / 1.5, math.log(1e-2) / 0.3, DH, dtype=np.float32))
    c["ndelta"] = (-deltas).astype(np.float32).reshape(6, 128).T.copy()
    return c


class Prog:
    pass


def build(Ls, stages=None, dbg=()):
    nc = bass.Bass("TRN2", target_bir_lowering=False)
    es = contextlib.ExitStack()
    S = Sched(nc, es)
    NS = len(Ls)
    P = Prog()
    P.nc, P.S, P.es = nc, S, es

    def din(name, shape, dt=F32):
        return nc.dram_tensor(name, list(shape), dt, kind="ExternalInput")

    def dscr(name, shape, dt):
        kind = "ExternalOutput" if name in dbg else "Internal"
        return nc.dram_tensor(name, list(shape), dt, kind=kind)

    I = {}
    for s, L in enumerate(Ls):
        I[f"x{s}"] = din(f"x{s}", [L, D])
    I["c"] = din("c", [NS, D])
    for nm, shp in (("rel_bias", [32, NH]), ("ada_w", [D, 6 * D]), ("ada_b", [1, 6 * D]), ("norm1_g", [1, D]),
                    ("w_in", [D, WIN]), ("conv_w", [3, 3 * DH]), ("conv_b", [1, 3 * DH]),
                    ("filt_w1", [33, 64]), ("filt_b1", [1, 64]), ("filt_w2", [64, 64]), ("filt_b2", [1, 64]),
                    ("filt_w3", [64, 64]), ("filt_b3", [1, 64]), ("filt_freq", [1, 64]),
                    ("filt_w_out", [64, 2 * DH]), ("hyena_d", [1, DH]), ("q_norm_g", [1, NH * HD]),
                    ("k_norm_g", [1, NH * HD]), ("w_hy_br", [DH, D]), ("w_at_br", [256, D]), ("w_out", [D, D]),
                    ("norm2_g", [1, D]), ("w_up", [D, DFF]), ("w_down", [DFF, D])):
        I[nm] = din(nm, shp)
    hc = host_consts(Ls)
    for nm, arr in hc.items():
        I[nm] = din(nm, arr.shape, BF16 if arr.dtype == ml_dtypes.bfloat16 else F32)
    O = [nc.dram_tensor(f"y{s}", [L, D], F32, kind="ExternalOutput") for s, L in enumerate(Ls)]

    SC = []
    for s, L in enumerate(Ls):
        N1 = 2 * L // 128
        d = {}
        d["zhy"] = dscr(f"zhy{s}", [3 * DH, L + 2], BF16)
        d["qT"] = dscr(f"qT{s}", [DH, L], BF16)
        d["kT"] = dscr(f"kT{s}", [DH, L], BF16)
        d["v"] = dscr(f"v{s}", [L, DH], BF16)
        d["gT"] = dscr(f"gT{s}", [2 * D, L], BF16)
        d["aT"] = dscr(f"aT{s}", [DH, L], BF16)
        d["k2"] = dscr(f"k2{s}", [DH, 2 * L], BF16)
        d["kf"] = dscr(f"kf{s}", [6, N1 // 2, 128, 512], F32)
        d["yhy"] = dscr(f"yhy{s}", [DH, L], BF16)
        d["od"] = dscr(f"od{s}", [3, L, 4 * 65], F32)
        d["h"] = dscr(f"h{s}", [L, D], F32)
        d["res"] = {k: Res(f"{k}{s}", multi=True) for k in list(d.keys())}
        SC.append(d)
    gvd = dscr("gvd", [NH, 512], F32)
    gvd_r = Res("gvd", multi=True)
    gtbd = dscr("gtbd", [128, NS * 2 * D], F32)
    gtbd_r = Res("gtbd", multi=True)
    IN = Res("inputs", multi=True)

    PS = []
    for b in range(8):
        t = es.enter_context(nc.psum_tensor(f"ps{b}", [128, 512], F32))
        PS.append((t, Res(f"ps{b}")))
    P.psi = 0
    P.nsb = 0

    def psum():
        t, r = PS[P.psi % 8]
        P.psi += 1
        return t, r

    def sb(name, shape, dt, stack):
        P.nsb += 1
        t = stack.enter_context(nc.sbuf_tensor(f"s{P.nsb}_" + name, list(shape), dt))
        return t, Res(name)

    gs = es
    ident_b, ident_b_r = sb("ident_b", [128, 128], BF16, gs)
    ident_f, ident_f_r = sb("ident_f", [128, 128], F32, gs)
    modT, modT_r = sb("modT", [128, NS, 4, 8], F32, gs)
    S.dma("sp", ident_b[:], I["ident_b"][:, :], [IN], [ident_b_r], ident_b_r)
    S.dma("sp", ident_f[:], I["ident_f"][:, :], [IN], [ident_f_r], ident_f_r)

    want = (lambda n: stages is None or n in stages)

    def pe_flush():
        pt, pr = psum()
        S.op("pe", lambda e: e.transpose(out=pt.bitcast(BF16)[:, 0:128], in_=ident_b[:], identity=ident_b[:]),
             [ident_b_r], [pr])
    P.pe_flush = pe_flush

    if want("mod"):
        with contextlib.ExitStack() as st:
            cT, cT_r = sb("cT", [128, 8, NS], F32, st)
            gtb, gtb_r = sb("gtb", [128, NS, 2, D], F32, st)
            crep, crep_r = sb("crep", [128, 8, NS, 128], F32, st)
            n1g, n1g_r = sb("n1g", [128, 8], F32, st)
            n2g, n2g_r = sb("n2g", [128, 8], F32, st)
            abT, abT_r = sb("abT", [128, 48], F32, st)
            abb, abb_r = sb("abb", [128, 2, D], F32, st)
            aw = [sb(f"aw{i}", [128, 8, 512], F32, st) for i in range(2)]
            with nc.allow_non_contiguous_dma(reason="tiny transposed loads"):
                for s in range(NS):
                    S.dma("sp", cT[:, :, s], I["c"][s, :].rearrange("(k p) -> p k", p=128), [IN], [cT_r], cT_r)
                S.dma("sp", n1g[:], I["norm1_g"][0, :].rearrange("(k p) -> p k", p=128), [IN], [n1g_r], n1g_r)
                S.dma("sp", n2g[:], I["norm2_g"][0, :].rearrange("(k p) -> p k", p=128), [IN], [n2g_r], n2g_r)
                S.dma("sp", abT[:], I["ada_b"][0, :].rearrange("(k p) -> p k", p=128), [IN], [abT_r], abT_r)
            for j, col in enumerate((2 * D, 5 * D)):
                S.dma("sp", abb[:, j, :], I["ada_b"][0:1, col:col + D].partition_broadcast(128), [IN], [abb_r], abb_r)
            S.op("act", lambda e: e.activation(out=cT[:], in_=cT[:], func=AF.Silu), [cT_r], [cT_r])
            S.op("dve", lambda e: e.memset(crep[:], 1.0), [], [crep_r])
            for kc in range(8):
                for s in range(NS):
                    S.op("dve", lambda e, kc=kc, s=s: e.tensor_scalar_mul(out=crep[:, kc, s, :], in0=crep[:, kc, s, :],
                                                                             scalar1=cT[:, kc, s:s + 1]),
                         [cT_r, crep_r], [crep_r])
            for grp in range(12):
                awt, awr = aw[grp % 2]
                S.dma("sp", awt[:], I["ada_w"][:, grp * 512:(grp + 1) * 512].rearrange("(k p) n -> p k n", p=128),
                      [IN], [awr], awr)
                sec = grp // 2
                if sec in (2, 5):
                    j = 0 if sec == 2 else 1
                    for s in range(NS):
                        pt, pr = psum()
                        for kc in range(8):
                            S.op("pe", lambda e, kc=kc, s=s, pt=pt, awt=awt: e.matmul(
                                pt[:, :], lhsT=crep[:, kc, s, :], rhs=awt[:, kc, :], start=(kc == 0), stop=(kc == 7)),
                                [crep_r, awr], [pr], signal=(kc == 7))
                        c0 = (grp % 2) * 512
                        S.op("dve", lambda e, pt=pt, s=s, j=j, c0=c0: e.tensor_tensor(
                            out=gtb[:, s, j, c0:c0 + 512], in0=pt[:, :], in1=abb[:, j, c0:c0 + 512], op=ALU.add),
                            [pr, abb_r], [gtb_r])
                else:
                    slot = {0: 0, 1: 1, 3: 2, 4: 3}[sec]
                    pt, pr = psum()
                    for sub in range(4):
                        for kc in range(8):
                            S.op("pe", lambda e, kc=kc, sub=sub, pt=pt, awt=awt: e.matmul(
                                pt[:, sub * NS:(sub + 1) * NS], lhsT=awt[:, kc, sub * 128:(sub + 1) * 128],
                                rhs=cT[:, kc, :], start=(kc == 0), stop=(kc == 7)),
                                [cT_r, awr], [pr], signal=(kc == 7 and sub == 3))
                    for sub in range(4):
                        ch = (grp % 2) * 4 + sub
                        acol = sec * 8 + ch
                        for s in range(NS):
                            S.op("dve", lambda e, pt=pt, sub=sub, s=s, slot=slot, ch=ch, acol=acol: e.tensor_tensor(
                                out=modT[:, s, slot, ch:ch + 1], in0=pt[:, sub * NS + s:sub * NS + s + 1],
                                in1=abT[:, acol:acol + 1], op=ALU.add), [pr, abT_r], [modT_r])
            for s in range(NS):
                for slot, gt_, gr_ in ((1, n1g, n1g_r), (3, n2g, n2g_r)):
                    S.op("dve", lambda e, s=s, slot=slot, gt_=gt_: e.scalar_tensor_tensor(
                        out=modT[:, s, slot, :], in0=modT[:, s, slot, :], scalar=1.0, in1=gt_[:],
                        op0=ALU.add, op1=ALU.mult), [modT_r, gr_], [modT_r])
            S.dma("sp", gtbd[:, :], gtb[:].rearrange("p a b c -> p (a b c)"), [gtb_r], [gtbd_r], gtb_r)
            pe_flush()
            S.stage_end()

    P.I, P.O, P.SC, P.IN = I, O, SC, IN
    P.sb, P.psum, P.want = sb, psum, want
    P.ident_b, P.ident_b_r, P.ident_f, P.ident_f_r = ident_b, ident_b_r, ident_f, ident_f_r
    P.modT, P.modT_r, P.gtbd, P.gtbd_r = modT, modT_r, gtbd, gtbd_r
    P.gvd, P.gvd_r = gvd, gvd_r
    P.Ls = Ls
    return P, hc


def finish(P):
    P.S.barrier()
    P.es.close()
    return P.nc


def stage_A(P):
    nc, S, I, IN, sb, psum = P.nc, P.S, P.I, P.IN, P.sb, P.psum
    import os
    AQ = os.environ.get("AQ", "act")
    with contextlib.ExitStack() as st:
        w_sb, _ = sb("w_in_sb", [128, 8, WIN], BF16, st)
        wr = [Res(f"w_in{kc}") for kc in range(8)]
        for kc in range(8):
            S.dma("pool", w_sb[:, kc, :], I["w_in"][kc * 128:(kc + 1) * 128, :], [IN], [wr[kc]], wr[kc])
        bd, bd_r = sb("bd", [128, 128], BF16, st)
        S.dma("sp", bd[:], I["bdones"][:, :], [IN], [bd_r], bd_r)
        gq, gq_r = sb("gq", [128, 12], F32, st)
        with nc.allow_non_contiguous_dma(reason="tiny transposed loads"):
            S.dma("sp", gq[:, 0:6], I["q_norm_g"][0, :].rearrange("(k p) -> p k", p=128), [IN], [gq_r], gq_r)
            S.dma("sp", gq[:, 6:12], I["k_norm_g"][0, :].rearrange("(k p) -> p k", p=128), [IN], [gq_r], gq_r)
        S.op("dve", lambda e: e.tensor_scalar_mul(out=gq[:, 0:6], in0=gq[:, 0:6], scalar1=HD ** -0.5), [gq_r], [gq_r])
        S.op("act", lambda e: e.activation(out=gq[:], in_=gq[:], func=AF.Ln), [gq_r], [gq_r])
        epsb, epsb_r = sb("epsb", [128, 1], F32, st)
        S.op("dve", lambda e: e.memset(epsb[:], EPS), [], [epsb_r])
        zt, zt_r = sb("zt", [128, 18], BF16, st)
        S.op("dve", lambda e: e.memset(zt[:], 0.0), [], [zt_r])
        xt = [sb(f"xt{i}", [128, 4, D], F32, st) for i in range(2)]
        xnb = [sb(f"xn{i}", [128, 4, D], BF16, st) for i in range(2)]
        uT = [sb(f"uT{i}", [128, 8, 512], BF16, st) for i in range(2)]
        ssb = [sb(f"ss{i}", [128, 4], F32, st) for i in range(2)]
        junk, junk_r = sb("junk", [128, D], BF16, st)
        evb = [sb(f"evb{i}", [128, 512], BF16, st) for i in range(8)]
        qf = [sb(f"qf{i}", [128, 512], F32, st) for i in range(4)]
        sq = [sb(f"sq{i}", [128, 512], BF16, st) for i in range(4)]
        rs = [sb(f"rs{i}", [128, 512], F32, st) for i in range(4)]
        tiles = [(s_, mt) for s_, L in enumerate(P.Ls) for mt in range(L // 512)]
        nt = len(tiles)
        for s_, L in enumerate(P.Ls):
            sc = P.SC[s_]
            with nc.allow_non_contiguous_dma(reason="zero pads"):
                for col in (0, L + 1):
                    S.dma("sp", sc["zhy"][:, col].rearrange("(k p) -> p k", p=128), zt[:], [zt_r], [sc["res"]["zhy"]], zt_r)

        def load(i):
            s_, mt = tiles[i]
            xt_t, xt_r = xt[i % 2]
            S.dma("sp", xt_t[:], I[f"x{s_}"][mt * 512:(mt + 1) * 512, :].rearrange("(j p) d -> p j d", p=128), [IN], [xt_r], xt_r)

        def norm(i):
            xt_t, xt_r = xt[i % 2]
            xn, xn_r = xnb[i % 2]
            ss, ss_r = ssb[i % 2]
            S.op("dve", lambda e: e.memset(ss[:], 0.0), [], [ss_r])
            for j in range(4):
                S.op("act", lambda e, j=j: e.activation(out=junk[:], in_=xt_t[:, j, :], func=AF.Square,
                                                        accum_out=ss[:, j:j + 1]), [xt_r, ss_r], [junk_r, ss_r])
            S.op("dve", lambda e: e.tensor_scalar(out=ss[:], in0=ss[:], scalar1=1.0 / D, scalar2=EPS,
                                                  op0=ALU.mult, op1=ALU.add), [ss_r], [ss_r])
            S.op("act", lambda e: e.activation(out=ss[:], in_=ss[:], func=AF.Sqrt), [ss_r], [ss_r])
            S.op("dve", lambda e: e.reciprocal(out=ss[:], in_=ss[:]), [ss_r], [ss_r])
            for j in range(4):
                S.op("act", lambda e, j=j: e.activation(out=xn[:, j, :], in_=xt_t[:, j, :], func=AF.Copy,
                                                        scale=ss[:, j:j + 1]), [xt_r, ss_r], [xn_r])

        def xpose(i):
            s_, mt = tiles[i]
            xn, xn_r = xnb[i % 2]
            uT_t, uT_r = uT[i % 2]
            for kc in range(8):
                pt, pr = psum()
                ptb = pt.bitcast(BF16)
                for j in range(4):
                    S.op("pe", lambda e, j=j, kc=kc, ptb=ptb: e.transpose(
                        out=ptb[:, j * 128:(j + 1) * 128], in_=xn[:, j, kc * 128:(kc + 1) * 128],
                        identity=P.ident_b[:]), [xn_r, P.ident_b_r], [pr], signal=(j == 3))
                S.op("dve", lambda e, kc=kc, ptb=ptb: e.tensor_scalar(
                    out=uT_t[:, kc, :], in0=ptb[:, 0:512], scalar1=P.modT[:, s_, 1, kc:kc + 1],
                    scalar2=P.modT[:, s_, 0, kc:kc + 1], op0=ALU.mult, op1=ALU.add), [pr, P.modT_r], [uT_r])

        ei = [0]

        def mm(i):
            s_, mt = tiles[i]
            sc = P.SC[s_]
            R = sc["res"]
            t0 = mt * 512
            uT_t, uT_r = uT[i % 2]
            pend = []
            for m in range(52):
                if 30 <= m < 36:
                    continue
                pt, pr = psum()
                for kc in range(8):
                    S.op("pe", lambda e, kc=kc, m=m, pt=pt: e.matmul(
                        pt[:, :], lhsT=w_sb[:, kc, m * 128:(m + 1) * 128], rhs=uT_t[:, kc, :],
                        start=(kc == 0), stop=(kc == 7)), [wr[kc], uT_r], [pr], signal=(kc == 7))
                et, er = evb[ei[0] % 8]
                ei[0] += 1
                if m < 18:
                    if m % 2 == 0:
                        S.op("act", lambda e, pt=pt, et=et: e.activation(out=et[:], in_=pt[:, :], func=AF.Copy), [pr], [er])
                        S.dma(AQ, sc["zhy"][m * 128:(m + 1) * 128, 1 + t0:1 + t0 + 512], et[:], [er], [R["zhy"]], er)
                    else:
                        S.op("dve", lambda e, pt=pt, et=et: e.tensor_copy(out=et[:], in_=pt[:, :]), [pr], [er])
                        S.dma("sp", sc["zhy"][m * 128:(m + 1) * 128, 1 + t0:1 + t0 + 512], et[:], [er], [R["zhy"]], er)
                elif m < 30:
                    idx = m - 18
                    qf_t, qf_r = qf[idx % 4]
                    sq_t, sq_r = sq[idx % 4]
                    rs_t, rs_r = rs[idx % 4]
                    S.op("dve", lambda e, pt=pt, qf_t=qf_t: e.tensor_copy(out=qf_t[:], in_=pt[:, :]), [pr], [qf_r])
                    S.op("act", lambda e, qf_t=qf_t, sq_t=sq_t: e.activation(out=sq_t[:], in_=qf_t[:], func=AF.Square), [qf_r], [sq_r])
                    pt2, pr2 = psum()
                    S.op("pe", lambda e, pt2=pt2, sq_t=sq_t: e.matmul(pt2[:, :], lhsT=bd[:], rhs=sq_t[:], start=True, stop=True),
                         [bd_r, sq_r], [pr2])

                    def st1(pt2=pt2, pr2=pr2, rs_t=rs_t, rs_r=rs_r, idx=idx):
                        S.op("act", lambda e: e.activation(out=rs_t[:], in_=pt2[:, :], func=AF.Ln, scale=1.0 / HD, bias=epsb[:, 0:1]),
                             [pr2, epsb_r], [rs_r])
                        S.op("act", lambda e: e.activation(out=rs_t[:], in_=rs_t[:], func=AF.Exp, scale=-0.5, bias=gq[:, idx:idx + 1]),
                             [rs_r, gq_r], [rs_r])

                    def st2(rs_t=rs_t, rs_r=rs_r, qf_t=qf_t, qf_r=qf_r, et=et, er=er, idx=idx, t0=t0):
                        S.op("pool", lambda e: e.tensor_tensor(out=et[:], in0=qf_t[:], in1=rs_t[:], op=ALU.mult), [rs_r, qf_r], [er])
                        dst = sc["qT"] if idx < 6 else sc["kT"]
                        dr = R["qT"] if idx < 6 else R["kT"]
                        S.dma("sp", dst[(idx % 6) * 128:(idx % 6 + 1) * 128, t0:t0 + 512], et[:], [er], [dr], er)
                    pend.append([st1, st2])
                    if os.environ.get("NOSKEW"):
                        st1(); st2(); del pend[:]
                        continue
                    if len(pend) > 1:
                        pend[-2][0]()
                    if len(pend) > 2:
                        pend[-3][1]()
                    if idx == 11:
                        pend[-1][0]()
                        pend[-2][1]()
                        pend[-1][1]()
                        del pend[:]
                else:
                    S.op("act", lambda e, pt=pt, et=et: e.activation(out=et[:], in_=pt[:, :], func=AF.Sigmoid), [pr], [er])
                    S.dma(AQ, sc["gT"][(m - 36) * 128:(m - 35) * 128, t0:t0 + 512], et[:], [er], [R["gT"]], er)
            for jb in range(4):
                for c0, cw_ in ((3840, 512), (4352, 256)):
                    pt, pr = psum()
                    for kc in range(8):
                        S.op("pe", lambda e, kc=kc, jb=jb, c0=c0, cw_=cw_, pt=pt: e.matmul(
                            pt[:, 0:cw_], lhsT=uT_t[:, kc, jb * 128:(jb + 1) * 128], rhs=w_sb[:, kc, c0:c0 + cw_],
                            start=(kc == 0), stop=(kc == 7)), [wr[kc], uT_r], [pr], signal=(kc == 7))
                    et, er = evb[ei[0] % 8]
                    ei[0] += 1
                    S.op("dve", lambda e, pt=pt, et=et, cw_=cw_: e.tensor_copy(out=et[:, 0:cw_], in_=pt[:, 0:cw_]), [pr], [er])
                    S.dma("sp", sc["v"][t0 + jb * 128:t0 + (jb + 1) * 128, c0 - 3840:c0 - 3840 + cw_], et[:, 0:cw_],
                          [er], [R["v"]], er)

        load(0)
        if nt > 1:
            load(1)
        norm(0)
        xpose(0)
        for i in range(nt):
            if i + 1 < nt:
                norm(i + 1)
            if i + 2 < nt:
                load(i + 2)
            mm(i)
            if i + 1 < nt:
                xpose(i + 1)
        S.stage_end()


def stage_attn(P):
    nc, S, I, IN, sb, psum = P.nc, P.S, P.I, P.IN, P.sb, P.psum
    PAD = 1024
    with contextlib.ExitStack() as st:
        biasmat, bm_r = sb("biasmat", [128, 24, 128], F32, st)
        with contextlib.ExitStack() as st2:
            rbx, rbx_r = sb("rbx", [33, NH], F32, st2)
            anti, anti_r = sb("anti", [128, 128], F32, st2)
            oh = [sb(f"oh{g}", [33, 512], F32, st2) for g in range(3)]
            gv = [sb(f"gv{g}", [4, 512], F32, st2) for g in range(3)]
            hk = [sb(f"hk{i}", [128, 128], F32, st2) for i in range(4)]
            S.op("dve", lambda e: e.memset(rbx[:], 1.0), [], [rbx_r])
            S.dma("sp", rbx[0:32, :], I["rel_bias"][:, :], [IN], [rbx_r], rbx_r)
            S.dma("sp", anti[:], I["antiid"][:, :], [IN], [anti_r], anti_r)
            for g in range(3):
                S.dma("sp", oh[g][0][:], I["bias_oh"][g, :, :], [IN], [oh[g][1]], oh[g][1])
                pt, pr = psum()
                S.op("pe", lambda e, g=g, pt=pt: e.matmul(pt[0:4, :], lhsT=rbx[:, 4 * g:4 * g + 4], rhs=oh[g][0][:],
                                                          start=True, stop=True), [rbx_r, oh[g][1]], [pr])
                S.op("dve", lambda e, g=g, pt=pt: e.tensor_copy(out=gv[g][0][:], in_=pt[0:4, :]), [pr], [gv[g][1]])
                S.dma("sp", P.gvd[4 * g:4 * g + 4, :], gv[g][0][:], [gv[g][1]], [P.gvd_r], gv[g][1])
            for h in range(NH):
                for ab in range(2):
                    i = h * 2 + ab
                    ht, hr = hk[i % 4]
                    S.dma("sp", ht[:], bass.AP(P.gvd, h * 512 + ab * 256, [[1, 128], [1, 128]]), [P.gvd_r], [hr], hr)
                    pt, pr = psum()
                    S.op("pe", lambda e, pt=pt, ht=ht: e.matmul(pt[:, 0:128], lhsT=anti[:], rhs=ht[:], start=True, stop=True),
                         [anti_r, hr], [pr])
                    S.op("dve", lambda e, pt=pt, i=i: e.tensor_copy(out=biasmat[:, i, :], in_=pt[:, 0:128]), [pr], [bm_r])
            P.pe_flush()
            S.barrier()
        import os
        STOP = int(os.environ.get("ATTN_STOP", "9"))
        if STOP <= 1:
            S.stage_end()
            return
        Lmax = max(P.Ls)
        OQ = os.environ.get("OQ", "act")
        qz = [[sb(f"qz{i}{h}", [128, Lmax], BF16, st) for h in range(2)] for i in range(2)]
        for i in range(2):
            S.op("pool", lambda e, i=i: e.memset(qz[i][0][0][64:128, :], 0.0), [], [qz[i][0][1]])
            S.op("pool", lambda e, i=i: e.memset(qz[i][1][0][0:64, :], 0.0), [], [qz[i][1][1]])
        ks = [sb(f"ks{i}", [128, Lmax + 2 * PAD], BF16, st) for i in range(2)]
        vfull = [sb(f"vf{i}", [128, 4, 128], BF16, st) for i in range(3)]
        vfirst, vfirst_r = sb("vfirst", [128, 4, 128], BF16, st)
        vlast, vlast_r = sb("vlast", [128, 4, 128], BF16, st)
        scs = [sb(f"scs{i}", [128, 8, 128], F32, st) for i in range(2)]
        pT = [sb(f"pT{i}", [128, 8, 128], BF16, st) for i in range(2)]
        osb = [sb(f"osb{i}", [128, 260], F32, st) for i in range(3)]
        for t_, r_ in vfull:
            S.op("dve", lambda e, t_=t_: e.memset(t_[:], 1.0), [], [r_])
        S.op("dve", lambda e: e.memset(vfirst[:], 0.0), [], [vfirst_r])
        S.op("dve", lambda e: e.memset(vlast[:], 0.0), [], [vlast_r])
        S.op("dve", lambda e: e.memset(vfirst[64:128, :, 64:65], 1.0), [], [vfirst_r])
        S.op("dve", lambda e: e.memset(vlast[0:64, :, 64:65], 1.0), [], [vlast_r])
        for t_, r_ in ks:
            S.op("pool", lambda e, t_=t_: e.memset(t_[:], 0.0), [], [r_])
        it = 0
        if STOP <= 2:
            S.stage_end()
            return
        for s, L in enumerate(P.Ls):
            sc = P.SC[s]
            R = sc["res"]
            for g, d in enumerate((1, 4, 16)):
                if STOP <= 3 + g - 1 and g > 0:
                    break
                Ssub = L // d
                nblk = Ssub // 128
                for pp in range(2):
                    r0 = (4 * g + 2 * pp) * 64
                    S.dma("sp", qz[pp][0][0][0:64, 0:L], sc["qT"][r0:r0 + 64, :], [R["qT"]], [qz[pp][0][1]], qz[pp][0][1])
                    S.dma("sp", qz[pp][1][0][64:128, 0:L], sc["qT"][r0 + 64:r0 + 128, :], [R["qT"]], [qz[pp][1][1]], qz[pp][1][1])
                    S.dma("sp", ks[pp][0][:, PAD:PAD + L], sc["kT"][r0:r0 + 128, :], [R["kT"]], [ks[pp][1]], ks[pp][1])
                vview = sc["v"].rearrange("(s d) c -> d s c", d=d)
                oview = sc["od"][g].rearrange("(s d) c -> d s c", d=d)
                for r in range(d):
                    def vload(m):
                        if m == 0:
                            t_, r_ = vfirst, vfirst_r
                            S.dma("sp", t_[64:128, :, 0:64],
                                  vview[r, 0:64, g * 256:(g + 1) * 256].rearrange("s (h e) -> s h e", e=64),
                                  [R["v"]], [r_], r_)
                        elif m == nblk:
                            t_, r_ = vlast, vlast_r
                            S.dma("sp", t_[0:64, :, 0:64],
                                  vview[r, Ssub - 64:Ssub, g * 256:(g + 1) * 256].rearrange("s (h e) -> s h e", e=64),
                                  [R["v"]], [r_], r_)
                        else:
                            t_, r_ = vfull[m % 3]
                            S.dma("sp", t_[:, :, 0:64],
                                  vview[r, 128 * m - 64:128 * m + 64, g * 256:(g + 1) * 256].rearrange("s (h e) -> s h e", e=64),
                                  [R["v"]], [r_], r_)
                        return t_, r_
                    vt = {0: vload(0)}

                    def phase1(j):
                        nonlocal it
                        vt[j + 1] = vload(j + 1)
                        sc_t, sc_r = scs[it % 2]
                        p_t, p_r = pT[it % 2]
                        o_t, o_r = osb[it % 3]
                        it += 1
                        banks = [psum(), psum()]
                        for hh in range(4):
                            pp = hh // 2
                            qa = qz[pp][hh % 2][0][:, 128 * j * d + r:128 * j * d + r + 127 * d + 1:d]
                            for ab in range(2):
                                m = j + ab
                                k0 = PAD + (128 * m - 64) * d + r
                                ka = ks[pp][0][:, k0:k0 + 127 * d + 1:d]
                                idx = hh * 2 + ab
                                pt, pr = banks[idx // 4]
                                S.op("pe", lambda e, pt=pt, idx=idx, ka=ka, qa=qa: e.matmul(
                                    pt[:, (idx % 4) * 128:(idx % 4 + 1) * 128], lhsT=ka, rhs=qa, start=True, stop=True),
                                    [qz[pp][hh % 2][1], ks[pp][1]], [pr], signal=(idx % 4 == 3))
                        for b_ in range(2):
                            pt, pr = banks[b_]
                            S.op("dve", lambda e, pt=pt, b_=b_, sc_t=sc_t: e.tensor_tensor(
                                out=sc_t[:, 4 * b_:4 * b_ + 4, :], in0=pt[:, :].rearrange("p (a q) -> p a q", q=128),
                                in1=biasmat[:, 8 * g + 4 * b_:8 * g + 4 * b_ + 4, :], op=ALU.add), [pr, bm_r], [sc_r])
                        S.op("act", lambda e, sc_t=sc_t, p_t=p_t: e.activation(out=p_t[:], in_=sc_t[:], func=AF.Exp), [sc_r], [p_r])
                        return (j, p_t, p_r, o_t, o_r, vt[j], vt[j + 1], it)

                    def phase2(st_):
                        j, p_t, p_r, o_t, o_r, va, vb, itn = st_
                        po, por = psum()
                        for hh in range(4):
                            for ab, (vt_t, vt_r) in enumerate((va, vb)):
                                S.op("pe", lambda e, hh=hh, ab=ab, vt_t=vt_t, p_t=p_t, po=po: e.matmul(
                                    po[:, hh * 128:hh * 128 + 65], lhsT=p_t[:, hh * 2 + ab, :], rhs=vt_t[:, hh, 0:65],
                                    start=(ab == 0), stop=(ab == 1)), [p_r, vt_r], [por], signal=(hh == 3 and ab == 1))
                        if itn % 2:
                            S.op("dve", lambda e, po=po, o_t=o_t: e.tensor_copy(
                                out=o_t[:].rearrange("p (h c) -> p h c", h=4),
                                in_=po[:, :].rearrange("p (h c) -> p h c", h=4)[:, :, 0:65]), [por], [o_r])
                        else:
                            S.op("act", lambda e, po=po, o_t=o_t: e.activation(
                                out=o_t[:].rearrange("p (h c) -> p h c", h=4),
                                in_=po[:, :].rearrange("p (h c) -> p h c", h=4)[:, :, 0:65], func=AF.Copy), [por], [o_r])
                        S.dma(OQ, oview[r, 128 * j:128 * (j + 1), :], o_t[:], [o_r], [R["od"]], o_r)

                    prev = None
                    for j in range(nblk):
                        cur = phase1(j)
                        if prev is not None:
                            phase2(prev)
                        prev = cur
                        vt.pop(j - 1, None)
                    phase2(prev)
        S.stage_end()


def stage_hyena(P):
    nc, S, I, IN, sb, psum = P.nc, P.S, P.I, P.IN, P.sb, P.psum
    PI = math.pi
    for s, L in enumerate(P.Ls):
        sc = P.SC[s]
        R = sc["res"]
        N = 2 * L
        N1 = N // 128
        H1 = N1 // 2
        ncg = L // 512
        h3d = nc.dram_tensor(f"h3d{s}", [2, 64, L], BF16, kind="Internal")
        h3d_r = Res(f"h3d{s}", multi=True)
        with contextlib.ExitStack() as st:
            w1, w1_r = sb("fw1", [33, 64], F32, st)
            w2, w2_r = sb("fw2", [64, 64], F32, st)
            w3, w3_r = sb("fw3", [64, 64], F32, st)
            fb, fb_r = sb("fb", [64, 4], F32, st)
            S.dma("sp", w1[:], I["filt_w1"][:, :], [IN], [w1_r], w1_r)
            S.dma("sp", w2[:], I["filt_w2"][:, :], [IN], [w2_r], w2_r)
            S.dma("sp", w3[:], I["filt_w3"][:, :], [IN], [w3_r], w3_r)
            with nc.allow_non_contiguous_dma(reason="tiny"):
                for i, nm in enumerate(("filt_b1", "filt_b2", "filt_b3", "filt_freq")):
                    S.dma("sp", fb[:, i:i + 1], I[nm][0, :].rearrange("(p o) -> p o", o=1), [IN], [fb_r], fb_r)
            G = 4
            ft = [sb(f"ft{i}", [33, 512], F32, st) for i in range(G)]
            ha = [[sb(f"ha{i}_{l}", [64, 512], F32, st) for l in range(3)] for i in range(G)]
            kts = [sb(f"kt{i}", [64, 512], F32, st) for i in range(G)]
            hbf = [sb(f"hbf{i}", [64, 512], BF16, st) for i in range(G)]
            items = [(dr, cg) for dr in range(2) for cg in range(ncg)]
            for b0 in range(0, len(items), G):
                batch = items[b0:b0 + G]
                cur = []
                for gi_, (dr, cg) in enumerate(batch):
                    ft_t, ft_r = ft[gi_]
                    S.dma("sp", ft_t[:], I[f"feats{L}"][dr, :, cg * 512:(cg + 1) * 512], [IN], [ft_r], ft_r)
                    cur.append((ft_t, ft_r, 33))
                for li, (w_, w_r) in enumerate(((w1, w1_r), (w2, w2_r), (w3, w3_r))):
                    nxt = []
                    for gi_ in range(len(batch)):
                        c_t, c_r, kdim = cur[gi_]
                        h_t, h_r = ha[gi_][li]
                        kt, kt_r = kts[gi_]
                        pt, pr = psum()
                        S.op("pe", lambda e, pt=pt, w_=w_, c_t=c_t, kdim=kdim: e.matmul(
                            pt[0:64, :], lhsT=w_[0:kdim, :], rhs=c_t[0:kdim, :], start=True, stop=True), [w_r, c_r], [pr])
                        S.op("dve", lambda e, pt=pt, h_t=h_t, li=li: e.tensor_scalar(
                            out=h_t[:], in0=pt[0:64, :], scalar1=fb[:, li:li + 1], scalar2=fb[:, 3:4],
                            op0=ALU.add, op1=ALU.mult), [pr, fb_r], [h_r])
                        S.op("dve", lambda e, h_t=h_t, kt=kt: e.tensor_scalar(
                            out=kt[:], in0=h_t[:], scalar1=1.0 / TWO_PI, scalar2=MAGIC, op0=ALU.mult, op1=ALU.add), [h_r], [kt_r])
                        S.op("dve", lambda e, kt=kt: e.tensor_scalar_add(out=kt[:], in0=kt[:], scalar1=-MAGIC), [kt_r], [kt_r])
                        S.op("dve", lambda e, h_t=h_t, kt=kt: e.scalar_tensor_tensor(
                            out=h_t[:], in0=kt[:], scalar=-TWO_PI, in1=h_t[:], op0=ALU.mult, op1=ALU.add), [kt_r, h_r], [h_r])
                        S.op("dve", lambda e, h_t=h_t: e.tensor_scalar(
                            out=h_t[:], in0=h_t[:], scalar1=-3.1415925, scalar2=3.1415925, op0=ALU.max, op1=ALU.min), [h_r], [h_r])
                        if li == 2:
                            o_t, o_r = hbf[gi_]
                            S.op("act", lambda e, h_t=h_t, o_t=o_t: e.activation(out=o_t[:], in_=h_t[:], func=AF.Sin), [h_r], [o_r])
                            nxt.append((o_t, o_r, 64))
                        else:
                            S.op("act", lambda e, h_t=h_t: e.activation(out=h_t[:], in_=h_t[:], func=AF.Sin), [h_r], [h_r])
                            nxt.append((h_t, h_r, 64))
                    cur = nxt
                for gi_, (dr, cg) in enumerate(batch):
                    c_t, c_r, _ = cur[gi_]
                    S.dma("act", h3d[dr, :, cg * 512:(cg + 1) * 512], c_t[:], [c_r], [h3d_r], c_r)
            P.pe_flush()
            S.stage_end()
        with contextlib.ExitStack() as st:
            wo, wo_r = sb("fwo", [64, 2 * DH], BF16, st)
            nd, nd_r = sb("nd", [128, 6], F32, st)
            S.dma("pool", wo[:], I["filt_w_out"][:, :], [IN], [wo_r], wo_r)
            S.dma("sp", nd[:], I["ndelta"][:, :], [IN], [nd_r], nd_r)
            kk, kk_r = sb("kk", [128, 2, L], F32, st)
            k2b, k2b_r = sb("k2b", [128, 2 * L], BF16, st)
            asum, asum_r = sb("asum", [128, 1], F32, st)
            h3 = [sb(f"h3_{i}", [64, 512], BF16, st) for i in range(4)]
            tv = [sb(f"tv{i}", [128, 512], F32, st) for i in range(4)]
            it = 0
            for cc in range(6):
                for dr in range(2):
                    for cg in range(ncg):
                        h_t, h_r = h3[it % 4]
                        t_t, t_r = tv[it % 4]
                        it += 1
                        S.dma("sp", h_t[:], h3d[dr, :, cg * 512:(cg + 1) * 512], [h3d_r], [h_r], h_r)
                        S.dma("sp", t_t[:], I[f"tvec{L}"][dr:dr + 1, cg * 512:(cg + 1) * 512].partition_broadcast(128),
                              [IN], [t_r], t_r)
                        pt, pr = psum()
                        S.op("pe", lambda e, pt=pt, h_t=h_t, dr=dr, cc=cc: e.matmul(
                            pt[:, :], lhsT=wo[:, dr * DH + cc * 128:dr * DH + (cc + 1) * 128], rhs=h_t[:],
                            start=True, stop=True), [wo_r, h_r], [pr])
                        S.op("act", lambda e, t_t=t_t, cc=cc: e.activation(out=t_t[:], in_=t_t[:], func=AF.Exp,
                                                                            scale=nd[:, cc:cc + 1]), [t_r, nd_r], [t_r])
                        S.op("dve", lambda e, pt=pt, t_t=t_t, dr=dr, cg=cg: e.tensor_tensor(
                            out=kk[:, dr, cg * 512:(cg + 1) * 512], in0=pt[:, :], in1=t_t[:], op=ALU.mult), [pr, t_r], [kk_r])
                S.op("dve", lambda e: e.memset(kk[:, 1, 0:1], 0.0), [], [kk_r])
                S.op("dve", lambda e: e.memset(asum[:], 0.0), [], [asum_r])
                S.op("act", lambda e: e.activation(out=k2b[:], in_=kk[:].rearrange("p a l -> p (a l)"), func=AF.Abs,
                                                   accum_out=asum[:]), [kk_r, asum_r], [k2b_r, asum_r])
                S.op("dve", lambda e: e.reciprocal(out=asum[:], in_=asum[:]), [asum_r], [asum_r])
                S.op("act", lambda e: e.activation(out=k2b[:], in_=kk[:].rearrange("p a l -> p (a l)"), func=AF.Copy,
                                                   scale=asum[:, 0:1]), [kk_r, asum_r], [k2b_r])
                S.dma("act", sc["k2"][cc * 128:(cc + 1) * 128, :], k2b[:], [k2b_r], [R["k2"]], k2b_r)
            P.pe_flush()
            S.stage_end()
        with contextlib.ExitStack() as st:
            cw, cw_r = sb("cw", [128, 3, 18], F32, st)
            cb, cb_r = sb("cb", [128, 18], F32, st)
            hd, hd_r = sb("hd", [128, 6], F32, st)
            with nc.allow_non_contiguous_dma(reason="tiny"):
                for k in range(3):
                    S.dma("sp", cw[:, k, :], I["conv_w"][k, :].rearrange("(c p) -> p c", p=128), [IN], [cw_r], cw_r)
                S.dma("sp", cb[:], I["conv_b"][0, :].rearrange("(c p) -> p c", p=128), [IN], [cb_r], cb_r)
                S.dma("sp", hd[:], I["hyena_d"][0, :].rearrange("(c p) -> p c", p=128), [IN], [hd_r], hd_r)
            f1t, f1t_r = sb("f1t", [N1, 2 * N1], BF16, st)
            i1t, i1t_r = sb("i1t", [128, 2, 256], BF16, st)
            S.dma("sp", f1t[:], I[f"f1tab{L}"][:, :], [IN], [f1t_r], f1t_r)
            S.dma("sp", i1t[:], I[f"i1tab{L}"][:, :, :], [IN], [i1t_r], i1t_r)
            xin, xin_r = sb("xin", [128, 128, 128], BF16, st)
            B1, B1_r = sb("B1", [128, 32768], BF16, st)
            B2, B2_r = sb("B2", [128, 2 * N1 * 128], BF16, st)
            B1_r.multi = True
            B2_r.multi = True
            dsA, dsB, dsC, dsD = Res("dsA"), Res("dsB"), Res("dsC"), Res("dsD")
            B1f = B1.bitcast(F32)
            B2f = B2.bitcast(F32)
            gts = [sb(f"gts{i}", [128, 2, 3, 128], BF16, st) for i in range(4)]
            kfs = [sb(f"kfs{i}", [128, 512], F32, st) for i in range(4)]
            i2s = [sb(f"i2s{i}", [N1, 512 // H1 if H1 * 128 > 512 else 128, 2, H1], BF16, st) for i in range(2)]
            tmp = [sb(f"ctmp{i}", [128, 2, 128], F32, st) for i in range(8)]
            xsb = [sb(f"xsb{i}", [128, 512], F32, st) for i in range(2)]
            tg = min(128, 512 // H1)
            KH = N1 // 2 + 2
            BT = B1[:, 0:2 * N1 * 128].rearrange("p (r k c) -> p r k c", r=2, k=N1)
            Zb = B1[:, :].rearrange("p (r t c) -> p r t c", r=2, t=128)
            Yb = B2[:, :].rearrange("p (r c k) -> p r c k", r=2, c=128)
            yv = B2f[:, 0:L].rearrange("p (a b) -> p a b", b=128)
            gi = 0

            def f1_pass(K):
                nonlocal gi
                for c0 in range(0, 128, 2):
                    pt, pr = psum()
                    for u in range(2):
                        S.op("pe", lambda e, pt=pt, u=u, c0=c0: e.matmul(
                            pt[:, u * 2 * N1:(u + 1) * 2 * N1], lhsT=xin[0:K, c0 + u, :], rhs=f1t[0:K, :],
                            start=True, stop=True), [xin_r, f1t_r], [pr], signal=(u == 1))
                    src = pt[:, 0:4 * N1].rearrange("p (u r k) -> p r k u", u=2, r=2)
                    gi += 1
                    if gi % 2:
                        S.op("act", lambda e, src=src, c0=c0: e.activation(out=BT[:, :, :, c0:c0 + 2], in_=src, func=AF.Copy),
                             [pr], [B1_r])
                    else:
                        S.op("dve", lambda e, src=src, c0=c0: e.tensor_copy(out=BT[:, :, :, c0:c0 + 2], in_=src), [pr], [B1_r])

            def f3_pass(cc, is_filter):
                pend_c = []
                f3_body(cc, is_filter, pend_c)
                if pend_c:
                    pend_c.pop()()

            def f3_body(cc, is_filter, pend_c):
                for q in range(N1 // 4 + 1):
                    g_t, g_r = gts[q % 4]
                    S.dma("sp", g_t[:], I[f"gtab{L}"][:, 2 * q:2 * q + 2, :, :], [IN], [g_r], g_r)
                    pt, pr = psum()
                    pv = pt[:, :].rearrange("p (u r c) -> p u r c", u=2, r=2)
                    for u in range(2):
                        k1 = 2 * q + u
                        for ri, (ga, gb) in enumerate(((0, 2), (1, 0))):
                            S.op("pe", lambda e, pv=pv, u=u, ri=ri, ga=ga, k1=k1, g_t=g_t: e.matmul(
                                pv[:, u, ri, :], lhsT=g_t[:, u, ga, :], rhs=BT[:, 0, k1, :], start=True, stop=False),
                                [g_r, B1_r], [pr], signal=False)
                            S.op("pe", lambda e, pv=pv, u=u, ri=ri, gb=gb, k1=k1, g_t=g_t: e.matmul(
                                pv[:, u, ri, :], lhsT=g_t[:, u, gb, :], rhs=BT[:, 1, k1, :], start=False, stop=True),
                                [g_r, B1_r], [pr], signal=(u == 1 and ri == 1))
                    k_t, k_r = kfs[q % 4]
                    if is_filter:
                        S.op("act", lambda e, pt=pt, k_t=k_t: e.activation(out=k_t[:], in_=pt[:, :], func=AF.Copy), [pr], [k_r])
                        S.dma("act", sc["kf"][cc, q, :, :], k_t[:], [k_r], [R["kf"]], k_r)
                    else:
                        S.dma("sp", k_t[:], sc["kf"][cc, q, :, :], [R["kf"]], [k_r], k_r)
                        kv = k_t[:, :].rearrange("p (u r c) -> p u r c", u=2, r=2)
                        x_t, x_r = xsb[q % 2]
                        S.op("act", lambda e, pt=pt, x_t=x_t: e.activation(out=x_t[:], in_=pt[:, :], func=AF.Copy), [pr], [x_r])
                        xv = x_t[:, :].rearrange("p (u r c) -> p u r c", u=2, r=2)
                        tr = [tmp[(q % 2) * 4 + i] for i in range(4)]
                        for i, (xa, ka, eng) in enumerate(((0, 1, "pool"), (1, 0, "pool"), (0, 0, "dve"), (1, 1, "dve"))):
                            S.op(eng, lambda e, i=i, xa=xa, ka=ka, xv=xv, kv=kv, tr=tr: e.tensor_tensor(
                                out=tr[i][0][:], in0=xv[:, :, xa, :], in1=kv[:, :, ka, :], op=ALU.mult), [x_r, k_r], [tr[i][1]])

                        def comb(q=q, tr=tr):
                            S.op("dve", lambda e: e.tensor_tensor(
                                out=Yb[:, 0, :, 2 * q:2 * q + 2], in0=tr[2][0][:].rearrange("p u c -> p c u"),
                                in1=tr[3][0][:].rearrange("p u c -> p c u"), op=ALU.subtract), [tr[2][1], tr[3][1]], [B2_r])
                            S.op("dve", lambda e: e.tensor_tensor(
                                out=Yb[:, 1, :, 2 * q:2 * q + 2], in0=tr[0][0][:].rearrange("p u c -> p c u"),
                                in1=tr[1][0][:].rearrange("p u c -> p c u"), op=ALU.add), [tr[0][1], tr[1][1]], [B2_r])
                        if pend_c:
                            pend_c.pop()()
                        pend_c.append(comb)

            for cc in range(6):
                T1 = B1[:, 0:L + 2]
                T2 = B1[:, L + 2:2 * L + 4]
                AO = B1[:, 2 * L + 4:3 * L + 4]
                x1c = B2f[:, 0:L]
                vc = B2f[:, L:2 * L]
                S.dma("sp", T1, sc["zhy"][DH + cc * 128:DH + (cc + 1) * 128, :], [R["zhy"]], [B1_r], dsA)
                S.dma("sp", T2, sc["zhy"][2 * DH + cc * 128:2 * DH + (cc + 1) * 128, :], [R["zhy"]], [B1_r], dsB)
                for src, dst, ch in ((T1, x1c, 6 + cc), (T2, vc, 12 + cc)):
                    S.op("act", lambda e, src=src, dst=dst, ch=ch: e.activation(
                        out=dst, in_=src[:, 1:L + 1], func=AF.Identity, scale=cw[:, 1, ch:ch + 1], bias=cb[:, ch:ch + 1]),
                        [B1_r, cw_r, cb_r], [B2_r])
                    for k in (0, 2):
                        S.op("dve", lambda e, src=src, dst=dst, ch=ch, k=k: e.scalar_tensor_tensor(
                            out=dst, in0=src[:, k:L + k], scalar=cw[:, k, ch:ch + 1], in1=dst, op0=ALU.mult, op1=ALU.add),
                            [B1_r, B2_r, cw_r], [B2_r])
                S.op("dve", lambda e: e.tensor_tensor(out=AO, in0=x1c, in1=vc, op=ALU.mult), [B2_r], [B1_r])
                S.dma("pool", sc["aT"][cc * 128:(cc + 1) * 128, :], AO, [B1_r], [R["aT"]], dsC)
                if cc == 0:
                    S.dma("sp", xin[0:N1, :, :], sc["k2"][0:128, :].rearrange("c (a b) -> a c b", b=128),
                          [R["k2"]], [xin_r], xin_r)
                f1_pass(N1)
                S.dma("sp", xin[0:H1, :, :], sc["aT"][cc * 128:(cc + 1) * 128, :].rearrange("c (a b) -> a c b", b=128),
                      [R["aT"]], [xin_r], xin_r)
                f3_pass(cc, True)
                f1_pass(H1)
                if cc + 1 < 6:
                    S.dma("act", xin[0:N1, :, :], sc["k2"][(cc + 1) * 128:(cc + 2) * 128, :].rearrange("c (a b) -> a c b", b=128),
                          [R["k2"]], [xin_r], xin_r)
                f3_pass(cc, False)
                for c0 in range(0, 128, 2):
                    pt, pr = psum()
                    for u in range(2):
                        for ri in range(2):
                            S.op("pe", lambda e, pt=pt, u=u, ri=ri, c0=c0: e.matmul(
                                pt[0:KH, u * 256:(u + 1) * 256], lhsT=Yb[:, ri, c0 + u, 0:KH], rhs=i1t[:, ri, :],
                                start=(ri == 0), stop=(ri == 1)), [B2_r, i1t_r], [pr], signal=(u == 1 and ri == 1))
                    src = pt[0:KH, :].rearrange("p (u r t) -> p r t u", u=2, r=2)
                    gi += 1
                    if gi % 2:
                        S.op("act", lambda e, src=src, c0=c0: e.activation(out=Zb[0:KH, :, :, c0:c0 + 2], in_=src, func=AF.Copy),
                             [pr], [B1_r])
                    else:
                        S.op("dve", lambda e, src=src, c0=c0: e.tensor_copy(out=Zb[0:KH, :, :, c0:c0 + 2], in_=src), [pr], [B1_r])
                for t2g in range(128 // tg):
                    i_t, i_r = i2s[t2g % 2]
                    S.dma("sp", i_t[:, 0:tg, :, :], I[f"i2tab{L}"][:, t2g * tg:(t2g + 1) * tg, :, :], [IN], [i_r], i_r)
                    pt, pr = psum()
                    pv = pt[:, 0:H1 * tg].rearrange("p (a w) -> p a w", w=tg)
                    for w in range(tg):
                        t2 = t2g * tg + w
                        for ri in range(2):
                            S.op("pe", lambda e, pv=pv, w=w, t2=t2, ri=ri, i_t=i_t: e.matmul(
                                pv[:, :, w], lhsT=Zb[0:KH, ri, t2, :], rhs=i_t[0:KH, w, ri, :], start=(ri == 0), stop=(ri == 1)),
                                [B1_r, i_r], [pr], signal=(w == tg - 1 and ri == 1))
                    gi += 1
                    if gi % 2:
                        S.op("act", lambda e, pv=pv, t2g=t2g: e.activation(out=yv[:, :, t2g * tg:(t2g + 1) * tg], in_=pv, func=AF.Copy),
                             [pr], [B2_r])
                    else:
                        S.op("dve", lambda e, pv=pv, t2g=t2g: e.tensor_copy(out=yv[:, :, t2g * tg:(t2g + 1) * tg], in_=pv), [pr], [B2_r])
                E1 = B1[:, 0:L]
                E2 = B1[:, L:2 * L + 2]
                E4 = B1[:, 2 * L + 2:3 * L + 2]
                ysb = B2f[:, 0:L]
                x0c = B2f[:, L:2 * L]
                S.dma("sp", E1, sc["aT"][cc * 128:(cc + 1) * 128, :], [R["aT"]], [B1_r], dsA)
                S.dma("sp", E2, sc["zhy"][cc * 128:(cc + 1) * 128, :], [R["zhy"]], [B1_r], dsB)
                S.op("act", lambda e, cc=cc: e.activation(out=x0c, in_=E2[:, 1:L + 1], func=AF.Identity,
                                                          scale=cw[:, 1, cc:cc + 1], bias=cb[:, cc:cc + 1]),
                     [B1_r, cw_r, cb_r], [B2_r])
                for k in (0, 2):
                    S.op("dve", lambda e, cc=cc, k=k: e.scalar_tensor_tensor(
                        out=x0c, in0=E2[:, k:L + k], scalar=cw[:, k, cc:cc + 1], in1=x0c, op0=ALU.mult, op1=ALU.add),
                        [B1_r, B2_r, cw_r], [B2_r])
                S.op("dve", lambda e, cc=cc: e.scalar_tensor_tensor(
                    out=ysb, in0=E1, scalar=hd[:, cc:cc + 1], in1=ysb, op0=ALU.mult, op1=ALU.add), [B1_r, B2_r, hd_r], [B2_r])
                S.op("dve", lambda e: e.tensor_tensor(out=E4, in0=ysb, in1=x0c, op=ALU.mult), [B2_r], [B1_r])
                S.dma("pool", sc["yhy"][cc * 128:(cc + 1) * 128, :], E4, [B1_r], [R["yhy"]], dsD)
            S.stage_end()


def stage_D1(P):
    nc, S, I, IN, sb, psum = P.nc, P.S, P.I, P.IN, P.sb, P.psum
    with contextlib.ExitStack() as st:
        whb, _ = sb("whb", [128, 6, D], BF16, st)
        wab, _ = sb("wab", [128, 2, D], BF16, st)
        wo, _ = sb("wo", [128, 8, D], BF16, st)
        whb_r, wab_r, wo_r = Res("whb"), Res("wab"), Res("wo")
        S.dma("pool", whb[:], I["w_hy_br"].rearrange("(k p) n -> p k n", p=128), [IN], [whb_r], whb_r)
        S.dma("pool", wab[:], I["w_at_br"].rearrange("(k p) n -> p k n", p=128), [IN], [wab_r], wab_r)
        S.dma("pool", wo[:], I["w_out"].rearrange("(k p) n -> p k n", p=128), [IN], [wo_r], wo_r)
        gtb1 = [sb(f"gt1_{s_}", [128, D], F32, st) for s_ in range(len(P.Ls))]
        for s_ in range(len(P.Ls)):
            S.dma("sp", gtb1[s_][0][:], P.gtbd[:, (s_ * 2) * D:(s_ * 2 + 1) * D], [P.gtbd_r], [gtb1[s_][1]], gtb1[s_][1])
        yh = [sb(f"yh{i}", [128, 6, 512], BF16, st) for i in range(2)]
        gg = [sb(f"gg{i}", [128, 16, 512], BF16, st) for i in range(2)]
        odt = [sb(f"odt{i}", [128, 4, 3, 260], F32, st) for i in range(2)]
        osums = [sb(f"osum{j}", [128, 4, 65], F32, st) for j in range(4)]
        rdens = [sb(f"rden{j}", [128, 4], F32, st) for j in range(4)]
        yatb = [sb(f"yat{i}", [128, 4, 256], BF16, st) for i in range(2)]
        yatTb = [sb(f"yatT{i}", [128, 2, 512], BF16, st) for i in range(2)]
        mix, mix_r = sb("mix", [128, 8, 512], BF16, st)
        xt = [sb(f"xd{i}", [128, 4, D], F32, st) for i in range(2)]
        tm = [sb(f"tm{i}", [128, 512], F32, st) for i in range(6)]
        ti = [0]
        tiles = [(s_, mt) for s_, L in enumerate(P.Ls) for mt in range(L // 512)]
        nt = len(tiles)

        def load(i):
            s_, mt = tiles[i]
            sc = P.SC[s_]
            R = sc["res"]
            t0 = mt * 512
            S.dma("sp", yh[i % 2][0][:], sc["yhy"][:, t0:t0 + 512].rearrange("(k p) t -> p k t", p=128), [R["yhy"]],
                  [yh[i % 2][1]], yh[i % 2][1])
            S.dma("sp", gg[i % 2][0][:], sc["gT"][:, t0:t0 + 512].rearrange("(k p) t -> p k t", p=128), [R["gT"]],
                  [gg[i % 2][1]], gg[i % 2][1])
            S.dma("sp", xt[i % 2][0][:], I[f"x{s_}"][t0:t0 + 512, :].rearrange("(j p) d -> p j d", p=128), [IN],
                  [xt[i % 2][1]], xt[i % 2][1])
            for jb in range(4):
                S.dma("act", odt[i % 2][0][:, jb, :, :],
                      sc["od"][:, t0 + jb * 128:t0 + (jb + 1) * 128, :].rearrange("g t c -> t g c"),
                      [R["od"]], [odt[i % 2][1]], Res(f"odsem{i % 2}{jb}") if False else odsem[i % 2][jb])

        odsem = [[Res(f"odsem{a_}{b_}") for b_ in range(4)] for a_ in range(2)]

        def merge(i):
            o_t, o_r = odt[i % 2]
            yat, yat_r = yatb[i % 2]
            ovs = [osums[jb][0][:].rearrange("p h c -> p (h c)") for jb in range(4)]
            for jb in range(4):
                S.op("dve", lambda e, jb=jb: e.tensor_tensor(out=ovs[jb], in0=o_t[:, jb, 0, :], in1=o_t[:, jb, 1, :], op=ALU.add),
                     [o_r], [osums[jb][1]])
            for jb in range(4):
                S.op("dve", lambda e, jb=jb: e.tensor_tensor(out=ovs[jb], in0=ovs[jb], in1=o_t[:, jb, 2, :], op=ALU.add),
                     [o_r, osums[jb][1]], [osums[jb][1]])
            for jb in range(4):
                S.op("dve", lambda e, jb=jb: e.reciprocal(out=rdens[jb][0][:], in_=osums[jb][0][:, :, 64]),
                     [osums[jb][1]], [rdens[jb][1]])
            for hh in range(4):
                for jb in range(4):
                    S.op("dve", lambda e, hh=hh, jb=jb: e.tensor_scalar_mul(
                        out=yat[:, jb, hh * 64:(hh + 1) * 64], in0=osums[jb][0][:, hh, 0:64], scalar1=rdens[jb][0][:, hh:hh + 1]),
                        [osums[jb][1], rdens[jb][1]], [yat_r])

        def xpose(i):
            yat, yat_r = yatb[i % 2]
            yatT, yatT_r = yatTb[i % 2]
            for fc in range(2):
                pt, pr = psum()
                ptb = pt.bitcast(BF16)
                for jb in range(4):
                    S.op("pe", lambda e, ptb=ptb, jb=jb, fc=fc: e.transpose(
                        out=ptb[:, jb * 128:(jb + 1) * 128], in_=yat[:, jb, fc * 128:(fc + 1) * 128], identity=P.ident_b[:]),
                        [yat_r, P.ident_b_r], [pr], signal=(jb == 3))
                S.op("act", lambda e, ptb=ptb, fc=fc: e.activation(out=yatT[:, fc, :], in_=ptb[:, 0:512], func=AF.Copy),
                     [pr], [yatT_r])

        def mm1(i):
            yh_t, yh_r = yh[i % 2]
            gg_t, gg_r = gg[i % 2]
            yatT, yatT_r = yatTb[i % 2]
            for m in range(8):
                p1, p1r = psum()
                for kc in range(6):
                    S.op("pe", lambda e, p1=p1, kc=kc, m=m: e.matmul(
                        p1[:, :], lhsT=whb[:, kc, m * 128:(m + 1) * 128], rhs=yh_t[:, kc, :], start=(kc == 0), stop=(kc == 5)),
                        [whb_r, yh_r], [p1r], signal=(kc == 5))
                p2, p2r = psum()
                for kc in range(2):
                    S.op("pe", lambda e, p2=p2, kc=kc, m=m: e.matmul(
                        p2[:, :], lhsT=wab[:, kc, m * 128:(m + 1) * 128], rhs=yatT[:, kc, :], start=(kc == 0), stop=(kc == 1)),
                        [wab_r, yatT_r], [p2r], signal=(kc == 1))
                ta, ta_r = tm[ti[0] % 6]
                tb, tb_r = tm[(ti[0] + 1) % 6]
                ti[0] += 2
                S.op("dve", lambda e, p1=p1, m=m, ta=ta: e.tensor_tensor(out=ta[:], in0=p1[:, :], in1=gg_t[:, m, :], op=ALU.mult),
                     [p1r, gg_r], [ta_r])
                S.op("dve", lambda e, p2=p2, m=m, tb=tb: e.tensor_tensor(out=tb[:], in0=p2[:, :], in1=gg_t[:, 8 + m, :], op=ALU.mult),
                     [p2r, gg_r], [tb_r])
                S.op("pool", lambda e, m=m, ta=ta, tb=tb: e.tensor_tensor(out=mix[:, m, :], in0=ta[:], in1=tb[:], op=ALU.add),
                     [ta_r, tb_r], [mix_r])

        def mm2(i):
            s_, mt = tiles[i]
            sc = P.SC[s_]
            x_t, x_r = xt[i % 2]
            gt1, gt1_r = gtb1[s_]
            for jb in range(4):
                for half in range(2):
                    pt, pr = psum()
                    for m in range(8):
                        S.op("pe", lambda e, pt=pt, m=m, jb=jb, half=half: e.matmul(
                            pt[:, :], lhsT=mix[:, m, jb * 128:(jb + 1) * 128], rhs=wo[:, m, half * 512:(half + 1) * 512],
                            start=(m == 0), stop=(m == 7)), [mix_r, wo_r], [pr], signal=(m == 7))
                    ta, ta_r = tm[ti[0] % 6]
                    ti[0] += 1
                    S.op("dve", lambda e, pt=pt, half=half, ta=ta: e.tensor_tensor(
                        out=ta[:], in0=pt[:, :], in1=gt1[:, half * 512:(half + 1) * 512], op=ALU.mult), [pr, gt1_r], [ta_r])
                    S.op("pool", lambda e, jb=jb, half=half, ta=ta: e.tensor_tensor(
                        out=x_t[:, jb, half * 512:(half + 1) * 512], in0=ta[:], in1=x_t[:, jb, half * 512:(half + 1) * 512],
                        op=ALU.add), [ta_r, x_r], [x_r])
            S.dma("sp", sc["h"][mt * 512:(mt + 1) * 512, :].rearrange("(j p) d -> p j d", p=128), x_t[:], [x_r],
                  [sc["res"]["h"]], x_r)

        load(0)
        merge(0)
        xpose(0)
        for i in range(nt):
            if i + 1 < nt:
                load(i + 1)
            mm1(i)
            if i + 1 < nt:
                merge(i + 1)
                xpose(i + 1)
            mm2(i)
        S.stage_end()


def stage_D2(P):
    nc, S, I, IN, sb, psum = P.nc, P.S, P.I, P.IN, P.sb, P.psum
    with contextlib.ExitStack() as st:
        wup, _ = sb("wup", [128, 8, DFF], BF16, st)
        wdn, _ = sb("wdn", [128, 32, D], BF16, st)
        wup_r = [Res(f"wup{k}") for k in range(8)]
        wdn_r = [Res(f"wdn{k}") for k in range(4)]
        for kc in range(8):
            S.dma("pool", wup[:, kc, :], I["w_up"][kc * 128:(kc + 1) * 128, :], [IN], [wup_r[kc]], wup_r[kc])
        for k in range(4):
            S.dma("pool", wdn[:, 8 * k:8 * k + 8, :], I["w_down"][1024 * k:1024 * (k + 1), :].rearrange("(f p) n -> p f n", p=128),
                  [IN], [wdn_r[k]], wdn_r[k])
        gtb2 = [sb(f"gt2_{s_}", [128, D], F32, st) for s_ in range(len(P.Ls))]
        for s_ in range(len(P.Ls)):
            S.dma("sp", gtb2[s_][0][:], P.gtbd[:, (s_ * 2 + 1) * D:(s_ * 2 + 2) * D], [P.gtbd_r], [gtb2[s_][1]], gtb2[s_][1])
        htb = [sb(f"ht{i}", [128, 2, D], F32, st) for i in range(3)]
        xnb = [sb(f"xn2_{i}", [128, 2, D], BF16, st) for i in range(2)]
        uT = [sb(f"u2T{i}", [128, 8, 256], BF16, st) for i in range(2)]
        hid, hid_r = sb("hid", [128, 32, 256], BF16, st)
        ssb = [sb(f"ss2_{i}", [128, 2], F32, st) for i in range(2)]
        junk, junk_r = sb("junk2", [128, D], BF16, st)
        tm = [sb(f"tn{i}", [128, 512], F32, st) for i in range(3)]
        ti = [0]
        tiles = [(s_, mt) for s_, L in enumerate(P.Ls) for mt in range(L // 256)]
        nt = len(tiles)

        def load(i):
            s_, mt = tiles[i]
            ht, ht_r = htb[i % 3]
            S.dma("sp", ht[:], P.SC[s_]["h"][mt * 256:(mt + 1) * 256, :].rearrange("(j p) d -> p j d", p=128),
                  [P.SC[s_]["res"]["h"]], [ht_r], ht_r)

        def norm(i):
            ht, ht_r = htb[i % 3]
            xn, xn_r = xnb[i % 2]
            ss, ss_r = ssb[i % 2]
            S.op("dve", lambda e: e.memset(ss[:], 0.0), [], [ss_r])
            for j in range(2):
                S.op("act", lambda e, j=j: e.activation(out=junk[:], in_=ht[:, j, :], func=AF.Square,
                                                        accum_out=ss[:, j:j + 1]), [ht_r, ss_r], [junk_r, ss_r])
            S.op("dve", lambda e: e.tensor_scalar(out=ss[:], in0=ss[:], scalar1=1.0 / D, scalar2=EPS,
                                                  op0=ALU.mult, op1=ALU.add), [ss_r], [ss_r])
            S.op("act", lambda e: e.activation(out=ss[:], in_=ss[:], func=AF.Sqrt), [ss_r], [ss_r])
            S.op("dve", lambda e: e.reciprocal(out=ss[:], in_=ss[:]), [ss_r], [ss_r])
            for j in range(2):
                S.op("act", lambda e, j=j: e.activation(out=xn[:, j, :], in_=ht[:, j, :], func=AF.Copy,
                                                        scale=ss[:, j:j + 1]), [ht_r, ss_r], [xn_r])

        def xpose(i):
            s_, mt = tiles[i]
            xn, xn_r = xnb[i % 2]
            uT_t, uT_r = uT[i % 2]
            for kc in range(8):
                pt, pr = psum()
                ptb = pt.bitcast(BF16)
                for j in range(2):
                    S.op("pe", lambda e, j=j, kc=kc, ptb=ptb: e.transpose(
                        out=ptb[:, j * 128:(j + 1) * 128], in_=xn[:, j, kc * 128:(kc + 1) * 128], identity=P.ident_b[:]),
                        [xn_r, P.ident_b_r], [pr], signal=(j == 1))
                S.op("dve", lambda e, kc=kc, ptb=ptb: e.tensor_scalar(
                    out=uT_t[:, kc, :], in0=ptb[:, 0:256], scalar1=P.modT[:, s_, 3, kc:kc + 1],
                    scalar2=P.modT[:, s_, 2, kc:kc + 1], op0=ALU.mult, op1=ALU.add), [pr, P.modT_r], [uT_r])

        def up(i):
            uT_t, uT_r = uT[i % 2]
            for f2 in range(16):
                pt, pr = psum()
                for u in range(2):
                    fc = 2 * f2 + u
                    for kc in range(8):
                        S.op("pe", lambda e, pt=pt, u=u, fc=fc, kc=kc: e.matmul(
                            pt[:, u * 256:(u + 1) * 256], lhsT=wup[:, kc, fc * 128:(fc + 1) * 128], rhs=uT_t[:, kc, :],
                            start=(kc == 0), stop=(kc == 7)), [wup_r[kc], uT_r], [pr], signal=(kc == 7 and u == 1))
                ta, ta_r = tm[ti[0] % 3]
                ti[0] += 1
                S.op("act", lambda e, pt=pt, ta=ta: e.activation(out=ta[:], in_=pt[:, :], func=AF.Relu), [pr], [ta_r])
                S.op("dve" if f2 % 2 else "pool", lambda e, ta=ta, f2=f2: e.tensor_tensor(
                    out=hid[:, 2 * f2:2 * f2 + 2, :], in0=ta[:].rearrange("p (u t) -> p u t", u=2),
                    in1=ta[:].rearrange("p (u t) -> p u t", u=2), op=ALU.mult), [ta_r], [hid_r])

        def down(i):
            s_, mt = tiles[i]
            ht, ht_r = htb[i % 3]
            gt2, gt2_r = gtb2[s_]
            for j in range(2):
                for half in range(2):
                    pt, pr = psum()
                    for fc in range(32):
                        S.op("pe", lambda e, pt=pt, fc=fc, j=j, half=half: e.matmul(
                            pt[:, :], lhsT=hid[:, fc, j * 128:(j + 1) * 128], rhs=wdn[:, fc, half * 512:(half + 1) * 512],
                            start=(fc == 0), stop=(fc == 31)), [hid_r, wdn_r[fc // 8]], [pr], signal=(fc == 31))
                    ta, ta_r = tm[ti[0] % 3]
                    ti[0] += 1
                    S.op("dve", lambda e, pt=pt, half=half, ta=ta: e.tensor_tensor(
                        out=ta[:], in0=pt[:, :], in1=gt2[:, half * 512:(half + 1) * 512], op=ALU.mult), [pr, gt2_r], [ta_r])
                    S.op("pool", lambda e, j=j, half=half, ta=ta: e.tensor_tensor(
                        out=ht[:, j, half * 512:(half + 1) * 512], in0=ta[:], in1=ht[:, j, half * 512:(half + 1) * 512],
                        op=ALU.add), [ta_r, ht_r], [ht_r])
            S.dma("sp", P.O[s_][mt * 256:(mt + 1) * 256, :].rearrange("(j p) d -> p j d", p=128), ht[:], [ht_r], [P.OUT_r], ht_r)

        load(0)
        if nt > 1:
            load(1)
        norm(0)
        xpose(0)
        for i in range(nt):
            if i + 1 < nt:
                norm(i + 1)
            if i + 2 < nt:
                load(i + 2)
            up(i)
            if i + 1 < nt:
                xpose(i + 1)
            down(i)
        S.stage_end()


def build_all(Ls):
    P, hc = build(Ls)
    P.OUT_r = Res("out", multi=True)
    stage_A(P)
    stage_attn(P)
    stage_hyena(P)
    stage_D1(P)
    stage_D2(P)
    return finish(P), hc


_CACHE = {}


def kernel(**inputs):
    Ls = [8192, 4096]
    if "nc" not in _CACHE:
        _CACHE["nc"], _CACHE["hc"] = build_all(Ls)
    nc, hc = _CACHE["nc"], _CACHE["hc"]
    f = lambda a: np.ascontiguousarray(np.asarray(a))
    shared = {}
    for nm in ("rel_bias",):
        shared[nm] = f(inputs[nm])
    for nm in ("ada_w", "w_in", "conv_w", "filt_w1", "filt_w2", "filt_w3", "filt_w_out", "w_hy_br", "w_at_br", "w_out",
               "w_up", "w_down"):
        shared[nm] = f(inputs[nm])[0]
    for nm in ("ada_b", "norm1_g", "conv_b", "filt_b1", "filt_b2", "filt_b3", "filt_freq", "hyena_d", "norm2_g"):
        shared[nm] = f(inputs[nm]).reshape(1, -1)
    shared["q_norm_g"] = f(inputs["q_norm_g"]).reshape(1, -1)
    shared["k_norm_g"] = f(inputs["k_norm_g"]).reshape(1, -1)
    shared.update(hc)
    xp, xs = f(inputs["x_prompt"]), f(inputs["x_sample"])
    cp, cs = f(inputs["c_prompt"]), f(inputs["c_sample"])
    in_maps = []
    for i in range(8):
        m = dict(shared)
        m["x0"] = xp[i]
        m["x1"] = xs[i]
        m["c"] = np.stack([cp[i], cs[i]], axis=0)
        in_maps.append(m)
    res = run_bass_kernel_spmd(nc, in_maps, core_ids=list(range(8)))
    yp = np.stack([np.asarray(r["y0"]) for r in res.results], axis=0).astype(np.float32)
    ys = np.stack([np.asarray(r["y1"]) for r in res.results], axis=0).astype(np.float32)
    return (yp, ys)
```

```python
import contextlib
import math
import numpy as np
import ml_dtypes
import concourse.bass as bass
import concourse.mybir as mybir
from concourse.bass_utils import run_bass_kernel_spmd

F32 = mybir.dt.float32
BF16 = mybir.dt.bfloat16
ALU = mybir.AluOpType
AF = mybir.ActivationFunctionType
AX = mybir.AxisListType

D = 1024
DH = 768
NH = 12
HD = 64
DFF = 4096
WIN = 6656
EPS = 1e-6
NEG = -30000.0
TWO_PI = 2.0 * math.pi
MAGIC = 12582912.0


class Res:
    __slots__ = ("name", "w", "r", "multi", "dsem")

    def __init__(self, name, multi=False):
        self.name = name
        self.w = {}
        self.r = {}
        self.multi = multi
        self.dsem = {}


class DSem:
    def __init__(self, sem, key, kind):
        self.sem = sem
        self.key = key
        self.kind = kind
        self.n = 0


class Sched:
    def __init__(self, nc, es, n_hw=44, n_sw=24, same_engine_sync=True):
        self.nc = nc
        self.E = {"pe": nc.tensor, "act": nc.scalar, "dve": nc.vector, "pool": nc.gpsimd, "sp": nc.sync}
        self.esem = {k: es.enter_context(nc.semaphore("e_" + k)) for k in ("pe", "act", "dve", "pool")}
        self.cnt = {k: 0 for k in self.esem}
        self.seen = {k: {} for k in self.E}
        self.same = same_engine_sync
        self.pool_ds = {
            "hw": [DSem(es.enter_context(nc.semaphore(f"dh{i}")), f"dh{i}", "hw") for i in range(n_hw)],
            "sw": [DSem(es.enter_context(nc.semaphore(f"ds{i}")), f"ds{i}", "sw") for i in range(n_sw)],
        }
        self.all_ds = self.pool_ds["hw"] + self.pool_ds["sw"]
        self.free_ds = {"hw": list(self.pool_ds["hw"]), "sw": list(self.pool_ds["sw"])}
        self.stage_res = []
        self.ninst = 0

    @staticmethod
    def _add(deps, d):
        for k, (s, v) in d.items():
            if k not in deps or deps[k][1] < v:
                deps[k] = (s, v)

    def _wait(self, eng, deps):
        seen = self.seen[eng]
        for k, (s, v) in deps.items():
            if k == "e_" + eng and (eng == "pe" or not self.same):
                continue
            if seen.get(k, 0) >= v:
                continue
            self.E[eng].wait_ge(s, v)
            seen[k] = v
            self.ninst += 1

    def _deps(self, reads, writes):
        deps = {}
        for r in reads:
            self._add(deps, r.w)
        for w in writes:
            self._add(deps, w.r)
            if not w.multi:
                self._add(deps, w.w)
        return deps

    @staticmethod
    def _record(key, sem, ev, reads, writes):
        for r in reads:
            if r.r.get(key, (None, 0))[1] < ev:
                r.r[key] = (sem, ev)
        for w in writes:
            if w.multi:
                if w.w.get(key, (None, 0))[1] < ev:
                    w.w[key] = (sem, ev)
            else:
                w.w = {key: (sem, ev)}
                w.r = {}

    def op(self, eng, fn, reads=(), writes=(), signal=True):
        self._wait(eng, self._deps(reads, writes))
        inst = fn(self.E[eng])
        self.ninst += 1
        sem = self.esem[eng]
        if signal:
            self.cnt[eng] += 1
            inst.then_inc(sem, 1)
            ev = self.cnt[eng]
        else:
            ev = self.cnt[eng] + 1
        self._record("e_" + eng, sem, ev, reads, writes)
        return inst

    def dma(self, q, out, in_, reads, writes, sb, **kw):
        kind = "sw" if q == "pool" else "hw"
        ds = sb.dsem.get(kind)
        if ds is None:
            assert self.free_ds[kind], "out of dma semaphores " + kind
            ds = self.free_ds[kind].pop()
            sb.dsem[kind] = ds
            self.stage_res.append(sb)
        deps = self._deps(reads, writes)
        if ds.n:
            self._add(deps, {ds.key: (ds.sem, 16 * ds.n)})
        self._wait(q, deps)
        inst = self.E[q].dma_start(out=out, in_=in_, **kw)
        inst.then_inc(ds.sem, 16)
        self.ninst += 1
        ds.n += 1
        self._record(ds.key, ds.sem, 16 * ds.n, reads, writes)
        return inst

    def barrier(self):
        deps = {}
        for k, s in self.esem.items():
            if self.cnt[k]:
                deps["e_" + k] = (s, self.cnt[k])
        for ds in self.all_ds:
            if ds.n:
                deps[ds.key] = (ds.sem, 16 * ds.n)
        for e in self.E:
            d = {k: v for k, v in deps.items() if k != "e_" + e}
            self._wait(e, d)

    def stage_end(self):
        self.barrier()
        for r in self.stage_res:
            for kind, ds in r.dsem.items():
                self.free_ds[kind].append(ds)
            r.dsem = {}
        self.stage_res = []


def t5_bucket_np(rel):
    half = 16
    max_exact = 8
    n = np.abs(rel)
    ret = np.where(rel > 0, half, 0)
    nf = np.maximum(n, 1).astype(np.float32)
    large = max_exact + (np.log(nf / np.float32(max_exact)) / np.float32(math.log(1024 / max_exact))
                         * np.float32(half - max_exact)).astype(np.int32)
    large = np.minimum(large, half - 1)
    return ret + np.where(n < max_exact, n, large)


def host_consts(Ls):
    c = {}
    c["ident_b"] = np.eye(128, dtype=np.float32).astype(ml_dtypes.bfloat16)
    c["ident_f"] = np.eye(128, dtype=np.float32)
    c["antiid"] = np.eye(128, dtype=np.float32)[::-1].copy()
    bd = np.zeros((128, 128), np.float32)
    bd[:64, :64] = 1.0
    bd[64:, 64:] = 1.0
    c["bdones"] = bd.astype(ml_dtypes.bfloat16)
    oh = np.zeros((3, 33, 512), np.float32)
    for g, dil in enumerate((1, 4, 16)):
        for ab in range(2):
            for m in range(255):
                delta = 127 - m
                if ab == 0:
                    valid = delta >= 0
                    rel = delta - 64
                else:
                    valid = delta <= 0
                    rel = delta + 64
                if valid and abs(rel) <= 64:
                    b = int(t5_bucket_np(np.array(rel * dil)))
                    oh[g, b, ab * 256 + m] = 1.0
                else:
                    oh[g, 32, ab * 256 + m] = NEG
            oh[g, 32, ab * 256 + 255] = NEG
    c["bias_oh"] = oh
    for L in sorted(set(Ls)):
        N = 2 * L
        N1 = N // 128
        f32 = np.float32
        t = np.linspace(0.0, 1.0, L, dtype=f32)[:, None]
        bands = np.linspace(1e-4, 15, 16, dtype=f32)[None, :]
        w = (f32(2.0 * math.pi) * np.arange(L, dtype=f32)[:, None] / f32(L)).astype(f32)
        feats = np.concatenate([t, np.cos(bands * w), -np.sin(bands * w)], axis=-1).astype(f32)
        ft = np.zeros((2, 33, L), f32)
        ft[0] = feats.T
        ft[1, :, 1:] = feats[1:][::-1].T
        c[f"feats{L}"] = ft
        tv = np.zeros((2, L), f32)
        tv[0] = t[:, 0]
        tv[1, 1:] = t[1:, 0][::-1]
        c[f"tvec{L}"] = tv
        n1 = np.arange(N1)[:, None]
        k1 = np.arange(N1)[None, :]
        th = 2 * np.pi * n1 * k1 / N1
        c[f"f1tab{L}"] = np.concatenate([np.cos(th), -np.sin(th)], axis=1).astype(ml_dtypes.bfloat16)
        n2 = np.arange(128)[:, None, None]
        kk1 = np.arange(N1)[None, :, None]
        kk2 = np.arange(128)[None, None, :]
        th = 2 * np.pi * ((n2 * (kk1 + N1 * kk2)) % N) / N
        gr = np.cos(th)
        gi = -np.sin(th)
        c[f"gtab{L}"] = np.stack([gr, gi, -gi], axis=2).astype(ml_dtypes.bfloat16)
        k2 = np.arange(128)[:, None]
        t2 = np.arange(128)[None, :]
        th = 2 * np.pi * k2 * t2 / 128
        c2 = np.cos(th)
        s2 = np.sin(th)
        c[f"i1tab{L}"] = np.stack([np.concatenate([c2, s2], 1), np.concatenate([-s2, c2], 1)], axis=1).astype(
            ml_dtypes.bfloat16)
        kk = np.arange(N1)[:, None, None]
        tt2 = np.arange(128)[None, :, None]
        tt1 = np.arange(N1 // 2)[None, None, :]
        th = 2 * np.pi * ((kk * (tt2 + 128 * tt1)) % N) / N
        wk = np.zeros((N1, 1, 1))
        wk[0] = 1.0
        wk[N1 // 2] = 1.0
        wk[1:N1 // 2] = 2.0
        c[f"i2tab{L}"] = np.stack([wk * np.cos(th) / N, -wk * np.sin(th) / N], axis=2).astype(ml_dtypes.bfloat16)
    deltas = np.abs(np.linspace(math.log(1e-2) / 1.5, math.log(1e-2) / 0.3, DH, dtype=np.float32))
    c["ndelta"] = (-deltas).astype(np.float32).reshape(6, 128).T.copy()
    return c


class Prog:
    pass


def build(Ls, stages=None, dbg=()):
    nc = bass.Bass("TRN2", target_bir_lowering=False)
    es = contextlib.ExitStack()
    S = Sched(nc, es)
    NS = len(Ls)
    P = Prog()
    P.nc, P.S, P.es = nc, S, es

    def din(name, shape, dt=F32):
        return nc.dram_tensor(name, list(shape), dt, kind="ExternalInput")

    def dscr(name, shape, dt):
        kind = "ExternalOutput" if name in dbg else "Internal"
        return nc.dram_tensor(name, list(shape), dt, kind=kind)

    I = {}
    for s, L in enumerate(Ls):
        I[f"x{s}"] = din(f"x{s}", [L, D])
    I["c"] = din("c", [NS, D])
    for nm, shp in (("rel_bias", [32, NH]), ("ada_w", [D, 6 * D]), ("ada_b", [1, 6 * D]), ("norm1_g", [1, D]),
                    ("w_in", [D, WIN]), ("conv_w", [3, 3 * DH]), ("conv_b", [1, 3 * DH]),
                    ("filt_w1", [33, 64]), ("filt_b1", [1, 64]), ("filt_w2", [64, 64]), ("filt_b2", [1, 64]),
                    ("filt_w3", [64, 64]), ("filt_b3", [1, 64]), ("filt_freq", [1, 64]),
                    ("filt_w_out", [64, 2 * DH]), ("hyena_d", [1, DH]), ("q_norm_g", [1, NH * HD]),
                    ("k_norm_g", [1, NH * HD]), ("w_hy_br", [DH, D]), ("w_at_br", [256, D]), ("w_out", [D, D]),
                    ("norm2_g", [1, D]), ("w_up", [D, DFF]), ("w_down", [DFF, D])):
        I[nm] = din(nm, shp)
    hc = host_consts(Ls)
    for nm, arr in hc.items():
        I[nm] = din(nm, arr.shape, BF16 if arr.dtype == ml_dtypes.bfloat16 else F32)
    O = [nc.dram_tensor(f"y{s}", [L, D], F32, kind="ExternalOutput") for s, L in enumerate(Ls)]

    SC = []
    for s, L in enumerate(Ls):
        N1 = 2 * L // 128
        d = {}
        d["zhy"] = dscr(f"zhy{s}", [3 * DH, L + 2], BF16)
        d["qT"] = dscr(f"qT{s}", [DH, L], BF16)
        d["kT"] = dscr(f"kT{s}", [DH, L], BF16)
        d["v"] = dscr(f"v{s}", [L, DH], BF16)
        d["gT"] = dscr(f"gT{s}", [2 * D, L], BF16)
        d["aT"] = dscr(f"aT{s}", [DH, L], BF16)
        d["k2"] = dscr(f"k2{s}", [DH, 2 * L], BF16)
        d["kf"] = dscr(f"kf{s}", [6, N1 // 2, 128, 512], F32)
        d["yhy"] = dscr(f"yhy{s}", [DH, L], BF16)
        d["od"] = dscr(f"od{s}", [3, L, 4 * 65], F32)
        d["h"] = dscr(f"h{s}", [L, D], F32)
        d["res"] = {k: Res(f"{k}{s}", multi=True) for k in list(d.keys())}
        SC.append(d)
    gvd = dscr("gvd", [NH, 512], F32)
    gvd_r = Res("gvd", multi=True)
    gtbd = dscr("gtbd", [128, NS * 2 * D], F32)
    gtbd_r = Res("gtbd", multi=True)
    IN = Res("inputs", multi=True)

    PS = []
    for b in range(8):
        t = es.enter_context(nc.psum_tensor(f"ps{b}", [128, 512], F32))
        PS.append((t, Res(f"ps{b}")))
    P.psi = 0
    P.nsb = 0

    def psum():
        t, r = PS[P.psi % 8]
        P.psi += 1
        return t, r

    def sb(name, shape, dt, stack):
        P.nsb += 1
        t = stack.enter_context(nc.sbuf_tensor(f"s{P.nsb}_" + name, list(shape), dt))
        return t, Res(name)

    gs = es
    ident_b, ident_b_r = sb("ident_b", [128, 128], BF16, gs)
    ident_f, ident_f_r = sb("ident_f", [128, 128], F32, gs)
    modT, modT_r = sb("modT", [128, NS, 4, 8], F32, gs)
    S.dma("sp", ident_b[:], I["ident_b"][:, :], [IN], [ident_b_r], ident_b_r)
    S.dma("sp", ident_f[:], I["ident_f"][:, :], [IN], [ident_f_r], ident_f_r)

    want = (lambda n: stages is None or n in stages)

    def pe_flush():
        pt, pr = psum()
        S.op("pe", lambda e: e.transpose(out=pt.bitcast(BF16)[:, 0:128], in_=ident_b[:], identity=ident_b[:]),
             [ident_b_r], [pr])
    P.pe_flush = pe_flush

    if want("mod"):
        with contextlib.ExitStack() as st:
            cT, cT_r = sb("cT", [128, 8, NS], F32, st)
            gtb, gtb_r = sb("gtb", [128, NS, 2, D], F32, st)
            crep, crep_r = sb("crep", [128, 8, NS, 128], F32, st)
            n1g, n1g_r = sb("n1g", [128, 8], F32, st)
            n2g, n2g_r = sb("n2g", [128, 8], F32, st)
            abT, abT_r = sb("abT", [128, 48], F32, st)
            abb, abb_r = sb("abb", [128, 2, D], F32, st)
            aw = [sb(f"aw{i}", [128, 8, 512], F32, st) for i in range(2)]
            with nc.allow_non_contiguous_dma(reason="tiny transposed loads"):
                for s in range(NS):
                    S.dma("sp", cT[:, :, s], I["c"][s, :].rearrange("(k p) -> p k", p=128), [IN], [cT_r], cT_r)
                S.dma("sp", n1g[:], I["norm1_g"][0, :].rearrange("(k p) -> p k", p=128), [IN], [n1g_r], n1g_r)
                S.dma("sp", n2g[:], I["norm2_g"][0, :].rearrange("(k p) -> p k", p=128), [IN], [n2g_r], n2g_r)
                S.dma("sp", abT[:], I["ada_b"][0, :].rearrange("(k p) -> p k", p=128), [IN], [abT_r], abT_r)
            for j, col in enumerate((2 * D, 5 * D)):
                S.dma("sp", abb[:, j, :], I["ada_b"][0:1, col:col + D].partition_broadcast(128), [IN], [abb_r], abb_r)
            S.op("act", lambda e: e.activation(out=cT[:], in_=cT[:], func=AF.Silu), [cT_r], [cT_r])
            S.op("dve", lambda e: e.memset(crep[:], 1.0), [], [crep_r])
            for kc in range(8):
                for s in range(NS):
                    S.op("dve", lambda e, kc=kc, s=s: e.tensor_scalar_mul(out=crep[:, kc, s, :], in0=crep[:, kc, s, :],
                                                                             scalar1=cT[:, kc, s:s + 1]),
                         [cT_r, crep_r], [crep_r])
            for grp in range(12):
                awt, awr = aw[grp % 2]
                S.dma("sp", awt[:], I["ada_w"][:, grp * 512:(grp + 1) * 512].rearrange("(k p) n -> p k n", p=128),
                      [IN], [awr], awr)
                sec = grp // 2
                if sec in (2, 5):
                    j = 0 if sec == 2 else 1
                    for s in range(NS):
                        pt, pr = psum()
                        for kc in range(8):
                            S.op("pe", lambda e, kc=kc, s=s, pt=pt, awt=awt: e.matmul(
                                pt[:, :], lhsT=crep[:, kc, s, :], rhs=awt[:, kc, :], start=(kc == 0), stop=(kc == 7)),
                                [crep_r, awr], [pr], signal=(kc == 7))
                        c0 = (grp % 2) * 512
                        S.op("dve", lambda e, pt=pt, s=s, j=j, c0=c0: e.tensor_tensor(
                            out=gtb[:, s, j, c0:c0 + 512], in0=pt[:, :], in1=abb[:, j, c0:c0 + 512], op=ALU.add),
                            [pr, abb_r], [gtb_r])
                else:
                    slot = {0: 0, 1: 1, 3: 2, 4: 3}[sec]
                    pt, pr = psum()
                    for sub in range(4):
                        for kc in range(8):
                            S.op("pe", lambda e, kc=kc, sub=sub, pt=pt, awt=awt: e.matmul(
                                pt[:, sub * NS:(sub + 1) * NS], lhsT=awt[:, kc, sub * 128:(sub + 1) * 128],
                                rhs=cT[:, kc, :], start=(kc == 0), stop=(kc == 7)),
                                [cT_r, awr], [pr], signal=(kc == 7 and sub == 3))
                    for sub in range(4):
                        ch = (grp % 2) * 4 + sub
                        acol = sec * 8 + ch
                        for s in range(NS):
                            S.op("dve", lambda e, pt=pt, sub=sub, s=s, slot=slot, ch=ch, acol=acol: e.tensor_tensor(
                                out=modT[:, s, slot, ch:ch + 1], in0=pt[:, sub * NS + s:sub * NS + s + 1],
                                in1=abT[:, acol:acol + 1], op=ALU.add), [pr, abT_r], [modT_r])
            for s in range(NS):
                for slot, gt_, gr_ in ((1, n1g, n1g_r), (3, n2g, n2g_r)):
                    S.op("dve", lambda e, s=s, slot=slot, gt_=gt_: e.scalar_tensor_tensor(
                        out=modT[:, s, slot, :], in0=modT[:, s, slot, :], scalar=1.0, in1=gt_[:],
                        op0=ALU.add, op1=ALU.mult), [modT_r, gr_], [modT_r])
            S.dma("sp", gtbd[:, :], gtb[:].rearrange("p a b c -> p (a b c)"), [gtb_r], [gtbd_r], gtb_r)
            pe_flush()
            S.stage_end()

    P.I, P.O, P.SC, P.IN = I, O, SC, IN
    P.sb, P.psum, P.want = sb, psum, want
    P.ident_b, P.ident_b_r, P.ident_f, P.ident_f_r = ident_b, ident_b_r, ident_f, ident_f_r
    P.modT, P.modT_r, P.gtbd, P.gtbd_r = modT, modT_r, gtbd, gtbd_r
    P.gvd, P.gvd_r = gvd, gvd_r
    P.Ls = Ls
    return P, hc


def finish(P):
    P.S.barrier()
    P.es.close()
    return P.nc


def stage_A(P):
    nc, S, I, IN, sb, psum = P.nc, P.S, P.I, P.IN, P.sb, P.psum
    AQ = "act"
    with contextlib.ExitStack() as st:
        w_sb, _ = sb("w_in_sb", [128, 8, WIN], BF16, st)
        wr = [Res(f"w_in{kc}") for kc in range(8)]
        for kc in range(8):
            S.dma("pool", w_sb[:, kc, :], I["w_in"][kc * 128:(kc + 1) * 128, :], [IN], [wr[kc]], wr[kc])
        bd, bd_r = sb("bd", [128, 128], BF16, st)
        S.dma("sp", bd[:], I["bdones"][:, :], [IN], [bd_r], bd_r)
        gq, gq_r = sb("gq", [128, 12], F32, st)
        with nc.allow_non_contiguous_dma(reason="tiny transposed loads"):
            S.dma("sp", gq[:, 0:6], I["q_norm_g"][0, :].rearrange("(k p) -> p k", p=128), [IN], [gq_r], gq_r)
            S.dma("sp", gq[:, 6:12], I["k_norm_g"][0, :].rearrange("(k p) -> p k", p=128), [IN], [gq_r], gq_r)
        S.op("dve", lambda e: e.tensor_scalar_mul(out=gq[:, 0:6], in0=gq[:, 0:6], scalar1=HD ** -0.5), [gq_r], [gq_r])
        S.op("act", lambda e: e.activation(out=gq[:], in_=gq[:], func=AF.Ln), [gq_r], [gq_r])
        epsb, epsb_r = sb("epsb", [128, 1], F32, st)
        S.op("dve", lambda e: e.memset(epsb[:], EPS), [], [epsb_r])
        zt, zt_r = sb("zt", [128, 18], BF16, st)
        S.op("dve", lambda e: e.memset(zt[:], 0.0), [], [zt_r])
        xt = [sb(f"xt{i}", [128, 4, D], F32, st) for i in range(2)]
        xnb = [sb(f"xn{i}", [128, 4, D], BF16, st) for i in range(2)]
        uT = [sb(f"uT{i}", [128, 8, 512], BF16, st) for i in range(2)]
        ssb = [sb(f"ss{i}", [128, 4], F32, st) for i in range(2)]
        junk, junk_r = sb("junk", [128, D], BF16, st)
        evb = [sb(f"evb{i}", [128, 512], BF16, st) for i in range(8)]
        qf = [sb(f"qf{i}", [128, 512], F32, st) for i in range(4)]
        sq = [sb(f"sq{i}", [128, 512], BF16, st) for i in range(4)]
        rs = [sb(f"rs{i}", [128, 512], F32, st) for i in range(4)]
        tiles = [(s_, mt) for s_, L in enumerate(P.Ls) for mt in range(L // 512)]
        nt = len(tiles)
        for s_, L in enumerate(P.Ls):
            sc = P.SC[s_]
            with nc.allow_non_contiguous_dma(reason="zero pads"):
                for col in (0, L + 1):
                    S.dma("sp", sc["zhy"][:, col].rearrange("(k p) -> p k", p=128), zt[:], [zt_r], [sc["res"]["zhy"]], zt_r)

        def load(i):
            s_, mt = tiles[i]
            xt_t, xt_r = xt[i % 2]
            S.dma("sp", xt_t[:], I[f"x{s_}"][mt * 512:(mt + 1) * 512, :].rearrange("(j p) d -> p j d", p=128), [IN], [xt_r], xt_r)

        def norm(i):
            xt_t, xt_r = xt[i % 2]
            xn, xn_r = xnb[i % 2]
            ss, ss_r = ssb[i % 2]
            S.op("dve", lambda e: e.memset(ss[:], 0.0), [], [ss_r])
            for j in range(4):
                S.op("act", lambda e, j=j: e.activation(out=junk[:], in_=xt_t[:, j, :], func=AF.Square,
                                                        accum_out=ss[:, j:j + 1]), [xt_r, ss_r], [junk_r, ss_r])
            S.op("dve", lambda e: e.tensor_scalar(out=ss[:], in0=ss[:], scalar1=1.0 / D, scalar2=EPS,
                                                  op0=ALU.mult, op1=ALU.add), [ss_r], [ss_r])
            S.op("act", lambda e: e.activation(out=ss[:], in_=ss[:], func=AF.Sqrt), [ss_r], [ss_r])
            S.op("dve", lambda e: e.reciprocal(out=ss[:], in_=ss[:]), [ss_r], [ss_r])
            for j in range(4):
                S.op("act", lambda e, j=j: e.activation(out=xn[:, j, :], in_=xt_t[:, j, :], func=AF.Copy,
                                                        scale=ss[:, j:j + 1]), [xt_r, ss_r], [xn_r])

        def xpose(i):
            s_, mt = tiles[i]
            xn, xn_r = xnb[i % 2]
            uT_t, uT_r = uT[i % 2]
            for kc in range(8):
                pt, pr = psum()
                ptb = pt.bitcast(BF16)
                for j in range(4):
                    S.op("pe", lambda e, j=j, kc=kc, ptb=ptb: e.transpose(
                        out=ptb[:, j * 128:(j + 1) * 128], in_=xn[:, j, kc * 128:(kc + 1) * 128],
                        identity=P.ident_b[:]), [xn_r, P.ident_b_r], [pr], signal=(j == 3))
                S.op("dve", lambda e, kc=kc, ptb=ptb: e.tensor_scalar(
                    out=uT_t[:, kc, :], in0=ptb[:, 0:512], scalar1=P.modT[:, s_, 1, kc:kc + 1],
                    scalar2=P.modT[:, s_, 0, kc:kc + 1], op0=ALU.mult, op1=ALU.add), [pr, P.modT_r], [uT_r])

        ei = [0]

        def mm(i):
            s_, mt = tiles[i]
            sc = P.SC[s_]
            R = sc["res"]
            t0 = mt * 512
            uT_t, uT_r = uT[i % 2]
            pend = []
            for m in range(52):
                if 30 <= m < 36:
                    continue
                pt, pr = psum()
                for kc in range(8):
                    S.op("pe", lambda e, kc=kc, m=m, pt=pt: e.matmul(
                        pt[:, :], lhsT=w_sb[:, kc, m * 128:(m + 1) * 128], rhs=uT_t[:, kc, :],
                        start=(kc == 0), stop=(kc == 7)), [wr[kc], uT_r], [pr], signal=(kc == 7))
                et, er = evb[ei[0] % 8]
                ei[0] += 1
                if m < 18:
                    if m % 2 == 0:
                        S.op("act", lambda e, pt=pt, et=et: e.activation(out=et[:], in_=pt[:, :], func=AF.Copy), [pr], [er])
                        S.dma(AQ, sc["zhy"][m * 128:(m + 1) * 128, 1 + t0:1 + t0 + 512], et[:], [er], [R["zhy"]], er)
                    else:
                        S.op("dve", lambda e, pt=pt, et=et: e.tensor_copy(out=et[:], in_=pt[:, :]), [pr], [er])
                        S.dma("sp", sc["zhy"][m * 128:(m + 1) * 128, 1 + t0:1 + t0 + 512], et[:], [er], [R["zhy"]], er)
                elif m < 30:
                    idx = m - 18
                    qf_t, qf_r = qf[idx % 4]
                    sq_t, sq_r = sq[idx % 4]
                    rs_t, rs_r = rs[idx % 4]
                    S.op("dve", lambda e, pt=pt, qf_t=qf_t: e.tensor_copy(out=qf_t[:], in_=pt[:, :]), [pr], [qf_r])
                    S.op("act", lambda e, qf_t=qf_t, sq_t=sq_t: e.activation(out=sq_t[:], in_=qf_t[:], func=AF.Square), [qf_r], [sq_r])
                    pt2, pr2 = psum()
                    S.op("pe", lambda e, pt2=pt2, sq_t=sq_t: e.matmul(pt2[:, :], lhsT=bd[:], rhs=sq_t[:], start=True, stop=True),
                         [bd_r, sq_r], [pr2])

                    def st1(pt2=pt2, pr2=pr2, rs_t=rs_t, rs_r=rs_r, idx=idx):
                        S.op("act", lambda e: e.activation(out=rs_t[:], in_=pt2[:, :], func=AF.Ln, scale=1.0 / HD, bias=epsb[:, 0:1]),
                             [pr2, epsb_r], [rs_r])
                        S.op("act", lambda e: e.activation(out=rs_t[:], in_=rs_t[:], func=AF.Exp, scale=-0.5, bias=gq[:, idx:idx + 1]),
                             [rs_r, gq_r], [rs_r])

                    def st2(rs_t=rs_t, rs_r=rs_r, qf_t=qf_t, qf_r=qf_r, et=et, er=er, idx=idx, t0=t0):
                        S.op("pool", lambda e: e.tensor_tensor(out=et[:], in0=qf_t[:], in1=rs_t[:], op=ALU.mult), [rs_r, qf_r], [er])
                        dst = sc["qT"] if idx < 6 else sc["kT"]
                        dr = R["qT"] if idx < 6 else R["kT"]
                        S.dma("sp", dst[(idx % 6) * 128:(idx % 6 + 1) * 128, t0:t0 + 512], et[:], [er], [dr], er)
                    pend.append([st1, st2])
                    if len(pend) > 1:
                        pend[-2][0]()
                    if len(pend) > 2:
                        pend[-3][1]()
                    if idx == 11:
                        pend[-1][0]()
                        pend[-2][1]()
                        pend[-1][1]()
                        del pend[:]
                else:
                    S.op("act", lambda e, pt=pt, et=et: e.activation(out=et[:], in_=pt[:, :], func=AF.Sigmoid), [pr], [er])
                    S.dma(AQ, sc["gT"][(m - 36) * 128:(m - 35) * 128, t0:t0 + 512], et[:], [er], [R["gT"]], er)
            for jb in range(4):
                for c0, cw_ in ((3840, 512), (4352, 256)):
                    pt, pr = psum()
                    for kc in range(8):
                        S.op("pe", lambda e, kc=kc, jb=jb, c0=c0, cw_=cw_, pt=pt: e.matmul(
                            pt[:, 0:cw_], lhsT=uT_t[:, kc, jb * 128:(jb + 1) * 128], rhs=w_sb[:, kc, c0:c0 + cw_],
                            start=(kc == 0), stop=(kc == 7)), [wr[kc], uT_r], [pr], signal=(kc == 7))
                    et, er = evb[ei[0] % 8]
                    ei[0] += 1
                    S.op("dve", lambda e, pt=pt, et=et, cw_=cw_: e.tensor_copy(out=et[:, 0:cw_], in_=pt[:, 0:cw_]), [pr], [er])
                    S.dma("sp", sc["v"][t0 + jb * 128:t0 + (jb + 1) * 128, c0 - 3840:c0 - 3840 + cw_], et[:, 0:cw_],
                          [er], [R["v"]], er)

        load(0)
        if nt > 1:
            load(1)
        norm(0)
        xpose(0)
        for i in range(nt):
            if i + 1 < nt:
                norm(i + 1)
            if i + 2 < nt:
                load(i + 2)
            mm(i)
            if i + 1 < nt:
                xpose(i + 1)
        S.stage_end()


def stage_attn(P):
    nc, S, I, IN, sb, psum = P.nc, P.S, P.I, P.IN, P.sb, P.psum
    PAD = 1024
    with contextlib.ExitStack() as st:
        biasmat, bm_r = sb("biasmat", [128, 24, 128], F32, st)
        with contextlib.ExitStack() as st2:
            rbx, rbx_r = sb("rbx", [33, NH], F32, st2)
            anti, anti_r = sb("anti", [128, 128], F32, st2)
            oh = [sb(f"oh{g}", [33, 512], F32, st2) for g in range(3)]
            gv = [sb(f"gv{g}", [4, 512], F32, st2) for g in range(3)]
            hk = [sb(f"hk{i}", [128, 128], F32, st2) for i in range(4)]
            S.op("dve", lambda e: e.memset(rbx[:], 1.0), [], [rbx_r])
            S.dma("sp", rbx[0:32, :], I["rel_bias"][:, :], [IN], [rbx_r], rbx_r)
            S.dma("sp", anti[:], I["antiid"][:, :], [IN], [anti_r], anti_r)
            for g in range(3):
                S.dma("sp", oh[g][0][:], I["bias_oh"][g, :, :], [IN], [oh[g][1]], oh[g][1])
                pt, pr = psum()
                S.op("pe", lambda e, g=g, pt=pt: e.matmul(pt[0:4, :], lhsT=rbx[:, 4 * g:4 * g + 4], rhs=oh[g][0][:],
                                                          start=True, stop=True), [rbx_r, oh[g][1]], [pr])
                S.op("dve", lambda e, g=g, pt=pt: e.tensor_copy(out=gv[g][0][:], in_=pt[0:4, :]), [pr], [gv[g][1]])
                S.dma("sp", P.gvd[4 * g:4 * g + 4, :], gv[g][0][:], [gv[g][1]], [P.gvd_r], gv[g][1])
            for h in range(NH):
                for ab in range(2):
                    i = h * 2 + ab
                    ht, hr = hk[i % 4]
                    S.dma("sp", ht[:], bass.AP(P.gvd, h * 512 + ab * 256, [[1, 128], [1, 128]]), [P.gvd_r], [hr], hr)
                    pt, pr = psum()
                    S.op("pe", lambda e, pt=pt, ht=ht: e.matmul(pt[:, 0:128], lhsT=anti[:], rhs=ht[:], start=True, stop=True),
                         [anti_r, hr], [pr])
                    S.op("dve", lambda e, pt=pt, i=i: e.tensor_copy(out=biasmat[:, i, :], in_=pt[:, 0:128]), [pr], [bm_r])
            P.pe_flush()
            S.barrier()
        Lmax = max(P.Ls)
        OQ = "act"
        qz = [[sb(f"qz{i}{h}", [128, Lmax], BF16, st) for h in range(2)] for i in range(2)]
        for i in range(2):
            S.op("pool", lambda e, i=i: e.memset(qz[i][0][0][64:128, :], 0.0), [], [qz[i][0][1]])
            S.op("pool", lambda e, i=i: e.memset(qz[i][1][0][0:64, :], 0.0), [], [qz[i][1][1]])
        ks = [sb(f"ks{i}", [128, Lmax + 2 * PAD], BF16, st) for i in range(2)]
        vfull = [sb(f"vf{i}", [128, 4, 128], BF16, st) for i in range(3)]
        vfirst, vfirst_r = sb("vfirst", [128, 4, 128], BF16, st)
        vlast, vlast_r = sb("vlast", [128, 4, 128], BF16, st)
        scs = [sb(f"scs{i}", [128, 8, 128], F32, st) for i in range(2)]
        pT = [sb(f"pT{i}", [128, 8, 128], BF16, st) for i in range(2)]
        osb = [sb(f"osb{i}", [128, 260], F32, st) for i in range(3)]
        for t_, r_ in vfull:
            S.op("dve", lambda e, t_=t_: e.memset(t_[:], 1.0), [], [r_])
        S.op("dve", lambda e: e.memset(vfirst[:], 0.0), [], [vfirst_r])
        S.op("dve", lambda e: e.memset(vlast[:], 0.0), [], [vlast_r])
        S.op("dve", lambda e: e.memset(vfirst[64:128, :, 64:65], 1.0), [], [vfirst_r])
        S.op("dve", lambda e: e.memset(vlast[0:64, :, 64:65], 1.0), [], [vlast_r])
        for t_, r_ in ks:
            S.op("pool", lambda e, t_=t_: e.memset(t_[:], 0.0), [], [r_])
        it = 0
        for s, L in enumerate(P.Ls):
            sc = P.SC[s]
            R = sc["res"]
            for g, d in enumerate((1, 4, 16)):
                Ssub = L // d
                nblk = Ssub // 128
                for pp in range(2):
                    r0 = (4 * g + 2 * pp) * 64
                    S.dma("sp", qz[pp][0][0][0:64, 0:L], sc["qT"][r0:r0 + 64, :], [R["qT"]], [qz[pp][0][1]], qz[pp][0][1])
                    S.dma("sp", qz[pp][1][0][64:128, 0:L], sc["qT"][r0 + 64:r0 + 128, :], [R["qT"]], [qz[pp][1][1]], qz[pp][1][1])
                    S.dma("sp", ks[pp][0][:, PAD:PAD + L], sc["kT"][r0:r0 + 128, :], [R["kT"]], [ks[pp][1]], ks[pp][1])
                vview = sc["v"].rearrange("(s d) c -> d s c", d=d)
                oview = sc["od"][g].rearrange("(s d) c -> d s c", d=d)
                for r in range(d):
                    def vload(m):
                        if m == 0:
                            t_, r_ = vfirst, vfirst_r
                            S.dma("sp", t_[64:128, :, 0:64],
                                  vview[r, 0:64, g * 256:(g + 1) * 256].rearrange("s (h e) -> s h e", e=64),
                                  [R["v"]], [r_], r_)
                        elif m == nblk:
                            t_, r_ = vlast, vlast_r
                            S.dma("sp", t_[0:64, :, 0:64],
                                  vview[r, Ssub - 64:Ssub, g * 256:(g + 1) * 256].rearrange("s (h e) -> s h e", e=64),
                                  [R["v"]], [r_], r_)
                        else:
                            t_, r_ = vfull[m % 3]
                            S.dma("sp", t_[:, :, 0:64],
                                  vview[r, 128 * m - 64:128 * m + 64, g * 256:(g + 1) * 256].rearrange("s (h e) -> s h e", e=64),
                                  [R["v"]], [r_], r_)
                        return t_, r_
                    vt = {0: vload(0)}

                    def phase1(j):
                        nonlocal it
                        vt[j + 1] = vload(j + 1)
                        sc_t, sc_r = scs[it % 2]
                        p_t, p_r = pT[it % 2]
                        o_t, o_r = osb[it % 3]
                        it += 1
                        banks = [psum(), psum()]
                        for hh in range(4):
                            pp = hh // 2
                            qa = qz[pp][hh % 2][0][:, 128 * j * d + r:128 * j * d + r + 127 * d + 1:d]
                            for ab in range(2):
                                m = j + ab
                                k0 = PAD + (128 * m - 64) * d + r
                                ka = ks[pp][0][:, k0:k0 + 127 * d + 1:d]
                                idx = hh * 2 + ab
                                pt, pr = banks[idx // 4]
                                S.op("pe", lambda e, pt=pt, idx=idx, ka=ka, qa=qa: e.matmul(
                                    pt[:, (idx % 4) * 128:(idx % 4 + 1) * 128], lhsT=ka, rhs=qa, start=True, stop=True),
                                    [qz[pp][hh % 2][1], ks[pp][1]], [pr], signal=(idx % 4 == 3))
                        for b_ in range(2):
                            pt, pr = banks[b_]
                            S.op("dve", lambda e, pt=pt, b_=b_, sc_t=sc_t: e.tensor_tensor(
                                out=sc_t[:, 4 * b_:4 * b_ + 4, :], in0=pt[:, :].rearrange("p (a q) -> p a q", q=128),
                                in1=biasmat[:, 8 * g + 4 * b_:8 * g + 4 * b_ + 4, :], op=ALU.add), [pr, bm_r], [sc_r])
                        S.op("act", lambda e, sc_t=sc_t, p_t=p_t: e.activation(out=p_t[:], in_=sc_t[:], func=AF.Exp), [sc_r], [p_r])
                        return (j, p_t, p_r, o_t, o_r, vt[j], vt[j + 1], it)

                    def phase2(st_):
                        j, p_t, p_r, o_t, o_r, va, vb, itn = st_
                        po, por = psum()
                        for hh in range(4):
                            for ab, (vt_t, vt_r) in enumerate((va, vb)):
                                S.op("pe", lambda e, hh=hh, ab=ab, vt_t=vt_t, p_t=p_t, po=po: e.matmul(
                                    po[:, hh * 128:hh * 128 + 65], lhsT=p_t[:, hh * 2 + ab, :], rhs=vt_t[:, hh, 0:65],
                                    start=(ab == 0), stop=(ab == 1)), [p_r, vt_r], [por], signal=(hh == 3 and ab == 1))
                        if itn % 2:
                            S.op("dve", lambda e, po=po, o_t=o_t: e.tensor_copy(
                                out=o_t[:].rearrange("p (h c) -> p h c", h=4),
                                in_=po[:, :].rearrange("p (h c) -> p h c", h=4)[:, :, 0:65]), [por], [o_r])
                        else:
                            S.op("act", lambda e, po=po, o_t=o_t: e.activation(
                                out=o_t[:].rearrange("p (h c) -> p h c", h=4),
                                in_=po[:, :].rearrange("p (h c) -> p h c", h=4)[:, :, 0:65], func=AF.Copy), [por], [o_r])
                        S.dma(OQ, oview[r, 128 * j:128 * (j + 1), :], o_t[:], [o_r], [R["od"]], o_r)

                    prev = None
                    for j in range(nblk):
                        cur = phase1(j)
                        if prev is not None:
                            phase2(prev)
                        prev = cur
                        vt.pop(j - 1, None)
                    phase2(prev)
        S.stage_end()


def stage_hyena(P):
    nc, S, I, IN, sb, psum = P.nc, P.S, P.I, P.IN, P.sb, P.psum
    PI = math.pi
    for s, L in enumerate(P.Ls):
        sc = P.SC[s]
        R = sc["res"]
        N = 2 * L
        N1 = N // 128
        H1 = N1 // 2
        ncg = L // 512
        h3d = nc.dram_tensor(f"h3d{s}", [2, 64, L], BF16, kind="Internal")
        h3d_r = Res(f"h3d{s}", multi=True)
        with contextlib.ExitStack() as st:
            w1, w1_r = sb("fw1", [33, 64], F32, st)
            w2, w2_r = sb("fw2", [64, 64], F32, st)
            w3, w3_r = sb("fw3", [64, 64], F32, st)
            fb, fb_r = sb("fb", [64, 4], F32, st)
            S.dma("sp", w1[:], I["filt_w1"][:, :], [IN], [w1_r], w1_r)
            S.dma("sp", w2[:], I["filt_w2"][:, :], [IN], [w2_r], w2_r)
            S.dma("sp", w3[:], I["filt_w3"][:, :], [IN], [w3_r], w3_r)
            with nc.allow_non_contiguous_dma(reason="tiny"):
                for i, nm in enumerate(("filt_b1", "filt_b2", "filt_b3", "filt_freq")):
                    S.dma("sp", fb[:, i:i + 1], I[nm][0, :].rearrange("(p o) -> p o", o=1), [IN], [fb_r], fb_r)
            G = 4
            ft = [sb(f"ft{i}", [33, 512], F32, st) for i in range(G)]
            ha = [[sb(f"ha{i}_{l}", [64, 512], F32, st) for l in range(3)] for i in range(G)]
            kts = [sb(f"kt{i}", [64, 512], F32, st) for i in range(G)]
            hbf = [sb(f"hbf{i}", [64, 512], BF16, st) for i in range(G)]
            items = [(dr, cg) for dr in range(2) for cg in range(ncg)]
            for b0 in range(0, len(items), G):
                batch = items[b0:b0 + G]
                cur = []
                for gi_, (dr, cg) in enumerate(batch):
                    ft_t, ft_r = ft[gi_]
                    S.dma("sp", ft_t[:], I[f"feats{L}"][dr, :, cg * 512:(cg + 1) * 512], [IN], [ft_r], ft_r)
                    cur.append((ft_t, ft_r, 33))
                for li, (w_, w_r) in enumerate(((w1, w1_r), (w2, w2_r), (w3, w3_r))):
                    nxt = []
                    for gi_ in range(len(batch)):
                        c_t, c_r, kdim = cur[gi_]
                        h_t, h_r = ha[gi_][li]
                        kt, kt_r = kts[gi_]
                        pt, pr = psum()
                        S.op("pe", lambda e, pt=pt, w_=w_, c_t=c_t, kdim=kdim: e.matmul(
                            pt[0:64, :], lhsT=w_[0:kdim, :], rhs=c_t[0:kdim, :], start=True, stop=True), [w_r, c_r], [pr])
                        S.op("dve", lambda e, pt=pt, h_t=h_t, li=li: e.tensor_scalar(
                            out=h_t[:], in0=pt[0:64, :], scalar1=fb[:, li:li + 1], scalar2=fb[:, 3:4],
                            op0=ALU.add, op1=ALU.mult), [pr, fb_r], [h_r])
                        S.op("dve", lambda e, h_t=h_t, kt=kt: e.tensor_scalar(
                            out=kt[:], in0=h_t[:], scalar1=1.0 / TWO_PI, scalar2=MAGIC, op0=ALU.mult, op1=ALU.add), [h_r], [kt_r])
                        S.op("dve", lambda e, kt=kt: e.tensor_scalar_add(out=kt[:], in0=kt[:], scalar1=-MAGIC), [kt_r], [kt_r])
                        S.op("dve", lambda e, h_t=h_t, kt=kt: e.scalar_tensor_tensor(
                            out=h_t[:], in0=kt[:], scalar=-TWO_PI, in1=h_t[:], op0=ALU.mult, op1=ALU.add), [kt_r, h_r], [h_r])
                        S.op("dve", lambda e, h_t=h_t: e.tensor_scalar(
                            out=h_t[:], in0=h_t[:], scalar1=-3.1415925, scalar2=3.1415925, op0=ALU.max, op1=ALU.min), [h_r], [h_r])
                        if li == 2:
                            o_t, o_r = hbf[gi_]
                            S.op("act", lambda e, h_t=h_t, o_t=o_t: e.activation(out=o_t[:], in_=h_t[:], func=AF.Sin), [h_r], [o_r])
                            nxt.append((o_t, o_r, 64))
                        else:
                            S.op("act", lambda e, h_t=h_t: e.activation(out=h_t[:], in_=h_t[:], func=AF.Sin), [h_r], [h_r])
                            nxt.append((h_t, h_r, 64))
                    cur = nxt
                for gi_, (dr, cg) in enumerate(batch):
                    c_t, c_r, _ = cur[gi_]
                    S.dma("act", h3d[dr, :, cg * 512:(cg + 1) * 512], c_t[:], [c_r], [h3d_r], c_r)
            P.pe_flush()
            S.stage_end()
        with contextlib.ExitStack() as st:
            wo, wo_r = sb("fwo", [64, 2 * DH], BF16, st)
            nd, nd_r = sb("nd", [128, 6], F32, st)
            S.dma("pool", wo[:], I["filt_w_out"][:, :], [IN], [wo_r], wo_r)
            S.dma("sp", nd[:], I["ndelta"][:, :], [IN], [nd_r], nd_r)
            kk, kk_r = sb("kk", [128, 2, L], F32, st)
            k2b, k2b_r = sb("k2b", [128, 2 * L], BF16, st)
            asum, asum_r = sb("asum", [128, 1], F32, st)
            h3 = [sb(f"h3_{i}", [64, 512], BF16, st) for i in range(4)]
            tv = [sb(f"tv{i}", [128, 512], F32, st) for i in range(4)]
            it = 0
            for cc in range(6):
                for dr in range(2):
                    for cg in range(ncg):
                        h_t, h_r = h3[it % 4]
                        t_t, t_r = tv[it % 4]
                        it += 1
                        S.dma("sp", h_t[:], h3d[dr, :, cg * 512:(cg + 1) * 512], [h3d_r], [h_r], h_r)
                        S.dma("sp", t_t[:], I[f"tvec{L}"][dr:dr + 1, cg * 512:(cg + 1) * 512].partition_broadcast(128),
                              [IN], [t_r], t_r)
                        pt, pr = psum()
                        S.op("pe", lambda e, pt=pt, h_t=h_t, dr=dr, cc=cc: e.matmul(
                            pt[:, :], lhsT=wo[:, dr * DH + cc * 128:dr * DH + (cc + 1) * 128], rhs=h_t[:],
                            start=True, stop=True), [wo_r, h_r], [pr])
                        S.op("act", lambda e, t_t=t_t, cc=cc: e.activation(out=t_t[:], in_=t_t[:], func=AF.Exp,
                                                                            scale=nd[:, cc:cc + 1]), [t_r, nd_r], [t_r])
                        S.op("dve", lambda e, pt=pt, t_t=t_t, dr=dr, cg=cg: e.tensor_tensor(
                            out=kk[:, dr, cg * 512:(cg + 1) * 512], in0=pt[:, :], in1=t_t[:], op=ALU.mult), [pr, t_r], [kk_r])
                S.op("dve", lambda e: e.memset(kk[:, 1, 0:1], 0.0), [], [kk_r])
                S.op("dve", lambda e: e.memset(asum[:], 0.0), [], [asum_r])
                S.op("act", lambda e: e.activation(out=k2b[:], in_=kk[:].rearrange("p a l -> p (a l)"), func=AF.Abs,
                                                   accum_out=asum[:]), [kk_r, asum_r], [k2b_r, asum_r])
                S.op("dve", lambda e: e.reciprocal(out=asum[:], in_=asum[:]), [asum_r], [asum_r])
                S.op("act", lambda e: e.activation(out=k2b[:], in_=kk[:].rearrange("p a l -> p (a l)"), func=AF.Copy,
                                                   scale=asum[:, 0:1]), [kk_r, asum_r], [k2b_r])
                S.dma("act", sc["k2"][cc * 128:(cc + 1) * 128, :], k2b[:], [k2b_r], [R["k2"]], k2b_r)
            P.pe_flush()
            S.stage_end()
        with contextlib.ExitStack() as st:
            cw, cw_r = sb("cw", [128, 3, 18], F32, st)
            cb, cb_r = sb("cb", [128, 18], F32, st)
            hd, hd_r = sb("hd", [128, 6], F32, st)
            with nc.allow_non_contiguous_dma(reason="tiny"):
                for k in range(3):
                    S.dma("sp", cw[:, k, :], I["conv_w"][k, :].rearrange("(c p) -> p c", p=128), [IN], [cw_r], cw_r)
                S.dma("sp", cb[:], I["conv_b"][0, :].rearrange("(c p) -> p c", p=128), [IN], [cb_r], cb_r)
                S.dma("sp", hd[:], I["hyena_d"][0, :].rearrange("(c p) -> p c", p=128), [IN], [hd_r], hd_r)
            f1t, f1t_r = sb("f1t", [N1, 2 * N1], BF16, st)
            i1t, i1t_r = sb("i1t", [128, 2, 256], BF16, st)
            S.dma("sp", f1t[:], I[f"f1tab{L}"][:, :], [IN], [f1t_r], f1t_r)
            S.dma("sp", i1t[:], I[f"i1tab{L}"][:, :, :], [IN], [i1t_r], i1t_r)
            xin, xin_r = sb("xin", [128, 128, 128], BF16, st)
            B1, B1_r = sb("B1", [128, 32768], BF16, st)
            B2, B2_r = sb("B2", [128, 2 * N1 * 128], BF16, st)
            KH = N1 // 2 + 2
            B1_r.multi = True
            B2_r.multi = True
            dsA, dsB, dsC, dsD = Res("dsA"), Res("dsB"), Res("dsC"), Res("dsD")
            B1f = B1.bitcast(F32)
            B2f = B2.bitcast(F32)
            gts = [sb(f"gts{i}", [128, 2, 3, 128], BF16, st) for i in range(4)]
            kfs = [sb(f"kfs{i}", [128, 512], F32, st) for i in range(4)]
            i2s = [sb(f"i2s{i}", [N1, 512 // H1 if H1 * 128 > 512 else 128, 2, H1], BF16, st) for i in range(2)]
            tmp = [sb(f"ctmp{i}", [128, 2, 128], F32, st) for i in range(8)]
            xsb = [sb(f"xsb{i}", [128, 512], F32, st) for i in range(2)]
            tg = min(128, 512 // H1)
            BT = B1[:, 0:2 * N1 * 128].rearrange("p (r k c) -> p r k c", r=2, k=N1)
            Zb = B1[:, :].rearrange("p (r t c) -> p r t c", r=2, t=128)
            Yb = B2[:, :].rearrange("p (r c k) -> p r c k", r=2, c=128)
            yv = B2f[:, 0:L].rearrange("p (a b) -> p a b", b=128)
            gi = 0

            def f1_pass(K):
                nonlocal gi
                for c0 in range(0, 128, 2):
                    pt, pr = psum()
                    for u in range(2):
                        S.op("pe", lambda e, pt=pt, u=u, c0=c0: e.matmul(
                            pt[:, u * 2 * KH:(u + 1) * 2 * KH].rearrange("p (r k) -> p r k", r=2), lhsT=xin[0:K, c0 + u, :],
                            rhs=f1t[0:K, :].rearrange("p (r k) -> p r k", r=2)[:, :, 0:KH],
                            start=True, stop=True), [xin_r, f1t_r], [pr], signal=(u == 1))
                    src = pt[:, 0:4 * KH].rearrange("p (u r k) -> p r k u", u=2, r=2)
                    gi += 1
                    if gi % 2:
                        S.op("act", lambda e, src=src, c0=c0: e.activation(out=BT[:, :, 0:KH, c0:c0 + 2], in_=src, func=AF.Copy),
                             [pr], [B1_r])
                    else:
                        S.op("dve", lambda e, src=src, c0=c0: e.tensor_copy(out=BT[:, :, 0:KH, c0:c0 + 2], in_=src), [pr], [B1_r])

            def f3_pass(cc, is_filter):
                for q in range(N1 // 4 + 1):
                    g_t, g_r = gts[q % 4]
                    S.dma("sp", g_t[:], I[f"gtab{L}"][:, 2 * q:2 * q + 2, :, :], [IN], [g_r], g_r)
                    pt, pr = psum()
                    pv = pt[:, :].rearrange("p (u r c) -> p u r c", u=2, r=2)
                    for u in range(2):
                        k1 = 2 * q + u
                        for ri, (ga, gb) in enumerate(((0, 2), (1, 0))):
                            S.op("pe", lambda e, pv=pv, u=u, ri=ri, ga=ga, k1=k1, g_t=g_t: e.matmul(
                                pv[:, u, ri, :], lhsT=g_t[:, u, ga, :], rhs=BT[:, 0, k1, :], start=True, stop=False),
                                [g_r, B1_r], [pr], signal=False)
                            S.op("pe", lambda e, pv=pv, u=u, ri=ri, gb=gb, k1=k1, g_t=g_t: e.matmul(
                                pv[:, u, ri, :], lhsT=g_t[:, u, gb, :], rhs=BT[:, 1, k1, :], start=False, stop=True),
                                [g_r, B1_r], [pr], signal=(u == 1 and ri == 1))
                    k_t, k_r = kfs[q % 4]
                    if is_filter:
                        S.op("act", lambda e, pt=pt, k_t=k_t: e.activation(out=k_t[:], in_=pt[:, :], func=AF.Copy), [pr], [k_r])
                        S.dma("act", sc["kf"][cc, q, :, :], k_t[:], [k_r], [R["kf"]], k_r)
                    else:
                        S.dma("sp", k_t[:], sc["kf"][cc, q, :, :], [R["kf"]], [k_r], k_r)
                        kv = k_t[:, :].rearrange("p (u r c) -> p u r c", u=2, r=2)
                        x_t, x_r = xsb[q % 2]
                        S.op("act", lambda e, pt=pt, x_t=x_t: e.activation(out=x_t[:], in_=pt[:, :], func=AF.Copy), [pr], [x_r])
                        xv = x_t[:, :].rearrange("p (u r c) -> p u r c", u=2, r=2)
                        tr = [tmp[(q % 2) * 4 + i] for i in range(4)]
                        for i, (xa, ka, eng) in enumerate(((0, 1, "pool"), (1, 0, "pool"), (0, 0, "dve"), (1, 1, "dve"))):
                            S.op(eng, lambda e, i=i, xa=xa, ka=ka, xv=xv, kv=kv, tr=tr: e.tensor_tensor(
                                out=tr[i][0][:], in0=xv[:, :, xa, :], in1=kv[:, :, ka, :], op=ALU.mult), [x_r, k_r], [tr[i][1]])
                        S.op("dve", lambda e, q=q, tr=tr: e.tensor_tensor(
                            out=Yb[:, 0, :, 2 * q:2 * q + 2], in0=tr[2][0][:].rearrange("p u c -> p c u"),
                            in1=tr[3][0][:].rearrange("p u c -> p c u"), op=ALU.subtract), [tr[2][1], tr[3][1]], [B2_r])
                        S.op("dve", lambda e, q=q, tr=tr: e.tensor_tensor(
                            out=Yb[:, 1, :, 2 * q:2 * q + 2], in0=tr[0][0][:].rearrange("p u c -> p c u"),
                            in1=tr[1][0][:].rearrange("p u c -> p c u"), op=ALU.add), [tr[0][1], tr[1][1]], [B2_r])

            for cc in range(6):
                T1 = B1[:, 0:L + 2]
                T2 = B1[:, L + 2:2 * L + 4]
                AO = B1[:, 2 * L + 4:3 * L + 4]
                x1c = B2f[:, 0:L]
                vc = B2f[:, L:2 * L]
                S.dma("sp", T1, sc["zhy"][DH + cc * 128:DH + (cc + 1) * 128, :], [R["zhy"]], [B1_r], dsA)
                S.dma("sp", T2, sc["zhy"][2 * DH + cc * 128:2 * DH + (cc + 1) * 128, :], [R["zhy"]], [B1_r], dsB)
                for src, dst, ch in ((T1, x1c, 6 + cc), (T2, vc, 12 + cc)):
                    S.op("act", lambda e, src=src, dst=dst, ch=ch: e.activation(
                        out=dst, in_=src[:, 1:L + 1], func=AF.Identity, scale=cw[:, 1, ch:ch + 1], bias=cb[:, ch:ch + 1]),
                        [B1_r, cw_r, cb_r], [B2_r])
                    for k in (0, 2):
                        S.op("dve", lambda e, src=src, dst=dst, ch=ch, k=k: e.scalar_tensor_tensor(
                            out=dst, in0=src[:, k:L + k], scalar=cw[:, k, ch:ch + 1], in1=dst, op0=ALU.mult, op1=ALU.add),
                            [B1_r, B2_r, cw_r], [B2_r])
                S.op("dve", lambda e: e.tensor_tensor(out=AO, in0=x1c, in1=vc, op=ALU.mult), [B2_r], [B1_r])
                S.dma("pool", sc["aT"][cc * 128:(cc + 1) * 128, :], AO, [B1_r], [R["aT"]], dsC)
                if cc == 0:
                    S.dma("sp", xin[0:N1, :, :], sc["k2"][0:128, :].rearrange("c (a b) -> a c b", b=128),
                          [R["k2"]], [xin_r], xin_r)
                f1_pass(N1)
                S.dma("sp", xin[0:H1, :, :], sc["aT"][cc * 128:(cc + 1) * 128, :].rearrange("c (a b) -> a c b", b=128),
                      [R["aT"]], [xin_r], xin_r)
                f3_pass(cc, True)
                f1_pass(H1)
                if cc + 1 < 6:
                    S.dma("act", xin[0:N1, :, :], sc["k2"][(cc + 1) * 128:(cc + 2) * 128, :].rearrange("c (a b) -> a c b", b=128),
                          [R["k2"]], [xin_r], xin_r)
                f3_pass(cc, False)
                for c0 in range(0, 128, 2):
                    pt, pr = psum()
                    for u in range(2):
                        for ri in range(2):
                            S.op("pe", lambda e, pt=pt, u=u, ri=ri, c0=c0: e.matmul(
                                pt[0:KH, u * 256:(u + 1) * 256], lhsT=Yb[:, ri, c0 + u, 0:KH], rhs=i1t[:, ri, :],
                                start=(ri == 0), stop=(ri == 1)), [B2_r, i1t_r], [pr], signal=(u == 1 and ri == 1))
                    src = pt[0:KH, :].rearrange("p (u r t) -> p r t u", u=2, r=2)
                    gi += 1
                    if gi % 2:
                        S.op("act", lambda e, src=src, c0=c0: e.activation(out=Zb[0:KH, :, :, c0:c0 + 2], in_=src, func=AF.Copy),
                             [pr], [B1_r])
                    else:
                        S.op("dve", lambda e, src=src, c0=c0: e.tensor_copy(out=Zb[0:KH, :, :, c0:c0 + 2], in_=src), [pr], [B1_r])
                for t2g in range(128 // tg):
                    i_t, i_r = i2s[t2g % 2]
                    S.dma("sp", i_t[:, 0:tg, :, :], I[f"i2tab{L}"][:, t2g * tg:(t2g + 1) * tg, :, :], [IN], [i_r], i_r)
                    pt, pr = psum()
                    pv = pt[:, 0:H1 * tg].rearrange("p (a w) -> p a w", w=tg)
                    for w in range(tg):
                        t2 = t2g * tg + w
                        for ri in range(2):
                            S.op("pe", lambda e, pv=pv, w=w, t2=t2, ri=ri, i_t=i_t: e.matmul(
                                pv[:, :, w], lhsT=Zb[0:KH, ri, t2, :], rhs=i_t[0:KH, w, ri, :], start=(ri == 0), stop=(ri == 1)),
                                [B1_r, i_r], [pr], signal=(w == tg - 1 and ri == 1))
                    gi += 1
                    if gi % 2:
                        S.op("act", lambda e, pv=pv, t2g=t2g: e.activation(out=yv[:, :, t2g * tg:(t2g + 1) * tg], in_=pv, func=AF.Copy),
                             [pr], [B2_r])
                    else:
                        S.op("dve", lambda e, pv=pv, t2g=t2g: e.tensor_copy(out=yv[:, :, t2g * tg:(t2g + 1) * tg], in_=pv), [pr], [B2_r])
                E1 = B1[:, 0:L]
                E2 = B1[:, L:2 * L + 2]
                E4 = B1[:, 2 * L + 2:3 * L + 2]
                ysb = B2f[:, 0:L]
                x0c = B2f[:, L:2 * L]
                S.dma("sp", E1, sc["aT"][cc * 128:(cc + 1) * 128, :], [R["aT"]], [B1_r], dsA)
                S.dma("sp", E2, sc["zhy"][cc * 128:(cc + 1) * 128, :], [R["zhy"]], [B1_r], dsB)
                S.op("act", lambda e, cc=cc: e.activation(out=x0c, in_=E2[:, 1:L + 1], func=AF.Identity,
                                                          scale=cw[:, 1, cc:cc + 1], bias=cb[:, cc:cc + 1]),
                     [B1_r, cw_r, cb_r], [B2_r])
                for k in (0, 2):
                    S.op("dve", lambda e, cc=cc, k=k: e.scalar_tensor_tensor(
                        out=x0c, in0=E2[:, k:L + k], scalar=cw[:, k, cc:cc + 1], in1=x0c, op0=ALU.mult, op1=ALU.add),
                        [B1_r, B2_r, cw_r], [B2_r])
                S.op("dve", lambda e, cc=cc: e.scalar_tensor_tensor(
                    out=ysb, in0=E1, scalar=hd[:, cc:cc + 1], in1=ysb, op0=ALU.mult, op1=ALU.add), [B1_r, B2_r, hd_r], [B2_r])
                S.op("dve", lambda e: e.tensor_tensor(out=E4, in0=ysb, in1=x0c, op=ALU.mult), [B2_r], [B1_r])
                S.dma("pool", sc["yhy"][cc * 128:(cc + 1) * 128, :], E4, [B1_r], [R["yhy"]], dsD)
            S.stage_end()


def stage_D1(P):
    nc, S, I, IN, sb, psum = P.nc, P.S, P.I, P.IN, P.sb, P.psum
    with contextlib.ExitStack() as st:
        whb, _ = sb("whb", [128, 6, D], BF16, st)
        wab, _ = sb("wab", [128, 2, D], BF16, st)
        wo, _ = sb("wo", [128, 8, D], BF16, st)
        whb_r, wab_r, wo_r = Res("whb"), Res("wab"), Res("wo")
        S.dma("pool", whb[:], I["w_hy_br"].rearrange("(k p) n -> p k n", p=128), [IN], [whb_r], whb_r)
        S.dma("pool", wab[:], I["w_at_br"].rearrange("(k p) n -> p k n", p=128), [IN], [wab_r], wab_r)
        S.dma("pool", wo[:], I["w_out"].rearrange("(k p) n -> p k n", p=128), [IN], [wo_r], wo_r)
        gtb1 = [sb(f"gt1_{s_}", [128, D], F32, st) for s_ in range(len(P.Ls))]
        for s_ in range(len(P.Ls)):
            S.dma("sp", gtb1[s_][0][:], P.gtbd[:, (s_ * 2) * D:(s_ * 2 + 1) * D], [P.gtbd_r], [gtb1[s_][1]], gtb1[s_][1])
        yh = [sb(f"yh{i}", [128, 6, 512], BF16, st) for i in range(2)]
        gg = [sb(f"gg{i}", [128, 16, 512], BF16, st) for i in range(2)]
        odt = [sb(f"odt{i}", [128, 4, 3, 260], F32, st) for i in range(2)]
        osum, osum_r = sb("osum", [128, 4, 65], F32, st)
        rden, rden_r = sb("rden", [128, 4], F32, st)
        yatb = [sb(f"yat{i}", [128, 4, 256], BF16, st) for i in range(2)]
        yatTb = [sb(f"yatT{i}", [128, 2, 512], BF16, st) for i in range(2)]
        mix, mix_r = sb("mix", [128, 8, 512], BF16, st)
        xt = [sb(f"xd{i}", [128, 4, D], F32, st) for i in range(2)]
        tm = [sb(f"tm{i}", [128, 512], F32, st) for i in range(6)]
        ti = [0]
        tiles = [(s_, mt) for s_, L in enumerate(P.Ls) for mt in range(L // 512)]
        nt = len(tiles)

        def load(i):
            s_, mt = tiles[i]
            sc = P.SC[s_]
            R = sc["res"]
            t0 = mt * 512
            S.dma("sp", yh[i % 2][0][:], sc["yhy"][:, t0:t0 + 512].rearrange("(k p) t -> p k t", p=128), [R["yhy"]],
                  [yh[i % 2][1]], yh[i % 2][1])
            S.dma("sp", gg[i % 2][0][:], sc["gT"][:, t0:t0 + 512].rearrange("(k p) t -> p k t", p=128), [R["gT"]],
                  [gg[i % 2][1]], gg[i % 2][1])
            S.dma("sp", xt[i % 2][0][:], I[f"x{s_}"][t0:t0 + 512, :].rearrange("(j p) d -> p j d", p=128), [IN],
                  [xt[i % 2][1]], xt[i % 2][1])
            for jb in range(4):
                S.dma("act", odt[i % 2][0][:, jb, :, :],
                      sc["od"][:, t0 + jb * 128:t0 + (jb + 1) * 128, :].rearrange("g t c -> t g c"),
                      [R["od"]], [odt[i % 2][1]], Res(f"odsem{i % 2}{jb}") if False else odsem[i % 2][jb])

        odsem = [[Res(f"odsem{a_}{b_}") for b_ in range(4)] for a_ in range(2)]

        def merge(i):
            o_t, o_r = odt[i % 2]
            yat, yat_r = yatb[i % 2]
            ov = osum[:].rearrange("p h c -> p (h c)")
            for jb in range(4):
                S.op("dve", lambda e, jb=jb: e.tensor_tensor(out=ov, in0=o_t[:, jb, 0, :], in1=o_t[:, jb, 1, :], op=ALU.add),
                     [o_r], [osum_r])
                S.op("dve", lambda e, jb=jb: e.tensor_tensor(out=ov, in0=ov, in1=o_t[:, jb, 2, :], op=ALU.add),
                     [o_r, osum_r], [osum_r])
                S.op("dve", lambda e: e.reciprocal(out=rden[:], in_=osum[:, :, 64]), [osum_r], [rden_r])
                for hh in range(4):
                    S.op("dve", lambda e, hh=hh, jb=jb: e.tensor_scalar_mul(
                        out=yat[:, jb, hh * 64:(hh + 1) * 64], in0=osum[:, hh, 0:64], scalar1=rden[:, hh:hh + 1]),
                        [osum_r, rden_r], [yat_r])

        def xpose(i):
            yat, yat_r = yatb[i % 2]
            yatT, yatT_r = yatTb[i % 2]
            for fc in range(2):
                pt, pr = psum()
                ptb = pt.bitcast(BF16)
                for jb in range(4):
                    S.op("pe", lambda e, ptb=ptb, jb=jb, fc=fc: e.transpose(
                        out=ptb[:, jb * 128:(jb + 1) * 128], in_=yat[:, jb, fc * 128:(fc + 1) * 128], identity=P.ident_b[:]),
                        [yat_r, P.ident_b_r], [pr], signal=(jb == 3))
                S.op("act", lambda e, ptb=ptb, fc=fc: e.activation(out=yatT[:, fc, :], in_=ptb[:, 0:512], func=AF.Copy),
                     [pr], [yatT_r])

        def mm1(i):
            yh_t, yh_r = yh[i % 2]
            gg_t, gg_r = gg[i % 2]
            yatT, yatT_r = yatTb[i % 2]
            for m in range(8):
                p1, p1r = psum()
                for kc in range(6):
                    S.op("pe", lambda e, p1=p1, kc=kc, m=m: e.matmul(
                        p1[:, :], lhsT=whb[:, kc, m * 128:(m + 1) * 128], rhs=yh_t[:, kc, :], start=(kc == 0), stop=(kc == 5)),
                        [whb_r, yh_r], [p1r], signal=(kc == 5))
                p2, p2r = psum()
                for kc in range(2):
                    S.op("pe", lambda e, p2=p2, kc=kc, m=m: e.matmul(
                        p2[:, :], lhsT=wab[:, kc, m * 128:(m + 1) * 128], rhs=yatT[:, kc, :], start=(kc == 0), stop=(kc == 1)),
                        [wab_r, yatT_r], [p2r], signal=(kc == 1))
                ta, ta_r = tm[ti[0] % 6]
                tb, tb_r = tm[(ti[0] + 1) % 6]
                ti[0] += 2
                S.op("dve", lambda e, p1=p1, m=m, ta=ta: e.tensor_tensor(out=ta[:], in0=p1[:, :], in1=gg_t[:, m, :], op=ALU.mult),
                     [p1r, gg_r], [ta_r])
                S.op("dve", lambda e, p2=p2, m=m, tb=tb: e.tensor_tensor(out=tb[:], in0=p2[:, :], in1=gg_t[:, 8 + m, :], op=ALU.mult),
                     [p2r, gg_r], [tb_r])
                S.op("pool", lambda e, m=m, ta=ta, tb=tb: e.tensor_tensor(out=mix[:, m, :], in0=ta[:], in1=tb[:], op=ALU.add),
                     [ta_r, tb_r], [mix_r])

        def mm2(i):
            s_, mt = tiles[i]
            sc = P.SC[s_]
            x_t, x_r = xt[i % 2]
            gt1, gt1_r = gtb1[s_]
            for jb in range(4):
                for half in range(2):
                    pt, pr = psum()
                    for m in range(8):
                        S.op("pe", lambda e, pt=pt, m=m, jb=jb, half=half: e.matmul(
                            pt[:, :], lhsT=mix[:, m, jb * 128:(jb + 1) * 128], rhs=wo[:, m, half * 512:(half + 1) * 512],
                            start=(m == 0), stop=(m == 7)), [mix_r, wo_r], [pr], signal=(m == 7))
                    ta, ta_r = tm[ti[0] % 6]
                    ti[0] += 1
                    S.op("dve", lambda e, pt=pt, half=half, ta=ta: e.tensor_tensor(
                        out=ta[:], in0=pt[:, :], in1=gt1[:, half * 512:(half + 1) * 512], op=ALU.mult), [pr, gt1_r], [ta_r])
                    S.op("pool", lambda e, jb=jb, half=half, ta=ta: e.tensor_tensor(
                        out=x_t[:, jb, half * 512:(half + 1) * 512], in0=ta[:], in1=x_t[:, jb, half * 512:(half + 1) * 512],
                        op=ALU.add), [ta_r, x_r], [x_r])
            S.dma("sp", sc["h"][mt * 512:(mt + 1) * 512, :].rearrange("(j p) d -> p j d", p=128), x_t[:], [x_r],
                  [sc["res"]["h"]], x_r)

        load(0)
        merge(0)
        xpose(0)
        for i in range(nt):
            if i + 1 < nt:
                load(i + 1)
            mm1(i)
            if i + 1 < nt:
                merge(i + 1)
                xpose(i + 1)
            mm2(i)
        S.stage_end()


def stage_D2(P):
    nc, S, I, IN, sb, psum = P.nc, P.S, P.I, P.IN, P.sb, P.psum
    with contextlib.ExitStack() as st:
        wup, _ = sb("wup", [128, 8, DFF], BF16, st)
        wdn, _ = sb("wdn", [128, 32, D], BF16, st)
        wup_r = [Res(f"wup{k}") for k in range(8)]
        wdn_r = [Res(f"wdn{k}") for k in range(4)]
        for kc in range(8):
            S.dma("pool", wup[:, kc, :], I["w_up"][kc * 128:(kc + 1) * 128, :], [IN], [wup_r[kc]], wup_r[kc])
        for k in range(4):
            S.dma("pool", wdn[:, 8 * k:8 * k + 8, :], I["w_down"][1024 * k:1024 * (k + 1), :].rearrange("(f p) n -> p f n", p=128),
                  [IN], [wdn_r[k]], wdn_r[k])
        gtb2 = [sb(f"gt2_{s_}", [128, D], F32, st) for s_ in range(len(P.Ls))]
        for s_ in range(len(P.Ls)):
            S.dma("sp", gtb2[s_][0][:], P.gtbd[:, (s_ * 2 + 1) * D:(s_ * 2 + 2) * D], [P.gtbd_r], [gtb2[s_][1]], gtb2[s_][1])
        htb = [sb(f"ht{i}", [128, 2, D], F32, st) for i in range(3)]
        xnb = [sb(f"xn2_{i}", [128, 2, D], BF16, st) for i in range(2)]
        uT = [sb(f"u2T{i}", [128, 8, 256], BF16, st) for i in range(2)]
        hid, hid_r = sb("hid", [128, 32, 256], BF16, st)
        ssb = [sb(f"ss2_{i}", [128, 2], F32, st) for i in range(2)]
        junk, junk_r = sb("junk2", [128, D], BF16, st)
        tm = [sb(f"tn{i}", [128, 512], F32, st) for i in range(3)]
        ti = [0]
        tiles = [(s_, mt) for s_, L in enumerate(P.Ls) for mt in range(L // 256)]
        nt = len(tiles)

        def load(i):
            s_, mt = tiles[i]
            ht, ht_r = htb[i % 3]
            S.dma("sp", ht[:], P.SC[s_]["h"][mt * 256:(mt + 1) * 256, :].rearrange("(j p) d -> p j d", p=128),
                  [P.SC[s_]["res"]["h"]], [ht_r], ht_r)

        def norm(i):
            ht, ht_r = htb[i % 3]
            xn, xn_r = xnb[i % 2]
            ss, ss_r = ssb[i % 2]
            S.op("dve", lambda e: e.memset(ss[:], 0.0), [], [ss_r])
            for j in range(2):
                S.op("act", lambda e, j=j: e.activation(out=junk[:], in_=ht[:, j, :], func=AF.Square,
                                                        accum_out=ss[:, j:j + 1]), [ht_r, ss_r], [junk_r, ss_r])
            S.op("dve", lambda e: e.tensor_scalar(out=ss[:], in0=ss[:], scalar1=1.0 / D, scalar2=EPS,
                                                  op0=ALU.mult, op1=ALU.add), [ss_r], [ss_r])
            S.op("act", lambda e: e.activation(out=ss[:], in_=ss[:], func=AF.Sqrt), [ss_r], [ss_r])
            S.op("dve", lambda e: e.reciprocal(out=ss[:], in_=ss[:]), [ss_r], [ss_r])
            for j in range(2):
                S.op("act", lambda e, j=j: e.activation(out=xn[:, j, :], in_=ht[:, j, :], func=AF.Copy,
                                                        scale=ss[:, j:j + 1]), [ht_r, ss_r], [xn_r])

        def xpose(i):
            s_, mt = tiles[i]
            xn, xn_r = xnb[i % 2]
            uT_t, uT_r = uT[i % 2]
            for kc in range(8):
                pt, pr = psum()
                ptb = pt.bitcast(BF16)
                for j in range(2):
                    S.op("pe", lambda e, j=j, kc=kc, ptb=ptb: e.transpose(
                        out=ptb[:, j * 128:(j + 1) * 128], in_=xn[:, j, kc * 128:(kc + 1) * 128], identity=P.ident_b[:]),
                        [xn_r, P.ident_b_r], [pr], signal=(j == 1))
                S.op("dve", lambda e, kc=kc, ptb=ptb: e.tensor_scalar(
                    out=uT_t[:, kc, :], in0=ptb[:, 0:256], scalar1=P.modT[:, s_, 3, kc:kc + 1],
                    scalar2=P.modT[:, s_, 2, kc:kc + 1], op0=ALU.mult, op1=ALU.add), [pr, P.modT_r], [uT_r])

        def up(i):
            uT_t, uT_r = uT[i % 2]
            for f2 in range(16):
                pt, pr = psum()
                for u in range(2):
                    fc = 2 * f2 + u
                    for kc in range(8):
                        S.op("pe", lambda e, pt=pt, u=u, fc=fc, kc=kc: e.matmul(
                            pt[:, u * 256:(u + 1) * 256], lhsT=wup[:, kc, fc * 128:(fc + 1) * 128], rhs=uT_t[:, kc, :],
                            start=(kc == 0), stop=(kc == 7)), [wup_r[kc], uT_r], [pr], signal=(kc == 7 and u == 1))
                ta, ta_r = tm[ti[0] % 3]
                ti[0] += 1
                S.op("act", lambda e, pt=pt, ta=ta: e.activation(out=ta[:], in_=pt[:, :], func=AF.Relu), [pr], [ta_r])
                S.op("dve" if f2 % 2 else "pool", lambda e, ta=ta, f2=f2: e.tensor_tensor(
                    out=hid[:, 2 * f2:2 * f2 + 2, :], in0=ta[:].rearrange("p (u t) -> p u t", u=2),
                    in1=ta[:].rearrange("p (u t) -> p u t", u=2), op=ALU.mult), [ta_r], [hid_r])

        def down(i):
            s_, mt = tiles[i]
            ht, ht_r = htb[i % 3]
            gt2, gt2_r = gtb2[s_]
            for j in range(2):
                for half in range(2):
                    pt, pr = psum()
                    for fc in range(32):
                        S.op("pe", lambda e, pt=pt, fc=fc, j=j, half=half: e.matmul(
                            pt[:, :], lhsT=hid[:, fc, j * 128:(j + 1) * 128], rhs=wdn[:, fc, half * 512:(half + 1) * 512],
                            start=(fc == 0), stop=(fc == 31)), [hid_r, wdn_r[fc // 8]], [pr], signal=(fc == 31))
                    ta, ta_r = tm[ti[0] % 3]
                    ti[0] += 1
                    S.op("dve", lambda e, pt=pt, half=half, ta=ta: e.tensor_tensor(
                        out=ta[:], in0=pt[:, :], in1=gt2[:, half * 512:(half + 1) * 512], op=ALU.mult), [pr, gt2_r], [ta_r])
                    S.op("pool", lambda e, j=j, half=half, ta=ta: e.tensor_tensor(
                        out=ht[:, j, half * 512:(half + 1) * 512], in0=ta[:], in1=ht[:, j, half * 512:(half + 1) * 512],
                        op=ALU.add), [ta_r, ht_r], [ht_r])
            S.dma("sp", P.O[s_][mt * 256:(mt + 1) * 256, :].rearrange("(j p) d -> p j d", p=128), ht[:], [ht_r], [P.OUT_r], ht_r)

        load(0)
        if nt > 1:
            load(1)
        norm(0)
        xpose(0)
        for i in range(nt):
            if i + 1 < nt:
                norm(i + 1)
            if i + 2 < nt:
                load(i + 2)
            up(i)
            if i + 1 < nt:
                xpose(i + 1)
            down(i)
        S.stage_end()


def build_all(Ls):
    P, hc = build(Ls)
    P.OUT_r = Res("out", multi=True)
    stage_A(P)
    stage_attn(P)
    stage_hyena(P)
    stage_D1(P)
    stage_D2(P)
    return finish(P), hc


_CACHE = {}


def kernel(**inputs):
    Ls = [8192, 4096]
    if "nc" not in _CACHE:
        _CACHE["nc"], _CACHE["hc"] = build_all(Ls)
    nc, hc = _CACHE["nc"], _CACHE["hc"]
    f = lambda a: np.ascontiguousarray(np.asarray(a))
    shared = {}
    for nm in ("rel_bias",):
        shared[nm] = f(inputs[nm])
    for nm in ("ada_w", "w_in", "conv_w", "filt_w1", "filt_w2", "filt_w3", "filt_w_out", "w_hy_br", "w_at_br", "w_out",
               "w_up", "w_down"):
        shared[nm] = f(inputs[nm])[0]
    for nm in ("ada_b", "norm1_g", "conv_b", "filt_b1", "filt_b2", "filt_b3", "filt_freq", "hyena_d", "norm2_g"):
        shared[nm] = f(inputs[nm]).reshape(1, -1)
    shared["q_norm_g"] = f(inputs["q_norm_g"]).reshape(1, -1)
    shared["k_norm_g"] = f(inputs["k_norm_g"]).reshape(1, -1)
    shared.update(hc)
    xp, xs = f(inputs["x_prompt"]), f(inputs["x_sample"])
    cp, cs = f(inputs["c_prompt"]), f(inputs["c_sample"])
    in_maps = []
    for i in range(8):
        m = dict(shared)
        m["x0"] = xp[i]
        m["x1"] = xs[i]
        m["c"] = np.stack([cp[i], cs[i]], axis=0)
        in_maps.append(m)
    res = run_bass_kernel_spmd(nc, in_maps, core_ids=list(range(8)))
    yp = np.stack([np.asarray(r["y0"]) for r in res.results], axis=0).astype(np.float32)
    ys = np.stack([np.asarray(r["y1"]) for r in res.results], axis=0).astype(np.float32)
    return (yp, ys)
```

```python
import contextlib
import math
import numpy as np
import ml_dtypes
import concourse.bass as bass
import concourse.mybir as mybir
from concourse.bass_utils import run_bass_kernel_spmd

F32 = mybir.dt.float32
BF16 = mybir.dt.bfloat16
ALU = mybir.AluOpType
AF = mybir.ActivationFunctionType
AX = mybir.AxisListType

D = 1024
DH = 768
NH = 12
HD = 64
DFF = 4096
WIN = 6656
EPS = 1e-6
NEG = -30000.0
TWO_PI = 2.0 * math.pi
MAGIC = 12582912.0


class Res:
    __slots__ = ("name", "w", "r", "multi", "dsem")

    def __init__(self, name, multi=False):
        self.name = name
        self.w = {}
        self.r = {}
        self.multi = multi
        self.dsem = {}


class DSem:
    def __init__(self, sem, key, kind):
        self.sem = sem
        self.key = key
        self.kind = kind
        self.n = 0


class Sched:
    def __init__(self, nc, es, n_hw=44, n_sw=24, same_engine_sync=True):
        self.nc = nc
        self.E = {"pe": nc.tensor, "act": nc.scalar, "dve": nc.vector, "pool": nc.gpsimd, "sp": nc.sync}
        self.esem = {k: es.enter_context(nc.semaphore("e_" + k)) for k in ("pe", "act", "dve", "pool")}
        self.cnt = {k: 0 for k in self.esem}
        self.seen = {k: {} for k in self.E}
        self.same = same_engine_sync
        self.pool_ds = {
            "hw": [DSem(es.enter_context(nc.semaphore(f"dh{i}")), f"dh{i}", "hw") for i in range(n_hw)],
            "sw": [DSem(es.enter_context(nc.semaphore(f"ds{i}")), f"ds{i}", "sw") for i in range(n_sw)],
        }
        self.all_ds = self.pool_ds["hw"] + self.pool_ds["sw"]
        self.free_ds = {"hw": list(self.pool_ds["hw"]), "sw": list(self.pool_ds["sw"])}
        self.stage_res = []
        self.ninst = 0

    @staticmethod
    def _add(deps, d):
        for k, (s, v) in d.items():
            if k not in deps or deps[k][1] < v:
                deps[k] = (s, v)

    def _wait(self, eng, deps):
        seen = self.seen[eng]
        for k, (s, v) in deps.items():
            if k == "e_" + eng and (eng == "pe" or not self.same):
                continue
            if seen.get(k, 0) >= v:
                continue
            self.E[eng].wait_ge(s, v)
            seen[k] = v
            self.ninst += 1

    def _deps(self, reads, writes):
        deps = {}
        for r in reads:
            self._add(deps, r.w)
        for w in writes:
            self._add(deps, w.r)
            if not w.multi:
                self._add(deps, w.w)
        return deps

    @staticmethod
    def _record(key, sem, ev, reads, writes):
        for r in reads:
            if r.r.get(key, (None, 0))[1] < ev:
                r.r[key] = (sem, ev)
        for w in writes:
            if w.multi:
                if w.w.get(key, (None, 0))[1] < ev:
                    w.w[key] = (sem, ev)
            else:
                w.w = {key: (sem, ev)}
                w.r = {}

    def op(self, eng, fn, reads=(), writes=(), signal=True):
        self._wait(eng, self._deps(reads, writes))
        inst = fn(self.E[eng])
        self.ninst += 1
        sem = self.esem[eng]
        if signal:
            self.cnt[eng] += 1
            inst.then_inc(sem, 1)
            ev = self.cnt[eng]
        else:
            ev = self.cnt[eng] + 1
        self._record("e_" + eng, sem, ev, reads, writes)
        return inst

    def dma(self, q, out, in_, reads, writes, sb, **kw):
        kind = "sw" if q == "pool" else "hw"
        ds = sb.dsem.get(kind)
        if ds is None:
            assert self.free_ds[kind], "out of dma semaphores " + kind
            ds = self.free_ds[kind].pop()
            sb.dsem[kind] = ds
            self.stage_res.append(sb)
        deps = self._deps(reads, writes)
        if ds.n:
            self._add(deps, {ds.key: (ds.sem, 16 * ds.n)})
        self._wait(q, deps)
        inst = self.E[q].dma_start(out=out, in_=in_, **kw)
        inst.then_inc(ds.sem, 16)
        self.ninst += 1
        ds.n += 1
        self._record(ds.key, ds.sem, 16 * ds.n, reads, writes)
        return inst

    def barrier(self):
        deps = {}
        for k, s in self.esem.items():
            if self.cnt[k]:
                deps["e_" + k] = (s, self.cnt[k])
        for ds in self.all_ds:
            if ds.n:
                deps[ds.key] = (ds.sem, 16 * ds.n)
        for e in self.E:
            d = {k: v for k, v in deps.items() if k != "e_" + e}
            self._wait(e, d)

    def stage_end(self):
        self.barrier()
        for r in self.stage_res:
            for kind, ds in r.dsem.items():
                self.free_ds[kind].append(ds)
            r.dsem = {}
        self.stage_res = []


def t5_bucket_np(rel):
    half = 16
    max_exact = 8
    n = np.abs(rel)
    ret = np.where(rel > 0, half, 0)
    nf = np.maximum(n, 1).astype(np.float32)
    large = max_exact + (np.log(nf / np.float32(max_exact)) / np.float32(math.log(1024 / max_exact))
                         * np.float32(half - max_exact)).astype(np.int32)
    large = np.minimum(large, half - 1)
    return ret + np.where(n < max_exact, n, large)


def host_consts(Ls):
    c = {}
    c["ident_b"] = np.eye(128, dtype=np.float32).astype(ml_dtypes.bfloat16)
    c["ident_f"] = np.eye(128, dtype=np.float32)
    c["antiid"] = np.eye(128, dtype=np.float32)[::-1].copy()
    bd = np.zeros((128, 128), np.float32)
    bd[:64, :64] = 1.0
    bd[64:, 64:] = 1.0
    c["bdones"] = bd.astype(ml_dtypes.bfloat16)
    oh = np.zeros((3, 33, 512), np.float32)
    for g, dil in enumerate((1, 4, 16)):
        for ab in range(2):
            for m in range(255):
                delta = 127 - m
                if ab == 0:
                    valid = delta >= 0
                    rel = delta - 64
                else:
                    valid = delta <= 0
                    rel = delta + 64
                if valid and abs(rel) <= 64:
                    b = int(t5_bucket_np(np.array(rel * dil)))
                    oh[g, b, ab * 256 + m] = 1.0
                else:
                    oh[g, 32, ab * 256 + m] = NEG
            oh[g, 32, ab * 256 + 255] = NEG
    c["bias_oh"] = oh
    for L in sorted(set(Ls)):
        N = 2 * L
        N1 = N // 128
        f32 = np.float32
        t = np.linspace(0.0, 1.0, L, dtype=f32)[:, None]
        bands = np.linspace(1e-4, 15, 16, dtype=f32)[None, :]
        w = (f32(2.0 * math.pi) * np.arange(L, dtype=f32)[:, None] / f32(L)).astype(f32)
        feats = np.concatenate([t, np.cos(bands * w), -np.sin(bands * w)], axis=-1).astype(f32)
        ft = np.zeros((2, 33, L), f32)
        ft[0] = feats.T
        ft[1, :, 1:] = feats[1:][::-1].T
        c[f"feats{L}"] = ft
        tv = np.zeros((2, L), f32)
        tv[0] = t[:, 0]
        tv[1, 1:] = t[1:, 0][::-1]
        c[f"tvec{L}"] = tv
        n1 = np.arange(N1)[:, None]
        k1 = np.arange(N1)[None, :]
        th = 2 * np.pi * n1 * k1 / N1
        c[f"f1tab{L}"] = np.concatenate([np.cos(th), -np.sin(th)], axis=1).astype(ml_dtypes.bfloat16)
        n2 = np.arange(128)[:, None, None]
        kk1 = np.arange(N1)[None, :, None]
        kk2 = np.arange(128)[None, None, :]
        th = 2 * np.pi * ((n2 * (kk1 + N1 * kk2)) % N) / N
        gr = np.cos(th)
        gi = -np.sin(th)
        c[f"gtab{L}"] = np.stack([gr, gi, -gi], axis=2).astype(ml_dtypes.bfloat16)
        k2 = np.arange(128)[:, None]
        t2 = np.arange(128)[None, :]
        th = 2 * np.pi * k2 * t2 / 128
        c2 = np.cos(th)
        s2 = np.sin(th)
        c[f"i1tab{L}"] = np.stack([np.concatenate([c2, s2], 1), np.concatenate([-s2, c2], 1)], axis=1).astype(
            ml_dtypes.bfloat16)
        kk = np.arange(N1)[:, None, None]
        tt2 = np.arange(128)[None, :, None]
        tt1 = np.arange(N1 // 2)[None, None, :]
        th = 2 * np.pi * ((kk * (tt2 + 128 * tt1)) % N) / N
        wk = np.zeros((N1, 1, 1))
        wk[0] = 1.0
        wk[N1 // 2] = 1.0
        wk[1:N1 // 2] = 2.0
        c[f"i2tab{L}"] = np.stack([wk * np.cos(th) / N, -wk * np.sin(th) / N], axis=2).astype(ml_dtypes.bfloat16)
    deltas = np.abs(np.linspace(math.log(1e-2) / 1.5, math.log(1e-2) / 0.3, DH, dtype=np.float32))
    c["ndelta"] = (-deltas).astype(np.float32).reshape(6, 128).T.copy()
    return c


class Prog:
    pass


def build(Ls, stages=None, dbg=()):
    nc = bass.Bass("TRN2", target_bir_lowering=False)
    es = contextlib.ExitStack()
    S = Sched(nc, es)
    NS = len(Ls)
    P = Prog()
    P.nc, P.S, P.es = nc, S, es

    def din(name, shape, dt=F32):
        return nc.dram_tensor(name, list(shape), dt, kind="ExternalInput")

    def dscr(name, shape, dt):
        kind = "ExternalOutput" if name in dbg else "Internal"
        return nc.dram_tensor(name, list(shape), dt, kind=kind)

    I = {}
    for s, L in enumerate(Ls):
        I[f"x{s}"] = din(f"x{s}", [L, D])
    I["c"] = din("c", [NS, D])
    for nm, shp in (("rel_bias", [32, NH]), ("ada_w", [D, 6 * D]), ("ada_b", [1, 6 * D]), ("norm1_g", [1, D]),
                    ("w_in", [D, WIN]), ("conv_w", [3, 3 * DH]), ("conv_b", [1, 3 * DH]),
                    ("filt_w1", [33, 64]), ("filt_b1", [1, 64]), ("filt_w2", [64, 64]), ("filt_b2", [1, 64]),
                    ("filt_w3", [64, 64]), ("filt_b3", [1, 64]), ("filt_freq", [1, 64]),
                    ("filt_w_out", [64, 2 * DH]), ("hyena_d", [1, DH]), ("q_norm_g", [1, NH * HD]),
                    ("k_norm_g", [1, NH * HD]), ("w_hy_br", [DH, D]), ("w_at_br", [256, D]), ("w_out", [D, D]),
                    ("norm2_g", [1, D]), ("w_up", [D, DFF]), ("w_down", [DFF, D])):
        I[nm] = din(nm, shp)
    hc = host_consts(Ls)
    for nm, arr in hc.items():
        I[nm] = din(nm, arr.shape, BF16 if arr.dtype == ml_dtypes.bfloat16 else F32)
    O = [nc.dram_tensor(f"y{s}", [L, D], F32, kind="ExternalOutput") for s, L in enumerate(Ls)]

    SC = []
    for s, L in enumerate(Ls):
        N1 = 2 * L // 128
        d = {}
        d["zhy"] = dscr(f"zhy{s}", [3 * DH, L + 2], BF16)
        d["qT"] = dscr(f"qT{s}", [DH, L], BF16)
        d["kT"] = dscr(f"kT{s}", [DH, L], BF16)
        d["v"] = dscr(f"v{s}", [L, DH], BF16)
        d["gT"] = dscr(f"gT{s}", [2 * D, L], BF16)
        d["aT"] = dscr(f"aT{s}", [DH, L], BF16)
        d["k2"] = dscr(f"k2{s}", [DH, 2 * L], BF16)
        d["kf"] = dscr(f"kf{s}", [6, N1 // 2, 128, 512], F32)
        d["yhy"] = dscr(f"yhy{s}", [DH, L], BF16)
        d["od"] = dscr(f"od{s}", [3, L, 4 * 65], F32)
        d["h"] = dscr(f"h{s}", [L, D], F32)
        d["res"] = {k: Res(f"{k}{s}", multi=True) for k in list(d.keys())}
        SC.append(d)
    gvd = dscr("gvd", [NH, 512], F32)
    gvd_r = Res("gvd", multi=True)
    gtbd = dscr("gtbd", [128, NS * 2 * D], F32)
    gtbd_r = Res("gtbd", multi=True)
    IN = Res("inputs", multi=True)

    PS = []
    for b in range(8):
        t = es.enter_context(nc.psum_tensor(f"ps{b}", [128, 512], F32))
        PS.append((t, Res(f"ps{b}")))
    P.psi = 0
    P.nsb = 0

    def psum():
        t, r = PS[P.psi % 8]
        P.psi += 1
        return t, r

    def sb(name, shape, dt, stack):
        P.nsb += 1
        t = stack.enter_context(nc.sbuf_tensor(f"s{P.nsb}_" + name, list(shape), dt))
        return t, Res(name)

    gs = es
    ident_b, ident_b_r = sb("ident_b", [128, 128], BF16, gs)
    ident_f, ident_f_r = sb("ident_f", [128, 128], F32, gs)
    modT, modT_r = sb("modT", [128, NS, 4, 8], F32, gs)
    S.dma("sp", ident_b[:], I["ident_b"][:, :], [IN], [ident_b_r], ident_b_r)
    S.dma("sp", ident_f[:], I["ident_f"][:, :], [IN], [ident_f_r], ident_f_r)

    want = (lambda n: stages is None or n in stages)

    def pe_flush():
        pt, pr = psum()
        S.op("pe", lambda e: e.transpose(out=pt.bitcast(BF16)[:, 0:128], in_=ident_b[:], identity=ident_b[:]),
             [ident_b_r], [pr])
    P.pe_flush = pe_flush

    if want("mod"):
        with contextlib.ExitStack() as st:
            cT, cT_r = sb("cT", [128, 8, NS], F32, st)
            gtb, gtb_r = sb("gtb", [128, NS, 2, D], F32, st)
            crep, crep_r = sb("crep", [128, 8, NS, 128], F32, st)
            n1g, n1g_r = sb("n1g", [128, 8], F32, st)
            n2g, n2g_r = sb("n2g", [128, 8], F32, st)
            abT, abT_r = sb("abT", [128, 48], F32, st)
            abb, abb_r = sb("abb", [128, 2, D], F32, st)
            aw = [sb(f"aw{i}", [128, 8, 512], F32, st) for i in range(2)]
            with nc.allow_non_contiguous_dma(reason="tiny transposed loads"):
                for s in range(NS):
                    S.dma("sp", cT[:, :, s], I["c"][s, :].rearrange("(k p) -> p k", p=128), [IN], [cT_r], cT_r)
                S.dma("sp", n1g[:], I["norm1_g"][0, :].rearrange("(k p) -> p k", p=128), [IN], [n1g_r], n1g_r)
                S.dma("sp", n2g[:], I["norm2_g"][0, :].rearrange("(k p) -> p k", p=128), [IN], [n2g_r], n2g_r)
                S.dma("sp", abT[:], I["ada_b"][0, :].rearrange("(k p) -> p k", p=128), [IN], [abT_r], abT_r)
            for j, col in enumerate((2 * D, 5 * D)):
                S.dma("sp", abb[:, j, :], I["ada_b"][0:1, col:col + D].partition_broadcast(128), [IN], [abb_r], abb_r)
            S.op("act", lambda e: e.activation(out=cT[:], in_=cT[:], func=AF.Silu), [cT_r], [cT_r])
            S.op("dve", lambda e: e.memset(crep[:], 1.0), [], [crep_r])
            for kc in range(8):
                for s in range(NS):
                    S.op("dve", lambda e, kc=kc, s=s: e.tensor_scalar_mul(out=crep[:, kc, s, :], in0=crep[:, kc, s, :],
                                                                             scalar1=cT[:, kc, s:s + 1]),
                         [cT_r, crep_r], [crep_r])
            for grp in range(12):
                awt, awr = aw[grp % 2]
                S.dma("sp", awt[:], I["ada_w"][:, grp * 512:(grp + 1) * 512].rearrange("(k p) n -> p k n", p=128),
                      [IN], [awr], awr)
                sec = grp // 2
                if sec in (2, 5):
                    j = 0 if sec == 2 else 1
                    for s in range(NS):
                        pt, pr = psum()
                        for kc in range(8):
                            S.op("pe", lambda e, kc=kc, s=s, pt=pt, awt=awt: e.matmul(
                                pt[:, :], lhsT=crep[:, kc, s, :], rhs=awt[:, kc, :], start=(kc == 0), stop=(kc == 7)),
                                [crep_r, awr], [pr], signal=(kc == 7))
                        c0 = (grp % 2) * 512
                        S.op("dve", lambda e, pt=pt, s=s, j=j, c0=c0: e.tensor_tensor(
                            out=gtb[:, s, j, c0:c0 + 512], in0=pt[:, :], in1=abb[:, j, c0:c0 + 512], op=ALU.add),
                            [pr, abb_r], [gtb_r])
                else:
                    slot = {0: 0, 1: 1, 3: 2, 4: 3}[sec]
                    pt, pr = psum()
                    for sub in range(4):
                        for kc in range(8):
                            S.op("pe", lambda e, kc=kc, sub=sub, pt=pt, awt=awt: e.matmul(
                                pt[:, sub * NS:(sub + 1) * NS], lhsT=awt[:, kc, sub * 128:(sub + 1) * 128],
                                rhs=cT[:, kc, :], start=(kc == 0), stop=(kc == 7)),
                                [cT_r, awr], [pr], signal=(kc == 7 and sub == 3))
                    for sub in range(4):
                        ch = (grp % 2) * 4 + sub
                        acol = sec * 8 + ch
                        for s in range(NS):
                            S.op("dve", lambda e, pt=pt, sub=sub, s=s, slot=slot, ch=ch, acol=acol: e.tensor_tensor(
                                out=modT[:, s, slot, ch:ch + 1], in0=pt[:, sub * NS + s:sub * NS + s + 1],
                                in1=abT[:, acol:acol + 1], op=ALU.add), [pr, abT_r], [modT_r])
            for s in range(NS):
                for slot, gt_, gr_ in ((1, n1g, n1g_r), (3, n2g, n2g_r)):
                    S.op("dve", lambda e, s=s, slot=slot, gt_=gt_: e.scalar_tensor_tensor(
                        out=modT[:, s, slot, :], in0=modT[:, s, slot, :], scalar=1.0, in1=gt_[:],
                        op0=ALU.add, op1=ALU.mult), [modT_r, gr_], [modT_r])
            S.dma("sp", gtbd[:, :], gtb[:].rearrange("p a b c -> p (a b c)"), [gtb_r], [gtbd_r], gtb_r)
            pe_flush()
            S.stage_end()

    P.I, P.O, P.SC, P.IN = I, O, SC, IN
    P.sb, P.psum, P.want = sb, psum, want
    P.ident_b, P.ident_b_r, P.ident_f, P.ident_f_r = ident_b, ident_b_r, ident_f, ident_f_r
    P.modT, P.modT_r, P.gtbd, P.gtbd_r = modT, modT_r, gtbd, gtbd_r
    P.gvd, P.gvd_r = gvd, gvd_r
    P.Ls = Ls
    return P, hc


def finish(P):
    P.S.barrier()
    P.es.close()
    return P.nc


def stage_A(P):
    nc, S, I, IN, sb, psum = P.nc, P.S, P.I, P.IN, P.sb, P.psum
    AQ = "act"
    with contextlib.ExitStack() as st:
        w_sb, _ = sb("w_in_sb", [128, 8, WIN], BF16, st)
        wr = [Res(f"w_in{kc}") for kc in range(8)]
        for kc in range(8):
            S.dma("pool", w_sb[:, kc, :], I["w_in"][kc * 128:(kc + 1) * 128, :], [IN], [wr[kc]], wr[kc])
        bd, bd_r = sb("bd", [128, 128], BF16, st)
        S.dma("sp", bd[:], I["bdones"][:, :], [IN], [bd_r], bd_r)
        gq, gq_r = sb("gq", [128, 12], F32, st)
        with nc.allow_non_contiguous_dma(reason="tiny transposed loads"):
            S.dma("sp", gq[:, 0:6], I["q_norm_g"][0, :].rearrange("(k p) -> p k", p=128), [IN], [gq_r], gq_r)
            S.dma("sp", gq[:, 6:12], I["k_norm_g"][0, :].rearrange("(k p) -> p k", p=128), [IN], [gq_r], gq_r)
        S.op("dve", lambda e: e.tensor_scalar_mul(out=gq[:, 0:6], in0=gq[:, 0:6], scalar1=HD ** -0.5), [gq_r], [gq_r])
        S.op("act", lambda e: e.activation(out=gq[:], in_=gq[:], func=AF.Ln), [gq_r], [gq_r])
        epsb, epsb_r = sb("epsb", [128, 1], F32, st)
        S.op("dve", lambda e: e.memset(epsb[:], EPS), [], [epsb_r])
        zt, zt_r = sb("zt", [128, 18], BF16, st)
        S.op("dve", lambda e: e.memset(zt[:], 0.0), [], [zt_r])
        xt = [sb(f"xt{i}", [128, 4, D], F32, st) for i in range(2)]
        xnb = [sb(f"xn{i}", [128, 4, D], BF16, st) for i in range(2)]
        uT = [sb(f"uT{i}", [128, 8, 512], BF16, st) for i in range(2)]
        ssb = [sb(f"ss{i}", [128, 4], F32, st) for i in range(2)]
        junk, junk_r = sb("junk", [128, D], BF16, st)
        evb = [sb(f"evb{i}", [128, 512], BF16, st) for i in range(8)]
        qf = [sb(f"qf{i}", [128, 512], F32, st) for i in range(4)]
        sq = [sb(f"sq{i}", [128, 512], BF16, st) for i in range(4)]
        rs = [sb(f"rs{i}", [128, 512], F32, st) for i in range(4)]
        tiles = [(s_, mt) for s_, L in enumerate(P.Ls) for mt in range(L // 512)]
        nt = len(tiles)
        for s_, L in enumerate(P.Ls):
            sc = P.SC[s_]
            with nc.allow_non_contiguous_dma(reason="zero pads"):
                for col in (0, L + 1):
                    S.dma("sp", sc["zhy"][:, col].rearrange("(k p) -> p k", p=128), zt[:], [zt_r], [sc["res"]["zhy"]], zt_r)

        def load(i):
            s_, mt = tiles[i]
            xt_t, xt_r = xt[i % 2]
            S.dma("sp", xt_t[:], I[f"x{s_}"][mt * 512:(mt + 1) * 512, :].rearrange("(j p) d -> p j d", p=128), [IN], [xt_r], xt_r)

        def norm(i):
            xt_t, xt_r = xt[i % 2]
            xn, xn_r = xnb[i % 2]
            ss, ss_r = ssb[i % 2]
            S.op("dve", lambda e: e.memset(ss[:], 0.0), [], [ss_r])
            for j in range(4):
                S.op("act", lambda e, j=j: e.activation(out=junk[:], in_=xt_t[:, j, :], func=AF.Square,
                                                        accum_out=ss[:, j:j + 1]), [xt_r, ss_r], [junk_r, ss_r])
            S.op("dve", lambda e: e.tensor_scalar(out=ss[:], in0=ss[:], scalar1=1.0 / D, scalar2=EPS,
                                                  op0=ALU.mult, op1=ALU.add), [ss_r], [ss_r])
            S.op("act", lambda e: e.activation(out=ss[:], in_=ss[:], func=AF.Sqrt), [ss_r], [ss_r])
            S.op("dve", lambda e: e.reciprocal(out=ss[:], in_=ss[:]), [ss_r], [ss_r])
            for j in range(4):
                S.op("act", lambda e, j=j: e.activation(out=xn[:, j, :], in_=xt_t[:, j, :], func=AF.Copy,
                                                        scale=ss[:, j:j + 1]), [xt_r, ss_r], [xn_r])

        def xpose(i):
            s_, mt = tiles[i]
            xn, xn_r = xnb[i % 2]
            uT_t, uT_r = uT[i % 2]
            for kc in range(8):
                pt, pr = psum()
                ptb = pt.bitcast(BF16)
                for j in range(4):
                    S.op("pe", lambda e, j=j, kc=kc, ptb=ptb: e.transpose(
                        out=ptb[:, j * 128:(j + 1) * 128], in_=xn[:, j, kc * 128:(kc + 1) * 128],
                        identity=P.ident_b[:]), [xn_r, P.ident_b_r], [pr], signal=(j == 3))
                S.op("dve", lambda e, kc=kc, ptb=ptb: e.tensor_scalar(
                    out=uT_t[:, kc, :], in0=ptb[:, 0:512], scalar1=P.modT[:, s_, 1, kc:kc + 1],
                    scalar2=P.modT[:, s_, 0, kc:kc + 1], op0=ALU.mult, op1=ALU.add), [pr, P.modT_r], [uT_r])

        ei = [0]

        def mm(i):
            s_, mt = tiles[i]
            sc = P.SC[s_]
            R = sc["res"]
            t0 = mt * 512
            uT_t, uT_r = uT[i % 2]
            pend = []
            for m in range(52):
                if 30 <= m < 36:
                    continue
                pt, pr = psum()
                for kc in range(8):
                    S.op("pe", lambda e, kc=kc, m=m, pt=pt: e.matmul(
                        pt[:, :], lhsT=w_sb[:, kc, m * 128:(m + 1) * 128], rhs=uT_t[:, kc, :],
                        start=(kc == 0), stop=(kc == 7)), [wr[kc], uT_r], [pr], signal=(kc == 7))
                et, er = evb[ei[0] % 8]
                ei[0] += 1
                if m < 18:
                    if m % 2 == 0:
                        S.op("act", lambda e, pt=pt, et=et: e.activation(out=et[:], in_=pt[:, :], func=AF.Copy), [pr], [er])
                        S.dma(AQ, sc["zhy"][m * 128:(m + 1) * 128, 1 + t0:1 + t0 + 512], et[:], [er], [R["zhy"]], er)
                    else:
                        S.op("dve", lambda e, pt=pt, et=et: e.tensor_copy(out=et[:], in_=pt[:, :]), [pr], [er])
                        S.dma("sp", sc["zhy"][m * 128:(m + 1) * 128, 1 + t0:1 + t0 + 512], et[:], [er], [R["zhy"]], er)
                elif m < 30:
                    idx = m - 18
                    qf_t, qf_r = qf[idx % 4]
                    sq_t, sq_r = sq[idx % 4]
                    rs_t, rs_r = rs[idx % 4]
                    S.op("dve", lambda e, pt=pt, qf_t=qf_t: e.tensor_copy(out=qf_t[:], in_=pt[:, :]), [pr], [qf_r])
                    S.op("act", lambda e, qf_t=qf_t, sq_t=sq_t: e.activation(out=sq_t[:], in_=qf_t[:], func=AF.Square), [qf_r], [sq_r])
                    ctx = {}

                    def st0(ctx=ctx, sq_t=sq_t, sq_r=sq_r):
                        ctx["pt2"], ctx["pr2"] = psum()
                        pt2 = ctx["pt2"]
                        S.op("pe", lambda e: e.matmul(pt2[:, :], lhsT=bd[:], rhs=sq_t[:], start=True, stop=True),
                             [bd_r, sq_r], [ctx["pr2"]])

                    def st1(ctx=ctx, rs_t=rs_t, rs_r=rs_r, idx=idx):
                        pt2, pr2 = ctx["pt2"], ctx["pr2"]
                        S.op("act", lambda e: e.activation(out=rs_t[:], in_=pt2[:, :], func=AF.Ln, scale=1.0 / HD, bias=epsb[:, 0:1]),
                             [pr2, epsb_r], [rs_r])
                        S.op("act", lambda e: e.activation(out=rs_t[:], in_=rs_t[:], func=AF.Exp, scale=-0.5, bias=gq[:, idx:idx + 1]),
                             [rs_r, gq_r], [rs_r])

                    def st2(rs_t=rs_t, rs_r=rs_r, qf_t=qf_t, qf_r=qf_r, et=et, er=er, idx=idx, t0=t0):
                        S.op("pool", lambda e: e.tensor_tensor(out=et[:], in0=qf_t[:], in1=rs_t[:], op=ALU.mult), [rs_r, qf_r], [er])
                        dst = sc["qT"] if idx < 6 else sc["kT"]
                        dr = R["qT"] if idx < 6 else R["kT"]
                        S.dma("sp", dst[(idx % 6) * 128:(idx % 6 + 1) * 128, t0:t0 + 512], et[:], [er], [dr], er)
                    pend.append([st0, st1, st2])
                    if len(pend) > 1:
                        pend[-2][0]()
                    if len(pend) > 2:
                        pend[-3][1]()
                    if len(pend) > 3:
                        pend[-4][2]()
                    if idx == 11:
                        pend[-1][0]()
                        pend[-2][1]()
                        pend[-3][2]()
                        pend[-1][1]()
                        pend[-2][2]()
                        pend[-1][2]()
                        del pend[:]
                else:
                    S.op("act", lambda e, pt=pt, et=et: e.activation(out=et[:], in_=pt[:, :], func=AF.Sigmoid), [pr], [er])
                    S.dma(AQ, sc["gT"][(m - 36) * 128:(m - 35) * 128, t0:t0 + 512], et[:], [er], [R["gT"]], er)
            for jb in range(4):
                for c0, cw_ in ((3840, 512), (4352, 256)):
                    pt, pr = psum()
                    for kc in range(8):
                        S.op("pe", lambda e, kc=kc, jb=jb, c0=c0, cw_=cw_, pt=pt: e.matmul(
                            pt[:, 0:cw_], lhsT=uT_t[:, kc, jb * 128:(jb + 1) * 128], rhs=w_sb[:, kc, c0:c0 + cw_],
                            start=(kc == 0), stop=(kc == 7)), [wr[kc], uT_r], [pr], signal=(kc == 7))
                    et, er = evb[ei[0] % 8]
                    ei[0] += 1
                    S.op("dve", lambda e, pt=pt, et=et, cw_=cw_: e.tensor_copy(out=et[:, 0:cw_], in_=pt[:, 0:cw_]), [pr], [er])
                    S.dma("sp", sc["v"][t0 + jb * 128:t0 + (jb + 1) * 128, c0 - 3840:c0 - 3840 + cw_], et[:, 0:cw_],
                          [er], [R["v"]], er)

        load(0)
        if nt > 1:
            load(1)
        norm(0)
        xpose(0)
        for i in range(nt):
            if i + 1 < nt:
                norm(i + 1)
            if i + 2 < nt:
                load(i + 2)
            mm(i)
            if i + 1 < nt:
                xpose(i + 1)
        S.stage_end()


def stage_attn(P):
    nc, S, I, IN, sb, psum = P.nc, P.S, P.I, P.IN, P.sb, P.psum
    PAD = 1024
    with contextlib.ExitStack() as st:
        biasmat, bm_r = sb("biasmat", [128, 24, 128], F32, st)
        with contextlib.ExitStack() as st2:
            rbx, rbx_r = sb("rbx", [33, NH], F32, st2)
            anti, anti_r = sb("anti", [128, 128], F32, st2)
            oh = [sb(f"oh{g}", [33, 512], F32, st2) for g in range(3)]
            gv = [sb(f"gv{g}", [4, 512], F32, st2) for g in range(3)]
            hk = [sb(f"hk{i}", [128, 128], F32, st2) for i in range(4)]
            S.op("dve", lambda e: e.memset(rbx[:], 1.0), [], [rbx_r])
            S.dma("sp", rbx[0:32, :], I["rel_bias"][:, :], [IN], [rbx_r], rbx_r)
            S.dma("sp", anti[:], I["antiid"][:, :], [IN], [anti_r], anti_r)
            for g in range(3):
                S.dma("sp", oh[g][0][:], I["bias_oh"][g, :, :], [IN], [oh[g][1]], oh[g][1])
                pt, pr = psum()
                S.op("pe", lambda e, g=g, pt=pt: e.matmul(pt[0:4, :], lhsT=rbx[:, 4 * g:4 * g + 4], rhs=oh[g][0][:],
                                                          start=True, stop=True), [rbx_r, oh[g][1]], [pr])
                S.op("dve", lambda e, g=g, pt=pt: e.tensor_copy(out=gv[g][0][:], in_=pt[0:4, :]), [pr], [gv[g][1]])
                S.dma("sp", P.gvd[4 * g:4 * g + 4, :], gv[g][0][:], [gv[g][1]], [P.gvd_r], gv[g][1])
            for h in range(NH):
                for ab in range(2):
                    i = h * 2 + ab
                    ht, hr = hk[i % 4]
                    S.dma("sp", ht[:], bass.AP(P.gvd, h * 512 + ab * 256, [[1, 128], [1, 128]]), [P.gvd_r], [hr], hr)
                    pt, pr = psum()
                    S.op("pe", lambda e, pt=pt, ht=ht: e.matmul(pt[:, 0:128], lhsT=anti[:], rhs=ht[:], start=True, stop=True),
                         [anti_r, hr], [pr])
                    S.op("dve", lambda e, pt=pt, i=i: e.tensor_copy(out=biasmat[:, i, :], in_=pt[:, 0:128]), [pr], [bm_r])
            P.pe_flush()
            S.barrier()
        Lmax = max(P.Ls)
        OQ = "act"
        qz = [[sb(f"qz{i}{h}", [128, Lmax], BF16, st) for h in range(2)] for i in range(2)]
        for i in range(2):
            S.op("pool", lambda e, i=i: e.memset(qz[i][0][0][64:128, :], 0.0), [], [qz[i][0][1]])
            S.op("pool", lambda e, i=i: e.memset(qz[i][1][0][0:64, :], 0.0), [], [qz[i][1][1]])
        ks = [sb(f"ks{i}", [128, Lmax + 2 * PAD], BF16, st) for i in range(2)]
        vfull = [sb(f"vf{i}", [128, 4, 128], BF16, st) for i in range(3)]
        vfirst, vfirst_r = sb("vfirst", [128, 4, 128], BF16, st)
        vlast, vlast_r = sb("vlast", [128, 4, 128], BF16, st)
        scs = [sb(f"scs{i}", [128, 8, 128], F32, st) for i in range(2)]
        pT = [sb(f"pT{i}", [128, 8, 128], BF16, st) for i in range(2)]
        osb = [sb(f"osb{i}", [128, 260], F32, st) for i in range(3)]
        for t_, r_ in vfull:
            S.op("dve", lambda e, t_=t_: e.memset(t_[:], 1.0), [], [r_])
        S.op("dve", lambda e: e.memset(vfirst[:], 0.0), [], [vfirst_r])
        S.op("dve", lambda e: e.memset(vlast[:], 0.0), [], [vlast_r])
        S.op("dve", lambda e: e.memset(vfirst[64:128, :, 64:65], 1.0), [], [vfirst_r])
        S.op("dve", lambda e: e.memset(vlast[0:64, :, 64:65], 1.0), [], [vlast_r])
        for t_, r_ in ks:
            S.op("pool", lambda e, t_=t_: e.memset(t_[:], 0.0), [], [r_])
        it = 0
        for s, L in enumerate(P.Ls):
            sc = P.SC[s]
            R = sc["res"]
            for g, d in enumerate((1, 4, 16)):
                Ssub = L // d
                nblk = Ssub // 128
                for pp in range(2):
                    r0 = (4 * g + 2 * pp) * 64
                    S.dma("sp", qz[pp][0][0][0:64, 0:L], sc["qT"][r0:r0 + 64, :], [R["qT"]], [qz[pp][0][1]], qz[pp][0][1])
                    S.dma("sp", qz[pp][1][0][64:128, 0:L], sc["qT"][r0 + 64:r0 + 128, :], [R["qT"]], [qz[pp][1][1]], qz[pp][1][1])
                    S.dma("sp", ks[pp][0][:, PAD:PAD + L], sc["kT"][r0:r0 + 128, :], [R["kT"]], [ks[pp][1]], ks[pp][1])
                vview = sc["v"].rearrange("(s d) c -> d s c", d=d)
                oview = sc["od"][g].rearrange("(s d) c -> d s c", d=d)
                for r in range(d):
                    def vload(m):
                        if m == 0:
                            t_, r_ = vfirst, vfirst_r
                            S.dma("sp", t_[64:128, :, 0:64],
                                  vview[r, 0:64, g * 256:(g + 1) * 256].rearrange("s (h e) -> s h e", e=64),
                                  [R["v"]], [r_], r_)
                        elif m == nblk:
                            t_, r_ = vlast, vlast_r
                            S.dma("sp", t_[0:64, :, 0:64],
                                  vview[r, Ssub - 64:Ssub, g * 256:(g + 1) * 256].rearrange("s (h e) -> s h e", e=64),
                                  [R["v"]], [r_], r_)
                        else:
                            t_, r_ = vfull[m % 3]
                            S.dma("sp", t_[:, :, 0:64],
                                  vview[r, 128 * m - 64:128 * m + 64, g * 256:(g + 1) * 256].rearrange("s (h e) -> s h e", e=64),
                                  [R["v"]], [r_], r_)
                        return t_, r_
                    vt = {0: vload(0)}

                    def phase1(j):
                        nonlocal it
                        vt[j + 1] = vload(j + 1)
                        sc_t, sc_r = scs[it % 2]
                        p_t, p_r = pT[it % 2]
                        o_t, o_r = osb[it % 3]
                        it += 1
                        banks = [psum(), psum()]
                        for hh in range(4):
                            pp = hh // 2
                            qa = qz[pp][hh % 2][0][:, 128 * j * d + r:128 * j * d + r + 127 * d + 1:d]
                            for ab in range(2):
                                m = j + ab
                                k0 = PAD + (128 * m - 64) * d + r
                                ka = ks[pp][0][:, k0:k0 + 127 * d + 1:d]
                                idx = hh * 2 + ab
                                pt, pr = banks[idx // 4]
                                S.op("pe", lambda e, pt=pt, idx=idx, ka=ka, qa=qa: e.matmul(
                                    pt[:, (idx % 4) * 128:(idx % 4 + 1) * 128], lhsT=ka, rhs=qa, start=True, stop=True),
                                    [qz[pp][hh % 2][1], ks[pp][1]], [pr], signal=(idx % 4 == 3))
                        for b_ in range(2):
                            pt, pr = banks[b_]
                            S.op("dve", lambda e, pt=pt, b_=b_, sc_t=sc_t: e.tensor_tensor(
                                out=sc_t[:, 4 * b_:4 * b_ + 4, :], in0=pt[:, :].rearrange("p (a q) -> p a q", q=128),
                                in1=biasmat[:, 8 * g + 4 * b_:8 * g + 4 * b_ + 4, :], op=ALU.add), [pr, bm_r], [sc_r])
                        S.op("act", lambda e, sc_t=sc_t, p_t=p_t: e.activation(out=p_t[:], in_=sc_t[:], func=AF.Exp), [sc_r], [p_r])
                        return (j, p_t, p_r, o_t, o_r, vt[j], vt[j + 1], it)

                    def phase2(st_):
                        j, p_t, p_r, o_t, o_r, va, vb, itn = st_
                        po, por = psum()
                        for hh in range(4):
                            for ab, (vt_t, vt_r) in enumerate((va, vb)):
                                S.op("pe", lambda e, hh=hh, ab=ab, vt_t=vt_t, p_t=p_t, po=po: e.matmul(
                                    po[:, hh * 128:hh * 128 + 65], lhsT=p_t[:, hh * 2 + ab, :], rhs=vt_t[:, hh, 0:65],
                                    start=(ab == 0), stop=(ab == 1)), [p_r, vt_r], [por], signal=(hh == 3 and ab == 1))
                        if itn % 2:
                            S.op("dve", lambda e, po=po, o_t=o_t: e.tensor_copy(
                                out=o_t[:].rearrange("p (h c) -> p h c", h=4),
                                in_=po[:, :].rearrange("p (h c) -> p h c", h=4)[:, :, 0:65]), [por], [o_r])
                        else:
                            S.op("act", lambda e, po=po, o_t=o_t: e.activation(
                                out=o_t[:].rearrange("p (h c) -> p h c", h=4),
                                in_=po[:, :].rearrange("p (h c) -> p h c", h=4)[:, :, 0:65], func=AF.Copy), [por], [o_r])
                        S.dma(OQ, oview[r, 128 * j:128 * (j + 1), :], o_t[:], [o_r], [R["od"]], o_r)

                    prev = None
                    for j in range(nblk):
                        cur = phase1(j)
                        if prev is not None:
                            phase2(prev)
                        prev = cur
                        vt.pop(j - 1, None)
                    phase2(prev)
        S.stage_end()


def stage_hyena(P):
    nc, S, I, IN, sb, psum = P.nc, P.S, P.I, P.IN, P.sb, P.psum
    PI = math.pi
    for s, L in enumerate(P.Ls):
        sc = P.SC[s]
        R = sc["res"]
        N = 2 * L
        N1 = N // 128
        H1 = N1 // 2
        ncg = L // 512
        h3d = nc.dram_tensor(f"h3d{s}", [2, 64, L], BF16, kind="Internal")
        h3d_r = Res(f"h3d{s}", multi=True)
        with contextlib.ExitStack() as st:
            w1, w1_r = sb("fw1", [33, 64], F32, st)
            w2, w2_r = sb("fw2", [64, 64], F32, st)
            w3, w3_r = sb("fw3", [64, 64], F32, st)
            fb, fb_r = sb("fb", [64, 4], F32, st)
            S.dma("sp", w1[:], I["filt_w1"][:, :], [IN], [w1_r], w1_r)
            S.dma("sp", w2[:], I["filt_w2"][:, :], [IN], [w2_r], w2_r)
            S.dma("sp", w3[:], I["filt_w3"][:, :], [IN], [w3_r], w3_r)
            with nc.allow_non_contiguous_dma(reason="tiny"):
                for i, nm in enumerate(("filt_b1", "filt_b2", "filt_b3", "filt_freq")):
                    S.dma("sp", fb[:, i:i + 1], I[nm][0, :].rearrange("(p o) -> p o", o=1), [IN], [fb_r], fb_r)
            G = 4
            ft = [sb(f"ft{i}", [33, 512], F32, st) for i in range(G)]
            ha = [[sb(f"ha{i}_{l}", [64, 512], F32, st) for l in range(3)] for i in range(G)]
            kts = [sb(f"kt{i}", [64, 512], F32, st) for i in range(G)]
            hbf = [sb(f"hbf{i}", [64, 512], BF16, st) for i in range(G)]
            items = [(dr, cg) for dr in range(2) for cg in range(ncg)]
            for b0 in range(0, len(items), G):
                batch = items[b0:b0 + G]
                cur = []
                for gi_, (dr, cg) in enumerate(batch):
                    ft_t, ft_r = ft[gi_]
                    S.dma("sp", ft_t[:], I[f"feats{L}"][dr, :, cg * 512:(cg + 1) * 512], [IN], [ft_r], ft_r)
                    cur.append((ft_t, ft_r, 33))
                for li, (w_, w_r) in enumerate(((w1, w1_r), (w2, w2_r), (w3, w3_r))):
                    nxt = []
                    for gi_ in range(len(batch)):
                        c_t, c_r, kdim = cur[gi_]
                        h_t, h_r = ha[gi_][li]
                        kt, kt_r = kts[gi_]
                        pt, pr = psum()
                        S.op("pe", lambda e, pt=pt, w_=w_, c_t=c_t, kdim=kdim: e.matmul(
                            pt[0:64, :], lhsT=w_[0:kdim, :], rhs=c_t[0:kdim, :], start=True, stop=True), [w_r, c_r], [pr])
                        S.op("dve", lambda e, pt=pt, h_t=h_t, li=li: e.tensor_scalar(
                            out=h_t[:], in0=pt[0:64, :], scalar1=fb[:, li:li + 1], scalar2=fb[:, 3:4],
                            op0=ALU.add, op1=ALU.mult), [pr, fb_r], [h_r])
                        S.op("dve", lambda e, h_t=h_t, kt=kt: e.tensor_scalar(
                            out=kt[:], in0=h_t[:], scalar1=1.0 / TWO_PI, scalar2=MAGIC, op0=ALU.mult, op1=ALU.add), [h_r], [kt_r])
                        S.op("dve", lambda e, kt=kt: e.tensor_scalar_add(out=kt[:], in0=kt[:], scalar1=-MAGIC), [kt_r], [kt_r])
                        S.op("dve", lambda e, h_t=h_t, kt=kt: e.scalar_tensor_tensor(
                            out=h_t[:], in0=kt[:], scalar=-TWO_PI, in1=h_t[:], op0=ALU.mult, op1=ALU.add), [kt_r, h_r], [h_r])
                        S.op("dve", lambda e, h_t=h_t: e.tensor_scalar(
                            out=h_t[:], in0=h_t[:], scalar1=-3.1415925, scalar2=3.1415925, op0=ALU.max, op1=ALU.min), [h_r], [h_r])
                        if li == 2:
                            o_t, o_r = hbf[gi_]
                            S.op("act", lambda e, h_t=h_t, o_t=o_t: e.activation(out=o_t[:], in_=h_t[:], func=AF.Sin), [h_r], [o_r])
                            nxt.append((o_t, o_r, 64))
                        else:
                            S.op("act", lambda e, h_t=h_t: e.activation(out=h_t[:], in_=h_t[:], func=AF.Sin), [h_r], [h_r])
                            nxt.append((h_t, h_r, 64))
                    cur = nxt
                for gi_, (dr, cg) in enumerate(batch):
                    c_t, c_r, _ = cur[gi_]
                    S.dma("act", h3d[dr, :, cg * 512:(cg + 1) * 512], c_t[:], [c_r], [h3d_r], c_r)
            P.pe_flush()
            S.stage_end()
        with contextlib.ExitStack() as st:
            wo, wo_r = sb("fwo", [64, 2 * DH], BF16, st)
            nd, nd_r = sb("nd", [128, 6], F32, st)
            S.dma("pool", wo[:], I["filt_w_out"][:, :], [IN], [wo_r], wo_r)
            S.dma("sp", nd[:], I["ndelta"][:, :], [IN], [nd_r], nd_r)
            kk, kk_r = sb("kk", [128, 2, L], F32, st)
            k2b, k2b_r = sb("k2b", [128, 2 * L], BF16, st)
            asum, asum_r = sb("asum", [128, 1], F32, st)
            h3 = [sb(f"h3_{i}", [64, 512], BF16, st) for i in range(4)]
            tv = [sb(f"tv{i}", [128, 512], F32, st) for i in range(4)]
            it = 0
            for cc in range(6):
                for dr in range(2):
                    for cg in range(ncg):
                        h_t, h_r = h3[it % 4]
                        t_t, t_r = tv[it % 4]
                        it += 1
                        S.dma("sp", h_t[:], h3d[dr, :, cg * 512:(cg + 1) * 512], [h3d_r], [h_r], h_r)
                        S.dma("sp", t_t[:], I[f"tvec{L}"][dr:dr + 1, cg * 512:(cg + 1) * 512].partition_broadcast(128),
                              [IN], [t_r], t_r)
                        pt, pr = psum()
                        S.op("pe", lambda e, pt=pt, h_t=h_t, dr=dr, cc=cc: e.matmul(
                            pt[:, :], lhsT=wo[:, dr * DH + cc * 128:dr * DH + (cc + 1) * 128], rhs=h_t[:],
                            start=True, stop=True), [wo_r, h_r], [pr])
                        S.op("act", lambda e, t_t=t_t, cc=cc: e.activation(out=t_t[:], in_=t_t[:], func=AF.Exp,
                                                                            scale=nd[:, cc:cc + 1]), [t_r, nd_r], [t_r])
                        S.op("dve", lambda e, pt=pt, t_t=t_t, dr=dr, cg=cg: e.tensor_tensor(
                            out=kk[:, dr, cg * 512:(cg + 1) * 512], in0=pt[:, :], in1=t_t[:], op=ALU.mult), [pr, t_r], [kk_r])
                S.op("dve", lambda e: e.memset(kk[:, 1, 0:1], 0.0), [], [kk_r])
                S.op("dve", lambda e: e.memset(asum[:], 0.0), [], [asum_r])
                S.op("act", lambda e: e.activation(out=k2b[:], in_=kk[:].rearrange("p a l -> p (a l)"), func=AF.Abs,
                                                   accum_out=asum[:]), [kk_r, asum_r], [k2b_r, asum_r])
                S.op("dve", lambda e: e.reciprocal(out=asum[:], in_=asum[:]), [asum_r], [asum_r])
                S.op("act", lambda e: e.activation(out=k2b[:], in_=kk[:].rearrange("p a l -> p (a l)"), func=AF.Copy,
                                                   scale=asum[:, 0:1]), [kk_r, asum_r], [k2b_r])
                S.dma("act", sc["k2"][cc * 128:(cc + 1) * 128, :], k2b[:], [k2b_r], [R["k2"]], k2b_r)
            P.pe_flush()
            S.stage_end()
        with contextlib.ExitStack() as st:
            cw, cw_r = sb("cw", [128, 3, 18], F32, st)
            cb, cb_r = sb("cb", [128, 18], F32, st)
            hd, hd_r = sb("hd", [128, 6], F32, st)
            with nc.allow_non_contiguous_dma(reason="tiny"):
                for k in range(3):
                    S.dma("sp", cw[:, k, :], I["conv_w"][k, :].rearrange("(c p) -> p c", p=128), [IN], [cw_r], cw_r)
                S.dma("sp", cb[:], I["conv_b"][0, :].rearrange("(c p) -> p c", p=128), [IN], [cb_r], cb_r)
                S.dma("sp", hd[:], I["hyena_d"][0, :].rearrange("(c p) -> p c", p=128), [IN], [hd_r], hd_r)
            f1t, f1t_r = sb("f1t", [N1, 2 * N1], BF16, st)
            i1t, i1t_r = sb("i1t", [128, 2, 256], BF16, st)
            S.dma("sp", f1t[:], I[f"f1tab{L}"][:, :], [IN], [f1t_r], f1t_r)
            S.dma("sp", i1t[:], I[f"i1tab{L}"][:, :, :], [IN], [i1t_r], i1t_r)
            xin, xin_r = sb("xin", [128, 128, 128], BF16, st)
            B1, B1_r = sb("B1", [128, 32768], BF16, st)
            B2, B2_r = sb("B2", [128, 2 * N1 * 128], BF16, st)
            KH = N1 // 2 + 2
            B1_r.multi = True
            B2_r.multi = True
            dsA, dsB, dsC, dsD = Res("dsA"), Res("dsB"), Res("dsC"), Res("dsD")
            B1f = B1.bitcast(F32)
            B2f = B2.bitcast(F32)
            gts = [sb(f"gts{i}", [128, 2, 3, 128], BF16, st) for i in range(4)]
            kfs = [sb(f"kfs{i}", [128, 512], F32, st) for i in range(4)]
            i2s = [sb(f"i2s{i}", [N1, 512 // H1 if H1 * 128 > 512 else 128, 2, H1], BF16, st) for i in range(2)]
            tmp = [sb(f"ctmp{i}", [128, 2, 128], F32, st) for i in range(8)]
            xsb = [sb(f"xsb{i}", [128, 512], F32, st) for i in range(2)]
            tg = min(128, 512 // H1)
            BT = B1[:, 0:2 * N1 * 128].rearrange("p (r k c) -> p r k c", r=2, k=N1)
            Zb = B1[:, :].rearrange("p (r t c) -> p r t c", r=2, t=128)
            Yb = B2[:, :].rearrange("p (r c k) -> p r c k", r=2, c=128)
            yv = B2f[:, 0:L].rearrange("p (a b) -> p a b", b=128)
            gi = 0

            def f1_pass(K):
                nonlocal gi
                for c0 in range(0, 128, 2):
                    pt, pr = psum()
                    for u in range(2):
                        S.op("pe", lambda e, pt=pt, u=u, c0=c0: e.matmul(
                            pt[:, u * 2 * KH:(u + 1) * 2 * KH].rearrange("p (r k) -> p r k", r=2), lhsT=xin[0:K, c0 + u, :],
                            rhs=f1t[0:K, :].rearrange("p (r k) -> p r k", r=2)[:, :, 0:KH],
                            start=True, stop=True), [xin_r, f1t_r], [pr], signal=(u == 1))
                    src = pt[:, 0:4 * KH].rearrange("p (u r k) -> p r k u", u=2, r=2)
                    gi += 1
                    if gi % 2:
                        S.op("act", lambda e, src=src, c0=c0: e.activation(out=BT[:, :, 0:KH, c0:c0 + 2], in_=src, func=AF.Copy),
                             [pr], [B1_r])
                    else:
                        S.op("dve", lambda e, src=src, c0=c0: e.tensor_copy(out=BT[:, :, 0:KH, c0:c0 + 2], in_=src), [pr], [B1_r])

            def f3_pass(cc, is_filter):
                for q in range(N1 // 4 + 1):
                    g_t, g_r = gts[q % 4]
                    S.dma("sp", g_t[:], I[f"gtab{L}"][:, 2 * q:2 * q + 2, :, :], [IN], [g_r], g_r)
                    pt, pr = psum()
                    pv = pt[:, :].rearrange("p (u r c) -> p u r c", u=2, r=2)
                    for u in range(2):
                        k1 = 2 * q + u
                        for ri, (ga, gb) in enumerate(((0, 2), (1, 0))):
                            S.op("pe", lambda e, pv=pv, u=u, ri=ri, ga=ga, k1=k1, g_t=g_t: e.matmul(
                                pv[:, u, ri, :], lhsT=g_t[:, u, ga, :], rhs=BT[:, 0, k1, :], start=True, stop=False),
                                [g_r, B1_r], [pr], signal=False)
                            S.op("pe", lambda e, pv=pv, u=u, ri=ri, gb=gb, k1=k1, g_t=g_t: e.matmul(
                                pv[:, u, ri, :], lhsT=g_t[:, u, gb, :], rhs=BT[:, 1, k1, :], start=False, stop=True),
                                [g_r, B1_r], [pr], signal=(u == 1 and ri == 1))
                    k_t, k_r = kfs[q % 4]
                    if is_filter:
                        S.op("act", lambda e, pt=pt, k_t=k_t: e.activation(out=k_t[:], in_=pt[:, :], func=AF.Copy), [pr], [k_r])
                        S.dma("act", sc["kf"][cc, q, :, :], k_t[:], [k_r], [R["kf"]], k_r)
                    else:
                        S.dma("sp", k_t[:], sc["kf"][cc, q, :, :], [R["kf"]], [k_r], k_r)
                        kv = k_t[:, :].rearrange("p (u r c) -> p u r c", u=2, r=2)
                        x_t, x_r = xsb[q % 2]
                        S.op("act", lambda e, pt=pt, x_t=x_t: e.activation(out=x_t[:], in_=pt[:, :], func=AF.Copy), [pr], [x_r])
                        xv = x_t[:, :].rearrange("p (u r c) -> p u r c", u=2, r=2)
                        tr = [tmp[(q % 2) * 4 + i] for i in range(4)]
                        for i, (xa, ka, eng) in enumerate(((0, 1, "pool"), (1, 0, "pool"), (0, 0, "dve"), (1, 1, "dve"))):
                            S.op(eng, lambda e, i=i, xa=xa, ka=ka, xv=xv, kv=kv, tr=tr: e.tensor_tensor(
                                out=tr[i][0][:], in0=xv[:, :, xa, :], in1=kv[:, :, ka, :], op=ALU.mult), [x_r, k_r], [tr[i][1]])
                        S.op("dve", lambda e, q=q, tr=tr: e.tensor_tensor(
                            out=Yb[:, 0, :, 2 * q:2 * q + 2], in0=tr[2][0][:].rearrange("p u c -> p c u"),
                            in1=tr[3][0][:].rearrange("p u c -> p c u"), op=ALU.subtract), [tr[2][1], tr[3][1]], [B2_r])
                        S.op("dve", lambda e, q=q, tr=tr: e.tensor_tensor(
                            out=Yb[:, 1, :, 2 * q:2 * q + 2], in0=tr[0][0][:].rearrange("p u c -> p c u"),
                            in1=tr[1][0][:].rearrange("p u c -> p c u"), op=ALU.add), [tr[0][1], tr[1][1]], [B2_r])

            for cc in range(6):
                T1 = B1[:, 0:L + 2]
                T2 = B1[:, L + 2:2 * L + 4]
                AO = B1[:, 2 * L + 4:3 * L + 4]
                x1c = B2f[:, 0:L]
                vc = B2f[:, L:2 * L]
                S.dma("sp", T1, sc["zhy"][DH + cc * 128:DH + (cc + 1) * 128, :], [R["zhy"]], [B1_r], dsA)
                S.dma("sp", T2, sc["zhy"][2 * DH + cc * 128:2 * DH + (cc + 1) * 128, :], [R["zhy"]], [B1_r], dsB)
                for src, dst, ch in ((T1, x1c, 6 + cc), (T2, vc, 12 + cc)):
                    S.op("act", lambda e, src=src, dst=dst, ch=ch: e.activation(
                        out=dst, in_=src[:, 1:L + 1], func=AF.Identity, scale=cw[:, 1, ch:ch + 1], bias=cb[:, ch:ch + 1]),
                        [B1_r, cw_r, cb_r], [B2_r])
                    for k in (0, 2):
                        S.op("dve", lambda e, src=src, dst=dst, ch=ch, k=k: e.scalar_tensor_tensor(
                            out=dst, in0=src[:, k:L + k], scalar=cw[:, k, ch:ch + 1], in1=dst, op0=ALU.mult, op1=ALU.add),
                            [B1_r, B2_r, cw_r], [B2_r])
                S.op("dve", lambda e: e.tensor_tensor(out=AO, in0=x1c, in1=vc, op=ALU.mult), [B2_r], [B1_r])
                S.dma("pool", sc["aT"][cc * 128:(cc + 1) * 128, :], AO, [B1_r], [R["aT"]], dsC)
                if cc == 0:
                    S.dma("sp", xin[0:N1, :, :], sc["k2"][0:128, :].rearrange("c (a b) -> a c b", b=128),
                          [R["k2"]], [xin_r], xin_r)
                f1_pass(N1)
                S.dma("sp", xin[0:H1, :, :], sc["aT"][cc * 128:(cc + 1) * 128, :].rearrange("c (a b) -> a c b", b=128),
                      [R["aT"]], [xin_r], xin_r)
                f3_pass(cc, True)
                f1_pass(H1)
                if cc + 1 < 6:
                    S.dma("act", xin[0:N1, :, :], sc["k2"][(cc + 1) * 128:(cc + 2) * 128, :].rearrange("c (a b) -> a c b", b=128),
                          [R["k2"]], [xin_r], xin_r)
                f3_pass(cc, False)
                for c0 in range(0, 128, 2):
                    pt, pr = psum()
                    for u in range(2):
                        for ri in range(2):
                            S.op("pe", lambda e, pt=pt, u=u, ri=ri, c0=c0: e.matmul(
                                pt[0:KH, u * 256:(u + 1) * 256], lhsT=Yb[:, ri, c0 + u, 0:KH], rhs=i1t[:, ri, :],
                                start=(ri == 0), stop=(ri == 1)), [B2_r, i1t_r], [pr], signal=(u == 1 and ri == 1))
                    src = pt[0:KH, :].rearrange("p (u r t) -> p r t u", u=2, r=2)
                    gi += 1
                    if gi % 2:
                        S.op("act", lambda e, src=src, c0=c0: e.activation(out=Zb[0:KH, :, :, c0:c0 + 2], in_=src, func=AF.Copy),
                             [pr], [B1_r])
                    else:
                        S.op("dve", lambda e, src=src, c0=c0: e.tensor_copy(out=Zb[0:KH, :, :, c0:c0 + 2], in_=src), [pr], [B1_r])
                for t2g in range(128 // tg):
                    i_t, i_r = i2s[t2g % 2]
                    S.dma("sp", i_t[:, 0:tg, :, :], I[f"i2tab{L}"][:, t2g * tg:(t2g + 1) * tg, :, :], [IN], [i_r], i_r)
                    pt, pr = psum()
                    pv = pt[:, 0:H1 * tg].rearrange("p (a w) -> p a w", w=tg)
                    for w in range(tg):
                        t2 = t2g * tg + w
                        for ri in range(2):
                            S.op("pe", lambda e, pv=pv, w=w, t2=t2, ri=ri, i_t=i_t: e.matmul(
                                pv[:, :, w], lhsT=Zb[0:KH, ri, t2, :], rhs=i_t[0:KH, w, ri, :], start=(ri == 0), stop=(ri == 1)),
                                [B1_r, i_r], [pr], signal=(w == tg - 1 and ri == 1))
                    gi += 1
                    if gi % 2:
                        S.op("act", lambda e, pv=pv, t2g=t2g: e.activation(out=yv[:, :, t2g * tg:(t2g + 1) * tg], in_=pv, func=AF.Copy),
                             [pr], [B2_r])
                    else:
                        S.op("dve", lambda e, pv=pv, t2g=t2g: e.tensor_copy(out=yv[:, :, t2g * tg:(t2g + 1) * tg], in_=pv), [pr], [B2_r])
                E1 = B1[:, 0:L]
                E2 = B1[:, L:2 * L + 2]
                E4 = B1[:, 2 * L + 2:3 * L + 2]
                ysb = B2f[:, 0:L]
                x0c = B2f[:, L:2 * L]
                S.dma("sp", E1, sc["aT"][cc * 128:(cc + 1) * 128, :], [R["aT"]], [B1_r], dsA)
                S.dma("sp", E2, sc["zhy"][cc * 128:(cc + 1) * 128, :], [R["zhy"]], [B1_r], dsB)
                S.op("act", lambda e, cc=cc: e.activation(out=x0c, in_=E2[:, 1:L + 1], func=AF.Identity,
                                                          scale=cw[:, 1, cc:cc + 1], bias=cb[:, cc:cc + 1]),
                     [B1_r, cw_r, cb_r], [B2_r])
                for k in (0, 2):
                    S.op("dve", lambda e, cc=cc, k=k: e.scalar_tensor_tensor(
                        out=x0c, in0=E2[:, k:L + k], scalar=cw[:, k, cc:cc + 1], in1=x0c, op0=ALU.mult, op1=ALU.add),
                        [B1_r, B2_r, cw_r], [B2_r])
                S.op("dve", lambda e, cc=cc: e.scalar_tensor_tensor(
                    out=ysb, in0=E1, scalar=hd[:, cc:cc + 1], in1=ysb, op0=ALU.mult, op1=ALU.add), [B1_r, B2_r, hd_r], [B2_r])
                S.op("dve", lambda e: e.tensor_tensor(out=E4, in0=ysb, in1=x0c, op=ALU.mult), [B2_r], [B1_r])
                S.dma("pool", sc["yhy"][cc * 128:(cc + 1) * 128, :], E4, [B1_r], [R["yhy"]], dsD)
            S.stage_end()


def stage_D1(P):
    nc, S, I, IN, sb, psum = P.nc, P.S, P.I, P.IN, P.sb, P.psum
    with contextlib.ExitStack() as st:
        whb, _ = sb("whb", [128, 6, D], BF16, st)
        wab, _ = sb("wab", [128, 2, D], BF16, st)
        wo, _ = sb("wo", [128, 8, D], BF16, st)
        whb_r, wab_r, wo_r = Res("whb"), Res("wab"), Res("wo")
        S.dma("pool", whb[:], I["w_hy_br"].rearrange("(k p) n -> p k n", p=128), [IN], [whb_r], whb_r)
        S.dma("pool", wab[:], I["w_at_br"].rearrange("(k p) n -> p k n", p=128), [IN], [wab_r], wab_r)
        S.dma("pool", wo[:], I["w_out"].rearrange("(k p) n -> p k n", p=128), [IN], [wo_r], wo_r)
        gtb1 = [sb(f"gt1_{s_}", [128, D], F32, st) for s_ in range(len(P.Ls))]
        for s_ in range(len(P.Ls)):
            S.dma("sp", gtb1[s_][0][:], P.gtbd[:, (s_ * 2) * D:(s_ * 2 + 1) * D], [P.gtbd_r], [gtb1[s_][1]], gtb1[s_][1])
        yh = [sb(f"yh{i}", [128, 6, 512], BF16, st) for i in range(2)]
        gg = [sb(f"gg{i}", [128, 16, 512], BF16, st) for i in range(2)]
        odt = [sb(f"odt{i}", [128, 4, 3, 260], F32, st) for i in range(2)]
        osums = [sb(f"osum{j}", [128, 4, 65], F32, st) for j in range(4)]
        rdens = [sb(f"rden{j}", [128, 4], F32, st) for j in range(4)]
        yatb = [sb(f"yat{i}", [128, 4, 256], BF16, st) for i in range(2)]
        yatTb = [sb(f"yatT{i}", [128, 2, 512], BF16, st) for i in range(2)]
        mix, mix_r = sb("mix", [128, 8, 512], BF16, st)
        xt = [sb(f"xd{i}", [128, 4, D], F32, st) for i in range(2)]
        tm = [sb(f"tm{i}", [128, 512], F32, st) for i in range(6)]
        ti = [0]
        tiles = [(s_, mt) for s_, L in enumerate(P.Ls) for mt in range(L // 512)]
        nt = len(tiles)

        def load(i):
            s_, mt = tiles[i]
            sc = P.SC[s_]
            R = sc["res"]
            t0 = mt * 512
            S.dma("sp", yh[i % 2][0][:], sc["yhy"][:, t0:t0 + 512].rearrange("(k p) t -> p k t", p=128), [R["yhy"]],
                  [yh[i % 2][1]], yh[i % 2][1])
            S.dma("sp", gg[i % 2][0][:], sc["gT"][:, t0:t0 + 512].rearrange("(k p) t -> p k t", p=128), [R["gT"]],
                  [gg[i % 2][1]], gg[i % 2][1])
            S.dma("sp", xt[i % 2][0][:], I[f"x{s_}"][t0:t0 + 512, :].rearrange("(j p) d -> p j d", p=128), [IN],
                  [xt[i % 2][1]], xt[i % 2][1])
            for jb in range(4):
                S.dma("act", odt[i % 2][0][:, jb, :, :],
                      sc["od"][:, t0 + jb * 128:t0 + (jb + 1) * 128, :].rearrange("g t c -> t g c"),
                      [R["od"]], [odt[i % 2][1]], Res(f"odsem{i % 2}{jb}") if False else odsem[i % 2][jb])

        odsem = [[Res(f"odsem{a_}{b_}") for b_ in range(4)] for a_ in range(2)]

        def merge(i):
            o_t, o_r = odt[i % 2]
            yat, yat_r = yatb[i % 2]
            ovs = [osums[jb][0][:].rearrange("p h c -> p (h c)") for jb in range(4)]
            for jb in range(4):
                S.op("dve", lambda e, jb=jb: e.tensor_tensor(out=ovs[jb], in0=o_t[:, jb, 0, :], in1=o_t[:, jb, 1, :], op=ALU.add),
                     [o_r], [osums[jb][1]])
            for jb in range(4):
                S.op("dve", lambda e, jb=jb: e.tensor_tensor(out=ovs[jb], in0=ovs[jb], in1=o_t[:, jb, 2, :], op=ALU.add),
                     [o_r, osums[jb][1]], [osums[jb][1]])
            for jb in range(4):
                S.op("dve", lambda e, jb=jb: e.reciprocal(out=rdens[jb][0][:], in_=osums[jb][0][:, :, 64]),
                     [osums[jb][1]], [rdens[jb][1]])
            for hh in range(4):
                for jb in range(4):
                    S.op("dve", lambda e, hh=hh, jb=jb: e.tensor_scalar_mul(
                        out=yat[:, jb, hh * 64:(hh + 1) * 64], in0=osums[jb][0][:, hh, 0:64], scalar1=rdens[jb][0][:, hh:hh + 1]),
                        [osums[jb][1], rdens[jb][1]], [yat_r])

        def xpose(i):
            yat, yat_r = yatb[i % 2]
            yatT, yatT_r = yatTb[i % 2]
            for fc in range(2):
                pt, pr = psum()
                ptb = pt.bitcast(BF16)
                for jb in range(4):
                    S.op("pe", lambda e, ptb=ptb, jb=jb, fc=fc: e.transpose(
                        out=ptb[:, jb * 128:(jb + 1) * 128], in_=yat[:, jb, fc * 128:(fc + 1) * 128], identity=P.ident_b[:]),
                        [yat_r, P.ident_b_r], [pr], signal=(jb == 3))
                S.op("act", lambda e, ptb=ptb, fc=fc: e.activation(out=yatT[:, fc, :], in_=ptb[:, 0:512], func=AF.Copy),
                     [pr], [yatT_r])

        def mm1(i):
            yh_t, yh_r = yh[i % 2]
            gg_t, gg_r = gg[i % 2]
            yatT, yatT_r = yatTb[i % 2]
            for m in range(8):
                p1, p1r = psum()
                for kc in range(6):
                    S.op("pe", lambda e, p1=p1, kc=kc, m=m: e.matmul(
                        p1[:, :], lhsT=whb[:, kc, m * 128:(m + 1) * 128], rhs=yh_t[:, kc, :], start=(kc == 0), stop=(kc == 5)),
                        [whb_r, yh_r], [p1r], signal=(kc == 5))
                p2, p2r = psum()
                for kc in range(2):
                    S.op("pe", lambda e, p2=p2, kc=kc, m=m: e.matmul(
                        p2[:, :], lhsT=wab[:, kc, m * 128:(m + 1) * 128], rhs=yatT[:, kc, :], start=(kc == 0), stop=(kc == 1)),
                        [wab_r, yatT_r], [p2r], signal=(kc == 1))
                ta, ta_r = tm[ti[0] % 6]
                tb, tb_r = tm[(ti[0] + 1) % 6]
                ti[0] += 2
                S.op("dve", lambda e, p1=p1, m=m, ta=ta: e.tensor_tensor(out=ta[:], in0=p1[:, :], in1=gg_t[:, m, :], op=ALU.mult),
                     [p1r, gg_r], [ta_r])
                S.op("dve", lambda e, p2=p2, m=m, tb=tb: e.tensor_tensor(out=tb[:], in0=p2[:, :], in1=gg_t[:, 8 + m, :], op=ALU.mult),
                     [p2r, gg_r], [tb_r])
                S.op("pool", lambda e, m=m, ta=ta, tb=tb: e.tensor_tensor(out=mix[:, m, :], in0=ta[:], in1=tb[:], op=ALU.add),
                     [ta_r, tb_r], [mix_r])

        def mm2(i):
            s_, mt = tiles[i]
            sc = P.SC[s_]
            x_t, x_r = xt[i % 2]
            gt1, gt1_r = gtb1[s_]
            for jb in range(4):
                for half in range(2):
                    pt, pr = psum()
                    for m in range(8):
                        S.op("pe", lambda e, pt=pt, m=m, jb=jb, half=half: e.matmul(
                            pt[:, :], lhsT=mix[:, m, jb * 128:(jb + 1) * 128], rhs=wo[:, m, half * 512:(half + 1) * 512],
                            start=(m == 0), stop=(m == 7)), [mix_r, wo_r], [pr], signal=(m == 7))
                    ta, ta_r = tm[ti[0] % 6]
                    ti[0] += 1
                    S.op("dve", lambda e, pt=pt, half=half, ta=ta: e.tensor_tensor(
                        out=ta[:], in0=pt[:, :], in1=gt1[:, half * 512:(half + 1) * 512], op=ALU.mult), [pr, gt1_r], [ta_r])
                    S.op("pool", lambda e, jb=jb, half=half, ta=ta: e.tensor_tensor(
                        out=x_t[:, jb, half * 512:(half + 1) * 512], in0=ta[:], in1=x_t[:, jb, half * 512:(half + 1) * 512],
                        op=ALU.add), [ta_r, x_r], [x_r])
            S.dma("sp", sc["h"][mt * 512:(mt + 1) * 512, :].rearrange("(j p) d -> p j d", p=128), x_t[:], [x_r],
                  [sc["res"]["h"]], x_r)

        load(0)
        merge(0)
        xpose(0)
        for i in range(nt):
            if i + 1 < nt:
                load(i + 1)
            mm1(i)
            if i + 1 < nt:
                merge(i + 1)
            mm2(i)
            if i + 1 < nt:
                xpose(i + 1)
        S.stage_end()


def stage_D2(P):
    nc, S, I, IN, sb, psum = P.nc, P.S, P.I, P.IN, P.sb, P.psum
    with contextlib.ExitStack() as st:
        wup, _ = sb("wup", [128, 8, DFF], BF16, st)
        wdn, _ = sb("wdn", [128, 32, D], BF16, st)
        wup_r = [Res(f"wup{k}") for k in range(8)]
        wdn_r = [Res(f"wdn{k}") for k in range(4)]
        for kc in range(8):
            S.dma("pool", wup[:, kc, :], I["w_up"][kc * 128:(kc + 1) * 128, :], [IN], [wup_r[kc]], wup_r[kc])
        for k in range(4):
            S.dma("pool", wdn[:, 8 * k:8 * k + 8, :], I["w_down"][1024 * k:1024 * (k + 1), :].rearrange("(f p) n -> p f n", p=128),
                  [IN], [wdn_r[k]], wdn_r[k])
        gtb2 = [sb(f"gt2_{s_}", [128, D], F32, st) for s_ in range(len(P.Ls))]
        for s_ in range(len(P.Ls)):
            S.dma("sp", gtb2[s_][0][:], P.gtbd[:, (s_ * 2 + 1) * D:(s_ * 2 + 2) * D], [P.gtbd_r], [gtb2[s_][1]], gtb2[s_][1])
        htb = [sb(f"ht{i}", [128, 2, D], F32, st) for i in range(3)]
        xnb = [sb(f"xn2_{i}", [128, 2, D], BF16, st) for i in range(2)]
        uT = [sb(f"u2T{i}", [128, 8, 256], BF16, st) for i in range(2)]
        hid, hid_r = sb("hid", [128, 32, 256], BF16, st)
        ssb = [sb(f"ss2_{i}", [128, 2], F32, st) for i in range(2)]
        junk, junk_r = sb("junk2", [128, D], BF16, st)
        tm = [sb(f"tn{i}", [128, 512], F32, st) for i in range(3)]
        ti = [0]
        tiles = [(s_, mt) for s_, L in enumerate(P.Ls) for mt in range(L // 256)]
        nt = len(tiles)

        def load(i):
            s_, mt = tiles[i]
            ht, ht_r = htb[i % 3]
            S.dma("sp", ht[:], P.SC[s_]["h"][mt * 256:(mt + 1) * 256, :].rearrange("(j p) d -> p j d", p=128),
                  [P.SC[s_]["res"]["h"]], [ht_r], ht_r)

        def norm(i):
            ht, ht_r = htb[i % 3]
            xn, xn_r = xnb[i % 2]
            ss, ss_r = ssb[i % 2]
            S.op("dve", lambda e: e.memset(ss[:], 0.0), [], [ss_r])
            for j in range(2):
                S.op("act", lambda e, j=j: e.activation(out=junk[:], in_=ht[:, j, :], func=AF.Square,
                                                        accum_out=ss[:, j:j + 1]), [ht_r, ss_r], [junk_r, ss_r])
            S.op("dve", lambda e: e.tensor_scalar(out=ss[:], in0=ss[:], scalar1=1.0 / D, scalar2=EPS,
                                                  op0=ALU.mult, op1=ALU.add), [ss_r], [ss_r])
            S.op("act", lambda e: e.activation(out=ss[:], in_=ss[:], func=AF.Sqrt), [ss_r], [ss_r])
            S.op("dve", lambda e: e.reciprocal(out=ss[:], in_=ss[:]), [ss_r], [ss_r])
            for j in range(2):
                S.op("act", lambda e, j=j: e.activation(out=xn[:, j, :], in_=ht[:, j, :], func=AF.Copy,
                                                        scale=ss[:, j:j + 1]), [ht_r, ss_r], [xn_r])

        def xpose(i):
            s_, mt = tiles[i]
            xn, xn_r = xnb[i % 2]
            uT_t, uT_r = uT[i % 2]
            for kc in range(8):
                pt, pr = psum()
                ptb = pt.bitcast(BF16)
                for j in range(2):
                    S.op("pe", lambda e, j=j, kc=kc, ptb=ptb: e.transpose(
                        out=ptb[:, j * 128:(j + 1) * 128], in_=xn[:, j, kc * 128:(kc + 1) * 128], identity=P.ident_b[:]),
                        [xn_r, P.ident_b_r], [pr], signal=(j == 1))
                S.op("dve", lambda e, kc=kc, ptb=ptb: e.tensor_scalar(
                    out=uT_t[:, kc, :], in0=ptb[:, 0:256], scalar1=P.modT[:, s_, 3, kc:kc + 1],
                    scalar2=P.modT[:, s_, 2, kc:kc + 1], op0=ALU.mult, op1=ALU.add), [pr, P.modT_r], [uT_r])

        def up(i):
            uT_t, uT_r = uT[i % 2]
            for f2 in range(16):
                pt, pr = psum()
                for u in range(2):
                    fc = 2 * f2 + u
                    for kc in range(8):
                        S.op("pe", lambda e, pt=pt, u=u, fc=fc, kc=kc: e.matmul(
                            pt[:, u * 256:(u + 1) * 256], lhsT=wup[:, kc, fc * 128:(fc + 1) * 128], rhs=uT_t[:, kc, :],
                            start=(kc == 0), stop=(kc == 7)), [wup_r[kc], uT_r], [pr], signal=(kc == 7 and u == 1))
                ta, ta_r = tm[ti[0] % 3]
                ti[0] += 1
                S.op("act", lambda e, pt=pt, ta=ta: e.activation(out=ta[:], in_=pt[:, :], func=AF.Relu), [pr], [ta_r])
                S.op("dve" if f2 % 2 else "pool", lambda e, ta=ta, f2=f2: e.tensor_tensor(
                    out=hid[:, 2 * f2:2 * f2 + 2, :], in0=ta[:].rearrange("p (u t) -> p u t", u=2),
                    in1=ta[:].rearrange("p (u t) -> p u t", u=2), op=ALU.mult), [ta_r], [hid_r])

        def down(i):
            s_, mt = tiles[i]
            ht, ht_r = htb[i % 3]
            gt2, gt2_r = gtb2[s_]
            for j in range(2):
                for half in range(2):
                    pt, pr = psum()
                    for fc in range(32):
                        S.op("pe", lambda e, pt=pt, fc=fc, j=j, half=half: e.matmul(
                            pt[:, :], lhsT=hid[:, fc, j * 128:(j + 1) * 128], rhs=wdn[:, fc, half * 512:(half + 1) * 512],
                            start=(fc == 0), stop=(fc == 31)), [hid_r, wdn_r[fc // 8]], [pr], signal=(fc == 31))
                    ta, ta_r = tm[ti[0] % 3]
                    ti[0] += 1
                    S.op("dve", lambda e, pt=pt, half=half, ta=ta: e.tensor_tensor(
                        out=ta[:], in0=pt[:, :], in1=gt2[:, half * 512:(half + 1) * 512], op=ALU.mult), [pr, gt2_r], [ta_r])
                    S.op("pool", lambda e, j=j, half=half, ta=ta: e.tensor_tensor(
                        out=ht[:, j, half * 512:(half + 1) * 512], in0=ta[:], in1=ht[:, j, half * 512:(half + 1) * 512],
                        op=ALU.add), [ta_r, ht_r], [ht_r])
            S.dma("sp", P.O[s_][mt * 256:(mt + 1) * 256, :].rearrange("(j p) d -> p j d", p=128), ht[:], [ht_r], [P.OUT_r], ht_r)

        load(0)
        if nt > 1:
            load(1)
        norm(0)
        xpose(0)
        for i in range(nt):
            if i + 1 < nt:
                norm(i + 1)
            if i + 2 < nt:
                load(i + 2)
            up(i)
            if i + 1 < nt:
                xpose(i + 1)
            down(i)
        S.stage_end()


def build_all(Ls):
    P, hc = build(Ls)
    P.OUT_r = Res("out", multi=True)
    stage_A(P)
    stage_attn(P)
    stage_hyena(P)
    stage_D1(P)
    stage_D2(P)
    return finish(P), hc


_CACHE = {}


def kernel(**inputs):
    Ls = [8192, 4096]
    if "nc" not in _CACHE:
        _CACHE["nc"], _CACHE["hc"] = build_all(Ls)
    nc, hc = _CACHE["nc"], _CACHE["hc"]
    f = lambda a: np.ascontiguousarray(np.asarray(a))
    shared = {}
    for nm in ("rel_bias",):
        shared[nm] = f(inputs[nm])
    for nm in ("ada_w", "w_in", "conv_w", "filt_w1", "filt_w2", "filt_w3", "filt_w_out", "w_hy_br", "w_at_br", "w_out",
               "w_up", "w_down"):
        shared[nm] = f(inputs[nm])[0]
    for nm in ("ada_b", "norm1_g", "conv_b", "filt_b1", "filt_b2", "filt_b3", "filt_freq", "hyena_d", "norm2_g"):
        shared[nm] = f(inputs[nm]).reshape(1, -1)
    shared["q_norm_g"] = f(inputs["q_norm_g"]).reshape(1, -1)
    shared["k_norm_g"] = f(inputs["k_norm_g"]).reshape(1, -1)
    shared.update(hc)
    xp, xs = f(inputs["x_prompt"]), f(inputs["x_sample"])
    cp, cs = f(inputs["c_prompt"]), f(inputs["c_sample"])
    in_maps = []
    for i in range(8):
        m = dict(shared)
        m["x0"] = xp[i]
        m["x1"] = xs[i]
        m["c"] = np.stack([cp[i], cs[i]], axis=0)
        in_maps.append(m)
    res = run_bass_kernel_spmd(nc, in_maps, core_ids=list(range(8)))
    yp = np.stack([np.asarray(r["y0"]) for r in res.results], axis=0).astype(np.float32)
    ys = np.stack([np.asarray(r["y1"]) for r in res.results], axis=0).astype(np.float32)
    return (yp, ys)
```
